# Optimizing a Trainium2 kernel written in Bass

```python
import jax
import jax.numpy as jnp
from jax import lax

D_MODEL = 1024
BATCH = 2
SEQ = 8192
DEPTH = 1

HEAD_DIM = 64
WIN_Q_HEADS = 8
WIN_KV_HEADS = 2
WIN_HALF = 128
DIL_SLOTS = 8
DIL_PAIRS = ((128, 1), (512, 4), (2048, 16))
N_DIL = len(DIL_PAIRS)
ROT_DIM = HEAD_DIM // 4
ROPE_THETA = 500000.0
MEM_LEN = 256
X_HEADS = 4
X_HEAD_DIM = D_MODEL // X_HEADS
D_FF = 2816
CONV_WIDTH = 3
WIN_WIDTH = WIN_Q_HEADS * HEAD_DIM
DIL_WIDTH = DIL_SLOTS * HEAD_DIM
MIX_WIDTH = WIN_WIDTH + DIL_WIDTH
A_Q = WIN_WIDTH
A_KV = WIN_KV_HEADS * HEAD_DIM
B_QKV = N_DIL * DIL_WIDTH
IN_WIDTH = A_Q + 2 * A_KV + 3 * B_QKV
SPLITS = (A_Q, A_Q + A_KV, A_Q + 2 * A_KV, A_Q + 2 * A_KV + B_QKV, A_Q + 2 * A_KV + 2 * B_QKV)
DEEPNORM_ALPHA = (2 * DEPTH) ** 0.25
DEEPNORM_BETA = (8 * DEPTH) ** -0.25
LN_EPS = 1e-5
NEG_INF = -1e30
POS_OFFSET_MAX = 4096

kernel_name = 'hymba_window_dilated_deepnorm_encoder'


def layer_norm(x, g, b):
    xf = x.astype(jnp.float32)
    mu = jnp.mean(xf, -1, keepdims=True)
    var = jnp.mean(jnp.square(xf - mu), -1, keepdims=True)
    return ((xf - mu) * lax.rsqrt(var + LN_EPS) * g + b).astype(x.dtype)


def rms_norm(x, g):
    xf = x.astype(jnp.float32)
    return (xf * lax.rsqrt(jnp.mean(jnp.square(xf), -1, keepdims=True) + LN_EPS) * g).astype(x.dtype)


def partial_rope(t, positions):
    half = ROT_DIM // 2
    inv_freq = ROPE_THETA ** (-jnp.arange(0, ROT_DIM, 2, dtype=jnp.float32) / ROT_DIM)
    ang = positions.astype(jnp.float32)[:, :, None] * inv_freq
    cos = jnp.cos(ang)[:, :, None, :]
    sin = jnp.sin(ang)[:, :, None, :]
    tr = t[..., :ROT_DIM].astype(jnp.float32)
    t1, t2 = tr[..., :half], tr[..., half:]
    rot = jnp.concatenate([t1 * cos - t2 * sin, t2 * cos + t1 * sin], -1).astype(t.dtype)
    return jnp.concatenate([rot, t[..., ROT_DIM:]], -1)


def banded_attention(q, k, v, n_side, sink=None):
    bt, seq_len, hkv, grp, dh = q.shape
    blk = n_side
    nb = -(-seq_len // blk)
    pad = nb * blk - seq_len
    qb = jnp.pad(q, ((0, 0), (0, pad), (0, 0), (0, 0), (0, 0))).reshape(bt, nb, blk, hkv, grp, dh)

    def neighbourhood(t):
        tp = jnp.pad(t, ((0, 0), (blk, blk + pad), (0, 0), (0, 0))).reshape(bt, nb + 2, blk, hkv, dh)
        return jnp.concatenate([tp[:, :-2], tp[:, 1:-1], tp[:, 2:]], axis=2)

    kw = neighbourhood(k)
    vw = neighbourhood(v)
    s = jnp.einsum('bnqhgd,bnkhd->bnhgqk', qb, kw).astype(jnp.float32) * (dh ** -0.5)
    qi = jnp.arange(blk)[:, None]
    kj = jnp.arange(3 * blk)[None, :]
    kabs = jnp.arange(nb)[:, None, None] * blk + kj[None] - blk
    mask = (jnp.abs(kj - blk - qi) <= n_side)[None] & (kabs >= 0) & (kabs < seq_len)
    mask = mask[None, :, None, None]
    s = jnp.where(mask, s, NEG_INF)
    m = jnp.max(s, -1)
    if sink is not None:
        sink_f = sink.astype(jnp.float32)[None, None, :, :, None]
        m = jnp.maximum(m, sink_f)
    p = jnp.where(mask, jnp.exp(s - m[..., None]), 0.0)
    denom = jnp.sum(p, -1)
    if sink is not None:
        denom = denom + jnp.exp(sink_f - m)
    o = jnp.einsum('bnhgqk,bnkhd->bnqhgd', p, vw.astype(jnp.float32))
    o = o / jnp.moveaxis(denom, -1, 2)[..., None]
    o = o.astype(q.dtype).reshape(bt, nb * blk, hkv, grp, dh)[:, :seq_len]
    lse = jnp.moveaxis(m + jnp.log(denom), -1, 2).reshape(bt, nb * blk, hkv, grp)[:, :seq_len]
    return o, lse


def to_residue(t, r):
    b, s = t.shape[:2]
    t = t.reshape((b, s // r, r) + t.shape[2:])
    t = jnp.moveaxis(t, 2, 1)
    return t.reshape((b * r, s // r) + t.shape[3:])


def from_residue(t, r, batch):
    t = t.reshape((batch, r) + t.shape[1:])
    t = jnp.moveaxis(t, 1, 2)
    return t.reshape((batch, t.shape[1] * r) + t.shape[3:])


def parallel_mixer(h, positions, w_in, attn_sink, g_win, g_dil, w_out):
    b, s, _ = h.shape
    z = h @ w_in
    qa, ka, va, qb, kb, vb = jnp.split(z, SPLITS, axis=-1)
    qa = partial_rope(qa.reshape(b, s, WIN_Q_HEADS, HEAD_DIM), positions)
    ka = partial_rope(ka.reshape(b, s, WIN_KV_HEADS, HEAD_DIM), positions)
    va = va.reshape(b, s, WIN_KV_HEADS, HEAD_DIM)
    qa = qa.reshape(b, s, WIN_KV_HEADS, WIN_Q_HEADS // WIN_KV_HEADS, HEAD_DIM)
    out_a, _ = banded_attention(qa, ka, va, WIN_HALF, attn_sink.reshape(WIN_KV_HEADS, -1))
    out_a = out_a.reshape(b, s, WIN_WIDTH)
    n_heads_b = N_DIL * DIL_SLOTS
    qb = partial_rope(qb.reshape(b, s, n_heads_b, HEAD_DIM), positions).reshape(b, s, N_DIL, DIL_SLOTS, HEAD_DIM)
    kb = partial_rope(kb.reshape(b, s, n_heads_b, HEAD_DIM), positions).reshape(b, s, N_DIL, DIL_SLOTS, HEAD_DIM)
    vb = vb.reshape(b, s, N_DIL, DIL_SLOTS, HEAD_DIM)
    outs = []
    lses = []
    for gi, (window, dil) in enumerate(DIL_PAIRS):
        n_side = window // (2 * dil)
        o, lse = banded_attention(to_residue(qb[:, :, gi], dil)[:, :, :, None],
                                  to_residue(kb[:, :, gi], dil),
                                  to_residue(vb[:, :, gi], dil), n_side)
        outs.append(from_residue(o[:, :, :, 0], dil, b))
        lses.append(from_residue(lse[..., 0], dil, b))
    wts = jax.nn.softmax(jnp.stack(lses), axis=0)[..., None]
    out_b = jnp.sum(wts * jnp.stack(outs).astype(jnp.float32), axis=0).astype(h.dtype).reshape(b, s, DIL_WIDTH)
    mixed = jnp.concatenate([rms_norm(out_a, g_win), rms_norm(out_b, g_dil)], -1)
    return mixed @ w_out


def memory_cross_attention(h, mem_n, w_q, w_k, w_v, w_o):
    b, s, _ = h.shape
    m_len = mem_n.shape[1]
    q = (h @ w_q).reshape(b, s, X_HEADS, X_HEAD_DIM)
    k = (mem_n @ w_k).reshape(b, m_len, X_HEADS, X_HEAD_DIM)
    v = (mem_n @ w_v).reshape(b, m_len, X_HEADS, X_HEAD_DIM)
    sc = jnp.einsum('bshd,bmhd->bhsm', q, k).astype(jnp.float32) * (X_HEAD_DIM ** -0.5)
    p = jax.nn.softmax(sc, axis=-1)
    o = jnp.einsum('bhsm,bmhd->bshd', p, v.astype(jnp.float32)).astype(h.dtype)
    return o.reshape(b, s, D_MODEL) @ w_o


def conv_glu(h, w_gate, w_up, conv_w, conv_b, w_down):
    s = h.shape[1]
    g = h @ w_gate
    half = CONV_WIDTH // 2
    gp = jnp.pad(g, ((0, 0), (half, half), (0, 0)))
    g = sum(gp[:, j:j + s] * conv_w[j] for j in range(CONV_WIDTH)) + conv_b
    return (jax.nn.gelu(g, approximate=False) * (h @ w_up)) @ w_down


def setup_inputs(seed: int = 0) -> dict:
    key = jax.random.key(seed)
    keys = list(jax.random.split(key, 40))
    f32 = jnp.float32
    d = D_MODEL
    nl = DEPTH
    beta = DEEPNORM_BETA

    def normal(shape, scale):
        return jax.random.normal(keys.pop(), shape, f32) * scale

    def gain(shape):
        return 1.0 + normal(shape, 0.02)

    def bias(shape):
        return normal(shape, 0.02)

    in_scale = jnp.concatenate([jnp.ones((A_Q + A_KV,), f32), jnp.full((A_KV,), beta, f32),
                                jnp.ones((2 * B_QKV,), f32), jnp.full((B_QKV,), beta, f32)])
    x = normal((BATCH, SEQ, d), 1.0)
    mem = normal((BATCH, MEM_LEN, d), 1.0)
    positions = jnp.arange(SEQ, dtype=jnp.int32)[None, :] + jax.random.randint(
        keys.pop(), (BATCH, 1), 0, POS_OFFSET_MAX, dtype=jnp.int32)
    return {
        'x': x,
        'mem': mem,
        'positions': positions,
        'ln_in_g': gain((d,)),
        'ln_in_b': bias((d,)),
        'w_in': normal((nl, d, IN_WIDTH), d ** -0.5) * in_scale,
        'attn_sink': normal((nl, WIN_Q_HEADS), 0.5),
        'g_win': gain((nl, WIN_WIDTH)),
        'g_dil': gain((nl, DIL_WIDTH)),
        'w_mix_out': normal((nl, MIX_WIDTH, d), MIX_WIDTH ** -0.5 * beta),
        'ln1_g': gain((nl, d)),
        'ln1_b': bias((nl, d)),
        'mem_ln_g': gain((nl, d)),
        'mem_ln_b': bias((nl, d)),
        'w_xq': normal((nl, d, d), d ** -0.5),
        'w_xk': normal((nl, d, d), d ** -0.5),
        'w_xv': normal((nl, d, d), d ** -0.5 * beta),
        'w_xo': normal((nl, d, d), d ** -0.5 * beta),
        'ln2_g': gain((nl, d)),
        'ln2_b': bias((nl, d)),
        'w_gate': normal((nl, d, D_FF), d ** -0.5),
        'w_up': normal((nl, d, D_FF), d ** -0.5 * beta),
        'conv_w': normal((nl, CONV_WIDTH, D_FF), CONV_WIDTH ** -0.5),
        'conv_b': bias((nl, D_FF)),
        'w_down': normal((nl, D_FF, d), D_FF ** -0.5 * beta),
        'ln3_g': gain((nl, d)),
        'ln3_b': bias((nl, d)),
    }


def reference(x, mem, positions, ln_in_g, ln_in_b, w_in, attn_sink, g_win, g_dil, w_mix_out,
              ln1_g, ln1_b, mem_ln_g, mem_ln_b, w_xq, w_xk, w_xv, w_xo, ln2_g, ln2_b,
              w_gate, w_up, conv_w, conv_b, w_down, ln3_g, ln3_b):
    h = layer_norm(x, ln_in_g, ln_in_b)
    for l in range(DEPTH):
        mix = parallel_mixer(h, positions, w_in[l], attn_sink[l], g_win[l], g_dil[l], w_mix_out[l])
        h = layer_norm(DEEPNORM_ALPHA * h + mix, ln1_g[l], ln1_b[l])
        mem_n = layer_norm(mem, mem_ln_g[l], mem_ln_b[l])
        xa = memory_cross_attention(h, mem_n, w_xq[l], w_xk[l], w_xv[l], w_xo[l])
        h = layer_norm(DEEPNORM_ALPHA * h + xa, ln2_g[l], ln2_b[l])
        ff = conv_glu(h, w_gate[l], w_up[l], conv_w[l], conv_b[l], w_down[l])
        h = layer_norm(DEEPNORM_ALPHA * h + ff, ln3_g[l], ln3_b[l])
    return h
```

```python
import math
from contextlib import ExitStack
import numpy as np
import ml_dtypes
import concourse.bass as bass
import concourse.mybir as mybir
from concourse.bass_utils import run_bass_kernel_spmd

F32 = mybir.dt.float32
BF16 = mybir.dt.bfloat16
I32 = mybir.dt.int32
AF = mybir.ActivationFunctionType
ALU = mybir.AluOpType
AX = mybir.AxisListType

D = 1024
SEQ = 8192
TOWN = 2048
HALO = 1152
EXT = TOWN + 2 * HALO
NXT = EXT // 128
OWN0 = HALO
NMID = TOWN + 2
EXTRA_E = (OWN0 - 1, OWN0 + TOWN)
DFF = 2816
NFF = DFF // 128
MEM = 256
ALPHA = 2.0 ** 0.25
EPS = 1e-5
QA0, KA0, VA0 = 0, 512, 640
QB0, KB0, VB0 = 768, 768 + 1536, 768 + 3072
GROUPS = {"A": (1, 128), "g1": (1, 64), "g2": (4, 64), "g3": (16, 64)}
PCOLS = (("lnin_g", 8), ("lnin_b", 8), ("ln1_g", 8), ("ln1_b", 8), ("ln2_g", 8), ("ln2_b", 8), ("mem_g", 8),
         ("mem_b", 8), ("gwin", 4), ("gdil", 4), ("sink0", 8), ("invf", 1), ("rotm", 1), ("sgn", 1), ("eps", 1), ("zero", 1),
         ("cw0", 22), ("cw1", 22), ("cw2", 22), ("cb", 22))


def mid_col_e(c):
    return OWN0 + c if c < TOWN else EXTRA_E[c - TOWN]


def e_mid_col(e):
    if OWN0 <= e < OWN0 + TOWN:
        return e - OWN0
    return TOWN + EXTRA_E.index(e)


def q_blocks(r):
    out = []
    for c in range(r):
        j0 = (OWN0 - c + r - 1) // r
        j1 = (OWN0 + TOWN - c + r - 1) // r
        for ja in range(j0, j1, 128):
            out.append((c, ja, min(128, j1 - ja)))
    for e in EXTRA_E:
        out.append((e % r, e // r, 1))
    return out


def key_tiles(r, W, blk):
    c, ja, nq = blk
    lo, hi = ja - W, ja + nq + W
    tiles = []
    ks = lo
    while ks < hi:
        nk = min(128, hi - ks)
        tiles.append((r, c, ks, nk, ja - ks))
        ks += 128
    return tiles


def group_plan(gname):
    r, W = GROUPS[gname]
    blks = q_blocks(r)
    vt = {}
    plan = []
    for b in blks:
        kts = key_tiles(r, W, b)
        for (rr, c, ks, nk, dl) in kts:
            key = (rr, c, ks, nk)
            if key not in vt:
                vt[key] = len(vt)
        plan.append((b, kts))
    emin = min(k[1] + k[0] * k[2] for k in vt)
    emax = max(k[1] + k[0] * (k[2] + k[3] - 1) for k in vt)
    return plan, vt, emin, emax + 1


PLANS = {g: group_plan(g) for g in GROUPS}
VT_BASE = {}
_n = 0
for _g in GROUPS:
    VT_BASE[_g] = _n
    _n += len(PLANS[_g][1])
NVT = _n


class Buf:
    __slots__ = ("name", "w", "r", "sem", "cnt", "excl")

    def __init__(self, name, excl=False):
        self.name = name
        self.excl = excl
        self.w = None
        self.r = {}
        self.sem = None
        self.cnt = 0


class _Rec:
    def __init__(self):
        self.calls = []

    def __getattr__(self, name):
        def f(*a, **k):
            self.calls.append((name, a, k))
            return self
        return f


class Op:
    __slots__ = ("eng", "fn", "deps", "signal", "count", "dma")

    def __init__(self, eng, fn):
        self.eng = eng
        if fn is not None:
            rec = _Rec()
            fn(rec)
            calls = rec.calls

            def replay(e, calls=calls):
                ins = None
                for (name, a, k) in calls:
                    ins = getattr(e, name)(*a, **k)
                return ins
            self.fn = replay
        else:
            self.fn = None
        self.deps = []
        self.signal = False
        self.count = 0
        self.dma = None


class Prog:
    ENGS = ("pe", "act", "dve", "pool", "sp")

    def __init__(self, nc, stack):
        self.nc = nc
        self.stack = stack
        self.streams = {e: [] for e in self.ENGS}
        self.sems = {e: stack.enter_context(nc.semaphore("s_" + e)) for e in self.ENGS}
        self.nsem = 0

    def _deps(self, eng, reads, writes):
        deps = []
        for b in reads:
            if b.w is not None:
                deps.append(b.w)
            if b.excl:
                deps.extend(v for k, v in b.r.items() if k != eng)
        for b in writes:
            if b.w is not None:
                deps.append(b.w)
            deps.extend(b.r.values())
        out = []
        for d in deps:
            if d[0] == "op":
                if eng == "pe" and d[1].eng == "pe":
                    continue
                d[1].signal = True
            out.append(d)
        return out

    def op(self, eng, fn, reads=(), writes=()):
        o = Op(eng, fn)
        o.deps = self._deps(eng, reads, writes)
        d = ("op", o)
        for b in reads:
            b.r[eng] = d
        for b in writes:
            b.w = d
            b.r = {}
        self.streams[eng].append(o)
        return o

    def dma(self, out, in_, reads=(), writes=(), sembuf=None, eng="sp", **kw):
        if sembuf.sem is None:
            sembuf.sem = self.stack.enter_context(self.nc.semaphore("d%d" % self.nsem))
            self.nsem += 1
        o = Op(eng, lambda e: e.dma_start(out=out, in_=in_, **kw))
        o.deps = self._deps(eng, reads, writes)
        sembuf.cnt += 16
        o.dma = (sembuf.sem, 16)
        d = ("dma", sembuf.sem, sembuf.cnt)
        for b in reads:
            b.r["dma%d" % id(sembuf)] = d
        for b in writes:
            b.w = d
            b.r = {}
        self.streams[eng].append(o)
        return d

    def barrier(self, bufs):
        lasts = []
        for e in ("pe", "act", "dve", "pool"):
            for o in reversed(self.streams[e]):
                if o.fn is None:
                    break
                if o.dma is None:
                    o.signal = True
                    lasts.append(("op", o))
                    break
        for e in ("pe", "act", "dve", "pool", "sp"):
            o = Op(e, None)
            o.deps = list(lasts)
            self.streams[e].append(o)

    def emit(self, final_deps):
        nc = self.nc
        for e in self.ENGS:
            c = 0
            for o in self.streams[e]:
                if o.signal:
                    c += 1
                o.count = c
        handles = {}

        def run(ename, eng, tail=None):
            seen = {}
            for o in self.streams[ename]:
                for d in o.deps:
                    if d[0] == "op":
                        key, val, sem = d[1].eng, d[1].count, self.sems[d[1].eng]
                    else:
                        key, val, sem = id(d[1]), d[2], d[1]
                    if seen.get(key, 0) < val:
                        eng.wait_ge(sem, val)
                        seen[key] = val
                if o.fn is None:
                    continue
                ins = o.fn(eng)
                if o.dma is not None:
                    ins.then_inc(o.dma[0], o.dma[1])
                elif o.signal:
                    ins.then_inc(self.sems[ename], 1)
            if tail:
                for d in tail:
                    eng.wait_ge(d[1], d[2])

        with nc.Block() as block:
            @block.tensor
            def _(e):
                run("pe", e)

            @block.scalar
            def _(e):
                run("act", e)

            @block.vector
            def _(e):
                run("dve", e)

            @block.gpsimd
            def _(e):
                run("pool", e)

            @block.sync
            def _(e):
                run("sp", e, tail=final_deps)


def build_program(dbg=None, stop_after=None):
    dbg = dbg or {}
    nc = bass.Bass("TRN2", target_bir_lowering=False)
    stack = ExitStack()
    P = Prog(nc, stack)

    def dram(name, shape, dt, kind="ExternalInput"):
        return nc.dram_tensor(name, list(shape), dt, kind=kind).ap()

    def sbp(name, shape, dt):
        return stack.enter_context(nc.sbuf_tensor("sb_" + name, list(shape), dt))

    arena_box = {}
    cur = [0]

    def seek(off):
        cur[0] = off

    def sb(name, shape, dt, st=None):
        esz = 2 if dt == BF16 else 4
        n = 1
        for d_ in shape[1:]:
            n *= d_
        nbytes = (n * esz + 31) // 32 * 32
        off = cur[0]
        assert off + nbytes <= arena_box["size"], (name, off, nbytes, arena_box["size"])
        cur[0] = off + nbytes
        ap = arena_box["t"][0:shape[0], off // 2:off // 2 + n * esz // 2]
        if esz == 4:
            ap = ap.bitcast(dt)
        if len(shape) == 3:
            ap = ap.rearrange("p (a b) -> p a b", a=shape[1])
        elif len(shape) == 4:
            ap = ap.rearrange("p (a b c) -> p a b c", a=shape[1], b=shape[2])
        return ap

    x_ext = dram("x_ext", [EXT, D], F32)
    pos_d = dram("pos", [1, EXT], I32)
    kvtab_d = dram("kvtab", [128, NVT], F32)
    gvalid_d = dram("gvalid", [1, 2], F32)
    cbf_d = dram("cbf", [128, 1024], BF16)
    pc_d = dram("pcols", [128, 256], F32)
    mem_d = dram("mem", [MEM, D], F32)
    w_in = dram("w_in", [D, 5376], F32)
    w_out = dram("w_out", [D, D], F32)
    w_xq = dram("w_xq", [D, D], F32)
    w_xk = dram("w_xk", [D, D], F32)
    w_xv = dram("w_xv", [D, D], F32)
    w_xo = dram("w_xo", [D, D], F32)
    w_gate = dram("w_gate", [D, DFF], F32)
    w_up = dram("w_up", [D, DFF], F32)
    w_down = dram("w_down", [DFF, D], F32)
    rows_d = dram("rows", [8, D], F32)
    out_d = dram("out", [TOWN, D], F32, kind="ExternalOutput")
    dbg_out = {}
    for k, (shape, dt) in dbg.items():
        dbg_out[k] = dram("dbg_" + k, shape, dt, kind="ExternalOutput")

    PC = {}
    _o = 0
    for nm, n in PCOLS:
        PC[nm] = _o
        _o += n
    assert _o <= 256

    cbf = sbp("cbf", [128, 1024], BF16)
    pc = sbp("pc", [128, 256], F32)
    kvt = sbp("kvt", [128, NVT], F32)
    stats = sbp("stats", [128, NXT, 2], F32)
    ones_bf = sbp("ones_bf", [128, 128], BF16)
    esk = sbp("esk", [128, 8], F32)
    gval = sbp("gval", [128, 2], F32)
    B_const = Buf("const")
    ident = cbf[:, 0:128]
    perm = cbf[:, 128:256]
    bnd = {64: cbf[:, 256:640], 128: cbf[:, 640:1024]}
    psS = [stack.enter_context(nc.psum_tensor("psS%d" % i, [128, 1024], F32)) for i in range(2)]
    psB0 = stack.enter_context(nc.psum_tensor("psB0", [128, 512], F32))
    psZZ = stack.enter_context(nc.psum_tensor("psZZ", [128, 1024], F32))
    psB3 = stack.enter_context(nc.psum_tensor("psB3", [128, 512], F32))
    psB = [psB0, psZZ[:, 0:512], psZZ[:, 512:1024], psB3]
    B_psS = [Buf("psS%d" % i, excl=True) for i in range(2)]
    B_psB = [Buf("psB%d" % i, excl=True) for i in range(4)]

    P.dma(cbf[:], cbf_d, writes=[B_const], sembuf=B_const)
    mpk = {64: bnd[64][:, 64:320].rearrange("p (t q) -> p t q", t=2), 128: bnd[128][:, 0:384].rearrange("p (t q) -> p t q", t=3)}
    P.dma(pc[:], pc_d, writes=[B_const], sembuf=B_const)
    P.dma(kvt[:], kvtab_d, writes=[B_const], sembuf=B_const)
    P.dma(gval[:], gvalid_d.partition_broadcast(128), writes=[B_const], sembuf=B_const)
    P.op("pool", lambda e: e.memset(ones_bf[:], 1.0), writes=[B_const])
    P.op("act", lambda e: e.activation(out=esk[:], in_=pc[:, PC["sink0"]:PC["sink0"] + 8], func=AF.Exp),
         reads=[B_const], writes=[B_const])

    def pcol(nm, k=0, p0=0, p1=128):
        return pc[p0:p1, PC[nm] + k: PC[nm] + k + 1]

    final = []
    B_dbg = Buf("dbg")

    def tap(name, ap, reads):
        if name in dbg_out:
            final.append(P.dma(dbg_out[name], ap, reads=reads, sembuf=B_dbg))

    lnw = [sbp("lnw%d" % i, [128, 16], F32) for i in range(2)]
    B_lnw = [Buf("lnw%d" % i) for i in range(2)]
    ln_ctr = [0]

    def ln_stats(x_ap, n, mean_out, rstd_out, rd, wr):
        i = ln_ctr[0] % 2
        ln_ctr[0] += 1
        w = lnw[i]
        bw = [B_lnw[i]]
        P.op("dve", lambda e: e.bn_stats(w[0:n, 0:6], x_ap[0:n, 0:512]), reads=rd, writes=bw)
        P.op("dve", lambda e: e.bn_stats(w[0:n, 6:12], x_ap[0:n, 512:1024]), reads=rd, writes=bw)
        P.op("dve", lambda e: e.bn_aggr(w[0:n, 12:14], w[0:n, 0:12]), reads=bw, writes=bw)
        P.op("dve", lambda e: e.tensor_copy(out=mean_out, in_=w[0:n, 12:13]), reads=bw, writes=wr)
        P.op("act", lambda e: e.activation(out=w[0:n, 14:15], in_=w[0:n, 13:14], func=AF.Ln, bias=pcol("eps", 0, 0, n)),
             reads=bw + [B_const], writes=bw)
        P.op("act", lambda e: e.activation(out=rstd_out, in_=w[0:n, 14:15], func=AF.Exp, scale=-0.5),
             reads=bw, writes=wr)

    mr = [sbp("mr%d" % i, [128, 4], F32) for i in range(2)]
    B_mr = [Buf("mr%d" % i) for i in range(2)]
    arena_box["size"] = (nc.sbuf_bytes_remaining - 256) // 64 * 64
    arena_box["t"] = stack.enter_context(nc.sbuf_tensor("arena", [128, arena_box["size"] // 2], BF16))
    stM = ExitStack()
    seek(0)
    hT = sb("hT", [128, 8, EXT], BF16, stM)
    B_hT = [Buf("hT%d" % t) for t in range(NXT)]
    ctab = sb("ctab", [128, EXT], F32, stM)
    stab = sb("stab", [128, EXT], F32, stM)
    B_tab = Buf("tab")
    B_stats = [Buf("stats%d" % t) for t in range(NXT)]
    mixedT = sb("mixedT", [128, 8, NMID], BF16)
    B_mixed = [Buf("mixed%d" % k) for k in range(8)]
    MIX_END = cur[0]

    ph0 = ExitStack()
    xs = [sb("xs%d" % i, [128, D], F32, ph0) for i in range(3)]
    B_xs = [Buf("xs%d" % i) for i in range(3)]
    xh = [sb("xh%d" % i, [128, D], BF16, ph0) for i in range(2)]
    B_xh = [Buf("xh%d" % i) for i in range(2)]
    posi = sb("posi", [128, EXT], I32, ph0)
    ang = sb("ang", [128, EXT], F32, ph0)
    ktf = sb("ktf", [128, EXT], F32, ph0)
    kti = posi
    B_pos = Buf("pos")
    B_ang = Buf("ang")
    B_kt = Buf("kt")

    deferred = []

    def dop(*a_, **k_):
        deferred.append((a_, k_))

    dop("dve", lambda e: e.tensor_copy(out=ang[:], in_=posi[:]), reads=[B_pos, B_const], writes=[B_ang])
    dop("dve", lambda e: e.tensor_scalar(out=ang[:], in0=ang[:], scalar1=pcol("invf"), scalar2=None,
                                          op0=ALU.mult), reads=[B_ang, B_const], writes=[B_ang])
    TWO_PI = 2.0 * math.pi
    C1 = 6.28125
    C2 = TWO_PI - C1

    def range_reduce(dst, bdst, shift):
        dop("dve", lambda e: e.tensor_scalar(out=ktf[:], in0=ang[:], scalar1=shift, scalar2=1.0 / TWO_PI,
                                              op0=ALU.add, op1=ALU.mult), reads=[B_ang], writes=[B_kt])
        dop("dve", lambda e: e.tensor_copy(out=kti[:], in_=ktf[:]), reads=[B_kt], writes=[B_kt, B_pos])
        dop("dve", lambda e: e.tensor_copy(out=ktf[:], in_=kti[:]), reads=[B_kt], writes=[B_kt])
        dop("dve", lambda e: e.tensor_scalar(out=dst[:], in0=ang[:], scalar1=shift, scalar2=None, op0=ALU.add),
             reads=[B_ang], writes=[bdst])
        for cc in (C1, C2):
            dop("dve", lambda e, cc=cc: e.scalar_tensor_tensor(out=dst[:], in0=ktf[:], scalar=-cc, in1=dst[:],
                                                                op0=ALU.mult, op1=ALU.add),
                 reads=[B_kt, bdst], writes=[bdst])
        dop("dve", lambda e: e.tensor_scalar(out=ktf[:], in0=dst[:], scalar1=math.pi, scalar2=-TWO_PI,
                                              op0=ALU.is_gt, op1=ALU.mult), reads=[bdst], writes=[B_kt])
        dop("dve", lambda e: e.tensor_tensor(out=dst[:], in0=dst[:], in1=ktf[:], op=ALU.add),
             reads=[B_kt, bdst], writes=[bdst])
        dop("dve", lambda e: e.tensor_scalar(out=dst[:], in0=dst[:], scalar1=3.1415925, scalar2=-3.1415925,
                                              op0=ALU.min, op1=ALU.max), reads=[bdst], writes=[bdst])

    range_reduce(stab, B_tab, 0.0)
    range_reduce(ctab, B_tab, 0.5 * math.pi)
    dop("act", lambda e: e.activation(out=stab[:], in_=stab[:], func=AF.Sin), reads=[B_tab], writes=[B_tab])
    dop("act", lambda e: e.activation(out=ctab[:], in_=ctab[:], func=AF.Sin), reads=[B_tab], writes=[B_tab])
    dop("dve", lambda e: e.tensor_scalar(out=stab[:], in0=stab[:], scalar1=pcol("sgn"), scalar2=None,
                                          op0=ALU.mult), reads=[B_tab, B_const], writes=[B_tab])
    dop("dve", lambda e: e.tensor_scalar(out=ctab[:], in0=ctab[:], scalar1=-1.0, scalar2=pcol("rotm"),
                                          op0=ALU.add, op1=ALU.mult), reads=[B_tab, B_const], writes=[B_tab])
    dop("dve", lambda e: e.tensor_scalar(out=ctab[:], in0=ctab[:], scalar1=1.0, scalar2=None,
                                          op0=ALU.add), reads=[B_tab], writes=[B_tab])

    def bc8(nm):
        return pc[:, PC[nm]:PC[nm] + 8].unsqueeze(2).to_broadcast([128, 8, 128])

    def p0_pre(t):
        j = t % 3
        P.dma(xs[j][:], x_ext[t * 128:(t + 1) * 128, :], writes=[B_xs[j]], sembuf=B_xs[j])

    def p0_front(t):
        j = t % 3
        ln_stats(xs[j], 128, stats[:, t, 0:1], stats[:, t, 1:2], [B_xs[j]], [B_stats[t]])

    def p0_back(t):
        i = t % 2
        j = t % 3
        P.op("dve", lambda e: e.tensor_scalar(out=xh[i][:], in0=xs[j][:], scalar1=stats[:, t, 0:1],
                                              scalar2=stats[:, t, 1:2], op0=ALU.subtract, op1=ALU.mult),
             reads=[B_xs[j], B_stats[t]], writes=[B_xh[i]])
        pb = t % 2
        psb = psS[pb][:, 0:512].bitcast(BF16)
        for k in range(8):
            P.op("pe", lambda e, k=k: e.transpose(psb[:, k * 128:(k + 1) * 128], xh[i][:, k * 128:(k + 1) * 128], ident),
                 reads=[B_xh[i], B_const], writes=[B_psS[pb]])
        for k in range(8):
            P.op("act", lambda e, k=k: e.activation(
                out=hT[:, k, t * 128:(t + 1) * 128], in_=psb[:, k * 128:(k + 1) * 128], func=AF.Identity,
                scale=pcol("lnin_g", k), bias=pcol("lnin_b", k)), reads=[B_psS[pb], B_const], writes=[B_hT[t]])

    p0_pre(0)
    p0_pre(1)
    P.dma(posi[:], pos_d.partition_broadcast(128), writes=[B_pos], sembuf=B_pos)
    for t in range(NXT + 1):
        if t < NXT:
            p0_front(t)
        if t >= 1:
            p0_back(t - 1)
        if t + 2 < NXT:
            p0_pre(t + 2)
        if deferred:
            a_, k_ = deferred.pop(0)
            P.op(*a_, **k_)
    while deferred:
        a_, k_ = deferred.pop(0)
        P.op(*a_, **k_)

    tap("hT", hT[:], B_hT)
    tap("ctab", ctab[:], [B_tab])
    tap("stab", stab[:], [B_tab])
    if stop_after == 0:
        P.emit(final)
        ph0.close(); stM.close(); stack.close()
        return nc
    P.barrier(None)
    ph0.close()

    phM = ExitStack()
    seek(MIX_END)
    stw = [sb("stw%d" % i, [128, 2, 128], F32, phM) for i in range(2)]
    stw.append(stw[0])
    wbf = [sb("wbf%d" % i, [128, 8, 128], BF16, phM) for i in range(3)]
    B_stw = [Buf("stw%d" % i) for i in range(2)]
    B_stw.append(B_stw[0])
    wtmp = [stw[0][:, 0:2, :].bitcast(BF16)[:, :, :].rearrange("p a b -> p (a b)")[:, 0:512].rearrange("p (k c) -> p k c", k=8),
            stw[1][:, 0:2, :].bitcast(BF16)[:, :, :].rearrange("p a b -> p (a b)")[:, 0:512].rearrange("p (k c) -> p k c", k=8)]
    B_wtmp = [Buf("wtmp0"), Buf("wtmp1")]
    B_wbf = [Buf("wbf%d" % i) for i in range(3)]
    QT = sb("QT", [128, NMID], BF16, phM)
    KT = sb("KT", [128, EXT], BF16, phM)
    NVMAX = max(len(PLANS[g][1]) for g in GROUPS)
    VT = sb("VT", [128, NVMAX, 192], BF16, phM)
    acc = sb("acc", [128, 2, NMID], F32, phM)
    Pt = [sb("Pt%d" % i, [128, 2, 3, 128], BF16, phM) for i in range(3)]
    zb = [sb("zb%d" % i, [128, 512], BF16, phM) for i in range(2)]
    t1 = [sb("t1%d" % i, [128, 512], F32, phM) for i in range(2)]
    t2 = [sb("t2_0", [128, 512], F32, phM)]
    t2.append(t2[0])
    rtmp = sb("rtmp", [128, 2, 256], F32, phM)
    B_QT, B_KT, B_acc, B_rtmp = Buf("QT"), Buf("KT"), Buf("acc"), Buf("rtmp")
    B_rtmp2 = [Buf("rtmpA"), Buf("rtmpB")]
    B_VT = [Buf("VT%d" % i) for i in range(NVMAX)]
    B_Pt = [Buf("Pt%d" % i) for i in range(3)]
    SSETS = [(psS[0], [B_psS[0]]), (psS[1], [B_psS[1]]), (psZZ, [B_psB[1], B_psB[2]])]
    B_zb = [Buf("zb%d" % i) for i in range(2)]
    B_t1 = [Buf("t1%d" % i) for i in range(2)]
    B_t2 = [Buf("t2_0")]
    B_t2.append(B_t2[0])
    psUD, psSW = psB[0], psB[3]
    B_psUD, B_psSW = B_psB[0], B_psB[3]
    psZ = [psB[1], psB[2]]
    B_psZ = [B_psB[1], B_psB[2]]
    ctr = {"z": 0, "r": 0, "s": 0}

    def w_dma(role, col0, ncols, dup):
        src = w_in[:, col0:col0 + ncols].rearrange("(k p) c -> p k c", p=128)
        if dup:
            P.dma(wtmp[role - 1][:], src, writes=[B_wtmp[role - 1]], sembuf=B_wtmp[role - 1], eng="pool")
            for hh in range(2):
                P.op("act", lambda e, hh=hh: e.copy(out=wbf[role][:, :, 64 * hh:64 * hh + 64], in_=wtmp[role - 1][:]),
                     reads=[B_wtmp[role - 1]], writes=[B_wbf[role]])
        else:
            P.dma(wbf[role][:], src, writes=[B_wbf[role]], sembuf=B_wbf[role], eng="pool")

    def w_cast(role):
        pass

    def ht_bufs(e0, n):
        return B_hT[e0 // 128:(e0 + n - 1) // 128 + 1]

    def rope_block(role, esl, n, dst_ap, dst_buf, hbufs):
        zi = ctr["z"] % 2
        ctr["z"] += 1
        pz, bpz = psZ[zi], B_psZ[zi]
        for k in range(8):
            P.op("pe", lambda e, k=k: e.matmul(pz[:, 0:n], lhsT=wbf[role][:, k, :], rhs=hT[:, k, esl],
                                               start=(k == 0), stop=(k == 7)),
                 reads=[B_wbf[role]] + hbufs, writes=[bpz])
        ri = ctr["r"] % 2
        ctr["r"] += 1
        P.op("act", lambda e: e.copy(out=zb[ri][:, 0:n], in_=pz[:, 0:n]), reads=[bpz], writes=[B_zb[ri]])

        def post():
            P.op("pe", lambda e: e.matmul(psSW[:, 0:n], lhsT=perm, rhs=zb[ri][:, 0:n], start=True, stop=True),
                 reads=[B_zb[ri], B_const], writes=[B_psSW])
            P.op("pool", lambda e: e.tensor_tensor(out=t1[ri][:, 0:n], in0=zb[ri][:, 0:n], in1=ctab[:, esl], op=ALU.mult),
                 reads=[B_zb[ri], B_tab], writes=[B_t1[ri]])
            P.op("dve", lambda e: e.tensor_tensor(out=t2[ri][:, 0:n], in0=psSW[:, 0:n], in1=stab[:, esl], op=ALU.mult),
                 reads=[B_psSW, B_tab], writes=[B_t2[ri]])
            P.op("dve", lambda e: e.tensor_tensor(out=dst_ap, in0=t1[ri][:, 0:n], in1=t2[ri][:, 0:n], op=ALU.add),
                 reads=[B_t1[ri], B_t2[ri]], writes=[dst_buf])
        prev = pend[0]
        pend[0] = post
        if prev is not None:
            prev()

    pend = [None]

    def flush_rope():
        if pend[0] is not None:
            pend[0]()
            pend[0] = None

    def do_unit(gname, qcol, kcol, vcol, dup, out_chunk, esk_cols, mode, hook_v=None, hook_att=None,
                prev_norm=None, reuse_kv=False):
        r, W = GROUPS[gname]
        plan, vts, emin, emax = PLANS[gname]
        stage = dbg.get("_stage", "all")
        for b4 in range(1 if stage == "q1" else 4):
            e0 = OWN0 + 512 * b4
            rope_block(0, slice(e0, e0 + 512), 512, QT[:, 512 * b4:512 * (b4 + 1)], B_QT, ht_bufs(e0, 512))
        if stage in ("q1", "q4"):
            flush_rope()
            return
        rope_block(0, slice(EXTRA_E[0], EXTRA_E[1] + 1, EXTRA_E[1] - EXTRA_E[0]), 2, QT[:, TOWN:TOWN + 2], B_QT,
                   [B_hT[EXTRA_E[0] // 128], B_hT[EXTRA_E[1] // 128]])
        if stage == "q":
            flush_rope()
            return
        if not reuse_kv:
            e0 = emin
            while e0 < emax:
                n = min(512, emax - e0)
                rope_block(1, slice(e0, e0 + n), n, KT[:, e0:e0 + n], B_KT, ht_bufs(e0, n))
                e0 += n
        flush_rope()
        pparts = list(prev_norm) if prev_norm else []
        if pparts:
            pparts.pop(0)()
        if hook_v is not None:
            hook_v()
        if stage == "k":
            return
        if not reuse_kv:
            nvt = len(vts)
            vb0 = VT_BASE[gname]
            P.op("dve", lambda e: e.tensor_copy(out=VT[:, 0:nvt, 64:128],
                                                in_=kvt[:, vb0:vb0 + nvt].unsqueeze(2).to_broadcast([128, nvt, 64])),
                 reads=[B_const], writes=B_VT[0:nvt])
            for (rr, c, ks, nk), vi in vts.items():
                zi = ctr["z"] % 2
                ctr["z"] += 1
                pz, bpz = psZ[zi], B_psZ[zi]
                tsl = slice(c + rr * ks, c + rr * (ks + nk - 1) + 1, rr)
                for k in range(8):
                    P.op("pe", lambda e, k=k, tsl=tsl, nk=nk, pz=pz: e.matmul(
                        pz[0:nk, 0:128], lhsT=hT[:, k, tsl], rhs=wbf[2][:, k, :], start=(k == 0), stop=(k == 7)),
                        reads=[B_wbf[2]] + B_hT, writes=[bpz])
                kc = VT_BASE[gname] + vi
                P.op("act", lambda e, nk=nk, vi=vi, pz=pz, kc=kc: e.activation(
                    out=VT[0:nk, vi, :].rearrange("p (a b) -> p a b", a=3)[:, 0:3:2, :],
                    in_=pz[0:nk, 0:128].rearrange("p (a b) -> p a b", a=2), func=AF.Copy, scale=kvt[0:nk, kc:kc + 1]),
                    reads=[bpz, B_const], writes=[B_VT[vi]])
                if pparts:
                    pparts.pop(0)()
        while pparts:
            pparts.pop(0)()
        if stage == "v":
            return
        def front(bi):
            blk, kts = plan[bi]
            c, ja, nq = blk
            qc0 = e_mid_col(c + r * ja)
            qsl = slice(qc0, qc0 + r * (nq - 1) + 1, r)
            si = bi % 3
            pS, bpS_l = SSETS[si]
            pt, bpt = Pt[si], B_Pt[si]
            T = len(kts)
            for ti, (rr, cc, ks, nk, dl) in enumerate(kts):
                ksl = slice(cc + rr * ks, cc + rr * (ks + nk - 1) + 1, rr)
                for h in range(2):
                    o0 = h * 512 + (T - 1 - ti) * 128
                    P.op("pe", lambda e: e.matmul(
                        pS[0:nk, o0:o0 + nq], lhsT=KT[64 * h:64 * h + 64, ksl], rhs=QT[64 * h:64 * h + 64, qsl],
                        start=True, stop=True), reads=[B_KT, B_QT], writes=bpS_l)
            pv4 = pS[:, :].rearrange("p (h t q) -> p h t q", h=2, t=4)
            if all(k[3] == 128 for k in kts):
                P.op("act", lambda e: e.activation(out=pt[:, :, 0:T, 0:nq], in_=pv4[:, :, 0:T, 0:nq],
                                                   func=AF.Exp, scale=0.125), reads=bpS_l, writes=[bpt])
            else:
                for ti, (rr, cc, ks, nk, dl) in enumerate(kts):
                    P.op("act", lambda e: e.activation(out=pt[0:nk, :, T - 1 - ti, 0:nq], in_=pv4[0:nk, :, T - 1 - ti, 0:nq],
                                                       func=AF.Exp, scale=0.125), reads=bpS_l, writes=[bpt])
            if nq == 128 and all(k[3] == 128 for k in kts):
                mp = mpk[W]
                P.op("dve", lambda e: e.tensor_tensor(out=pt[:, :, 0:T, :], in0=pt[:, :, 0:T, :],
                                                      in1=mp.unsqueeze(1).to_broadcast([128, 2, T, 128]), op=ALU.mult),
                     reads=[bpt, B_const], writes=[bpt])
            else:
                for ti, (rr, cc, ks, nk, dl) in enumerate(kts):
                    if dl - (nk - 1) >= -W and dl + nq - 1 <= W:
                        continue
                    msl = bnd[W][0:nk, dl + 128:dl + 128 + nq].unsqueeze(1).to_broadcast([nk, 2, nq])
                    P.op("dve", lambda e: e.tensor_tensor(out=pt[0:nk, :, T - 1 - ti, 0:nq], in0=pt[0:nk, :, T - 1 - ti, 0:nq],
                                                          in1=msl, op=ALU.mult), reads=[bpt, B_const], writes=[bpt])

        def back(bi):
            blk, kts = plan[bi]
            c, ja, nq = blk
            qc0 = e_mid_col(c + r * ja)
            qsl = slice(qc0, qc0 + r * (nq - 1) + 1, r)
            si = bi % 3
            pt, bpt = Pt[si], B_Pt[si]
            T = len(kts)
            for h in range(2):
                for ti, (rr, cc, ks, nk, dl) in enumerate(kts):
                    vi = vts[(rr, cc, ks, nk)]
                    P.op("pe", lambda e: e.matmul(
                        psUD[:, h * 128:h * 128 + nq], lhsT=VT[0:nk, vi, 64 * h:64 * h + 128],
                        rhs=pt[0:nk, h, T - 1 - ti, 0:nq], start=(ti == 0), stop=(ti == T - 1)),
                        reads=[bpt, B_VT[vi]], writes=[B_psUD])
            udv = psUD[:, 0:256].rearrange("p (h q) -> p h q", h=2)[:, :, 0:nq]
            if mode in ("A", "first"):
                P.op("dve", lambda e: e.tensor_copy(out=acc[:, :, qsl], in_=udv), reads=[B_psUD], writes=[B_acc])
            else:
                P.op("dve", lambda e: e.tensor_tensor(out=acc[:, :, qsl], in0=udv, in1=acc[:, :, qsl], op=ALU.add),
                     reads=[B_psUD, B_acc], writes=[B_acc])

        if hook_att is not None:
            hook_att()
        nb = len(plan)
        for bi in range(nb + 2):
            if bi < nb:
                front(bi)
            if bi >= 2:
                back(bi - 2)
        def norm_parts():
            parts = []
            if mode not in ("last", "A"):
                return parts
            if mode == "A":
                b0 = esk[64:128, esk_cols[0]:esk_cols[0] + 1]
                b1 = esk[0:64, esk_cols[1]:esk_cols[1] + 1]
            else:
                b0 = pcol("zero", 0, 64, 128)
                b1 = pcol("zero", 0, 0, 64)

            def lnpart():
                P.op("act", lambda e: e.activation(out=acc[64:128, 0, :], in_=acc[64:128, 0, :], func=AF.Ln, bias=b0),
                     reads=[B_acc, B_const], writes=[B_acc])
                P.op("act", lambda e: e.activation(out=acc[0:64, 1, :], in_=acc[0:64, 1, :], func=AF.Ln, bias=b1),
                     reads=[B_acc, B_const], writes=[B_acc])
            parts.append(lnpart)
            c0 = 0
            bix = 0
            while c0 < NMID:
                n = min(128, NMID - c0)

                def blk(c0=c0, n=n, o=128 * (bix % 2), brt=B_rtmp2[bix % 2]):
                    csl = slice(c0, c0 + n)
                    P.op("act", lambda e: e.activation(out=rtmp[0:64, 0, o:o + n], in_=acc[64:128, 0, csl], func=AF.Exp, scale=-1.0),
                         reads=[B_acc], writes=[brt])
                    P.op("act", lambda e: e.activation(out=rtmp[64:128, 1, o:o + n], in_=acc[0:64, 1, csl], func=AF.Exp, scale=-1.0),
                         reads=[B_acc], writes=[brt])
                    P.op("dve", lambda e: e.tensor_tensor(out=mixedT[0:64, out_chunk, csl], in0=acc[0:64, 0, csl],
                                                          in1=rtmp[0:64, 0, o:o + n], op=ALU.mult),
                         reads=[B_acc, brt], writes=[B_mixed[out_chunk]])
                    P.op("dve", lambda e: e.tensor_tensor(out=mixedT[64:128, out_chunk, csl], in0=acc[64:128, 1, csl],
                                                          in1=rtmp[64:128, 1, o:o + n], op=ALU.mult),
                         reads=[B_acc, brt], writes=[B_mixed[out_chunk]])
                parts.append(blk)
                c0 += n
                bix += 1
            return parts

        return norm_parts()

    ulist = []
    for ch in range(4):
        g = ch // 2
        ulist.append(("A", QA0 + 128 * ch, KA0 + 64 * g, VA0 + 64 * g, True, ch, (2 * ch, 2 * ch + 1), "A"))
    for ch in range(4):
        for gi, gname in enumerate(("g1", "g2", "g3")):
            ulist.append((gname, QB0 + 512 * gi + 128 * ch, KB0 + 512 * gi + 128 * ch, VB0 + 512 * gi + 128 * ch, False,
                          4 + ch, None, ("first", "mid", "last")[gi]))
    NU = len(ulist)

    def reuse(ui):
        return ulist[ui][0] == "A" and ui % 2 == 1

    def dmaQ(ui):
        if ui < NU:
            w_dma(0, ulist[ui][1], 128, False)

    def dmaK(ui):
        if ui < NU and not reuse(ui):
            w_dma(1, ulist[ui][2], 64 if ulist[ui][4] else 128, ulist[ui][4])

    def dmaV(ui):
        if ui < NU and not reuse(ui):
            w_dma(2, ulist[ui][3], 64 if ulist[ui][4] else 128, ulist[ui][4])

    def castQ(ui):
        if ui < NU:
            w_cast(0)

    def castK(ui):
        if ui < NU and not reuse(ui):
            w_cast(1)

    def castV(ui):
        if ui < NU and not reuse(ui):
            w_cast(2)

    dmaQ(0); dmaK(0); dmaV(0)
    pnorm = None
    for ui, uu in enumerate(ulist):
        def hook_att(ui=ui):
            dmaQ(ui + 1)
            dmaK(ui + 1)
            dmaV(ui + 1)
        pnorm = do_unit(*uu, hook_v=None, hook_att=hook_att, prev_norm=pnorm, reuse_kv=reuse(ui))
    for pp in (pnorm or []):
        pp()
    tap("mixedT", mixedT[:], B_mixed)
    tap("QT", QT[:], [B_QT])
    tap("KT", KT[:], [B_KT])
    tap("wbf", wbf[1][:], [B_wbf[1]])
    if stop_after == 1:
        P.emit(final)
        phM.close(); stM.close(); stack.close()
        return nc
    P.barrier(None)
    phM.close()
    stM.close()

    TT = [(128 * i, 128) for i in range(16)] + [(TOWN, 2)]
    NTT = len(TT)
    seek(0)
    wst = [sb("wst%d" % i, [128, 8, 256], F32) for i in range(2)]
    B_wst = [Buf("wst%d" % i) for i in range(2)]
    wctr = [0]

    def load_big(dst, bdst, wd, col0, ncols, kchunks=8, rscale=None, row0=0):
        c = 0
        while c < ncols:
            n = min(256, ncols - c)
            i = wctr[0] % 2
            wctr[0] += 1
            src = wd[row0:row0 + 128 * kchunks, col0 + c:col0 + c + n].rearrange("(k p) c -> p k c", p=128)
            P.dma(wst[i][:, 0:kchunks, 0:n], src, writes=[B_wst[i]], sembuf=B_wst[i])
            if rscale is None:
                P.op("act", lambda e, i=i, n=n, c=c: e.copy(out=dst[:, 0:kchunks, c:c + n],
                                                            in_=wst[i][:, 0:kchunks, 0:n]),
                     reads=[B_wst[i]], writes=[bdst])
            else:
                for k in range(kchunks):
                    P.op("act", lambda e, i=i, n=n, c=c, k=k: e.activation(
                        out=dst[:, k, c:c + n], in_=wst[i][:, k, 0:n], func=AF.Copy,
                        scale=pcol(rscale[k][0], rscale[k][1])), reads=[B_wst[i], B_const], writes=[bdst])
            c += n

    def load_cast(dst, bdst, wd, col0, ncols, kchunks=8, row0=0):
        src = wd[row0:row0 + 128 * kchunks, col0:col0 + ncols].rearrange("(k p) c -> p k c", p=128)
        P.dma(dst, src, writes=[bdst], sembuf=bdst, eng="pool")

    bcg = sb("bcg", [128, D], F32)
    bcb = sb("bcb", [128, D], F32)
    B_bc = Buf("bc")

    def load_bc(gi):
        P.dma(bcg[:], rows_d[gi:gi + 1, :].partition_broadcast(128), writes=[B_bc], sembuf=B_bc)
        P.dma(bcb[:], rows_d[gi + 1:gi + 2, :].partition_broadcast(128), writes=[B_bc], sembuf=B_bc)
        P.op("dve", lambda e: e.tensor_scalar(out=bcg[:], in0=bcg[:], scalar1=ALPHA, scalar2=None, op0=ALU.mult),
             reads=[B_bc], writes=[B_bc])
        P.op("dve", lambda e: e.tensor_scalar(out=bcb[:], in0=bcb[:], scalar1=ALPHA, scalar2=None, op0=ALU.mult),
             reads=[B_bc], writes=[B_bc])

    hres_d = dram("hres_scratch", [NTT * 128, D], F32, kind="Internal")
    B_hres = [Buf("hres%d" % t) for t in range(NTT)]
    hn = [sb("hn%d" % i, [128, D], F32) for i in range(2)]
    B_hn = [Buf("hn%d" % i) for i in range(2)]
    hxT = sb("hxT", [128, 8, NMID], BF16)
    B_hxT = [Buf("hxT%d" % t) for t in range(NTT)]
    uw = [sb("uw%d" % i, [128, D], F32) for i in range(2)]
    B_uw = [Buf("uw%d" % i) for i in range(2)]
    ub = [sb("ub%d" % i, [128, D], BF16) for i in range(2)]
    B_ub = [Buf("ub%d" % i) for i in range(2)]
    TAIL_END = [0]

    def ln_front(t, n, u, bu):
        i = t % 2
        ln_stats(u, n, mr[i][0:n, 0:1], mr[i][0:n, 1:2], [bu], [B_mr[i]])

    def ln_mid(t, n, u, bu):
        i = t % 2
        P.op("dve", lambda e: e.tensor_scalar(out=hn[i][0:n, :], in0=u[0:n, :], scalar1=mr[i][0:n, 0:1],
                                              scalar2=mr[i][0:n, 1:2], op0=ALU.subtract, op1=ALU.mult),
             reads=[bu, B_mr[i]], writes=[B_hn[i]])
        P.dma(hres_d[t * 128:t * 128 + n, :], hn[i][0:n, :], reads=[B_hn[i]], writes=[B_hres[t]], sembuf=B_hn[i],
              eng="pool")
        P.op("act", lambda e: e.copy(out=ub[i][0:n, :], in_=hn[i][0:n, :]), reads=[B_hn[i]],
             writes=[B_ub[i]])

    def ln_back(t, n, c0, gname, bname):
        i = t % 2
        pb = t % 2
        psb = psB[pb][:, 0:512].bitcast(BF16)
        for k in range(8):
            P.op("pe", lambda e, k=k: e.transpose(psb[:, k * 128:k * 128 + n], ub[i][0:n, k * 128:(k + 1) * 128],
                                                  ident[0:n, 0:n]),
                 reads=[B_ub[i], B_const], writes=[B_psB[pb]])
        for k in range(8):
            P.op("act", lambda e, k=k: e.activation(out=hxT[:, k, c0:c0 + n], in_=psb[:, k * 128:k * 128 + n],
                                                    func=AF.Identity, scale=pcol(gname, k), bias=pcol(bname, k)),
                 reads=[B_psB[pb], B_const], writes=[B_hxT[t]])

    def pipeline3(nt, front, mid, back, pre=None):
        if pre is not None:
            pre(0)
        for t in range(nt + 2):
            if pre is not None and t + 1 < nt:
                pre(t + 1)
            if t < nt:
                front(t)
            if 1 <= t <= nt:
                mid(t - 1)
            if t >= 2 and back is not None:
                back(t - 2)

    def flush_ln():
        pass

    phW = ExitStack()
    TAIL_END[0] = cur[0]
    woT = sb("woT", [128, 8, D], BF16, phW)
    B_wo = Buf("wo")
    xr = [sb("xr%d" % i, [128, D], F32, phW) for i in range(2)]
    B_xr = [Buf("xr%d" % i) for i in range(2)]
    rs = [sb("rs%d" % i, [128, 8], F32, phW) for i in range(2)]
    B_rs = [Buf("rs%d" % i) for i in range(2)]
    assert cur[0] <= MIX_END - 8 * NMID * 2, cur[0]
    seek(MIX_END)
    sq = sb("sq", [128, 8, NMID], BF16, phW)
    B_sq = Buf("sq")
    load_cast(woT[:], B_wo, w_out, 0, D)
    for k in range(8):
        nm = ("gwin", k) if k < 4 else ("gdil", k - 4)
        P.op("act", lambda e, k=k, nm=nm: e.activation(out=woT[:, k, :], in_=woT[:, k, :], func=AF.Copy,
                                                       scale=pcol(nm[0], nm[1])), reads=[B_wo, B_const], writes=[B_wo])
    load_bc(0)
    for k in range(8):
        P.op("act", lambda e, k=k: e.activation(out=sq[:, k, :], in_=mixedT[:, k, :], func=AF.Square),
             reads=[B_mixed[k]], writes=[B_sq])
    psSS = psB[3]
    B_psSS = B_psB[3]
    def w_pre(t):
        c0, n = TT[t]
        i = t % 2
        if n == 128:
            e0 = OWN0 + c0
            P.dma(xr[i][:], x_ext[e0:e0 + 128, :], writes=[B_xr[i]], sembuf=B_xr[i])
        else:
            P.dma(xr[i][0:1, :], x_ext[EXTRA_E[0]:EXTRA_E[0] + 1, :], writes=[B_xr[i]], sembuf=B_xr[i])
            P.dma(xr[i][1:2, :], x_ext[EXTRA_E[1]:EXTRA_E[1] + 1, :], writes=[], sembuf=B_xr[i])
            B_xr[i].w = ("dma", B_xr[i].sem, B_xr[i].cnt)

    def w_front(t):
        c0, n = TT[t]
        i = t % 2
        for half in range(2):
            for k in range(4):
                P.op("pe", lambda e, half=half, k=k: e.matmul(
                    psSS[0:n, 2 * i + half:2 * i + half + 1], lhsT=sq[:, 4 * half + k, c0:c0 + n], rhs=ones_bf[:, 0:1],
                    start=(k == 0), stop=(k == 3)), reads=[B_sq, B_const], writes=[B_psSS])
        P.op("act", lambda e: e.activation(out=rs[i][0:n, 0:2], in_=psSS[0:n, 2 * i:2 * i + 2], func=AF.Ln,
                                           bias=pcol("eps", 0, 0, n), scale=1.0 / 512.0),
             reads=[B_psSS, B_const], writes=[B_rs[i]])
        P.op("act", lambda e: e.activation(out=rs[i][0:n, 2:4], in_=rs[i][0:n, 0:2], func=AF.Exp, scale=-0.5),
             reads=[B_rs[i]], writes=[B_rs[i]])
        if n == 128:
            e0 = OWN0 + c0
            st_m, st_r = stats[:, e0 // 128, 0:1], stats[:, e0 // 128, 1:2]
            xin = xr[i]
        else:
            ln_stats(xr[i], n, mr[i][0:n, 2:3], mr[i][0:n, 3:4], [B_xr[i]], [B_mr[i]])
            st_m, st_r = mr[i][0:n, 2:3], mr[i][0:n, 3:4]
            xin = xr[i]
        u, bu = uw[i], B_uw[i]
        P.op("dve", lambda e, xin=xin, st_m=st_m: e.scalar_tensor_tensor(
            out=u[0:n, :], in0=xin[0:n, :], scalar=st_m[0:n, :], in1=bcg[0:n, :], op0=ALU.subtract, op1=ALU.mult),
            reads=[B_xr[i], B_bc, B_mr[i]] + B_stats, writes=[bu])
        P.op("dve", lambda e, st_r=st_r: e.scalar_tensor_tensor(
            out=u[0:n, :], in0=u[0:n, :], scalar=st_r[0:n, :], in1=bcb[0:n, :], op0=ALU.mult, op1=ALU.add),
            reads=[bu, B_bc, B_mr[i]] + B_stats, writes=[bu])
        for half in range(2):
            for hc in range(2):
                pb = psS[half][:, 512 * hc:512 * (hc + 1)]
                bpb = B_psS[half]
                for k in range(4):
                    P.op("pe", lambda e, half=half, hc=hc, k=k, pb=pb: e.matmul(
                        pb[0:n, :], lhsT=mixedT[:, 4 * half + k, c0:c0 + n], rhs=woT[:, 4 * half + k, 512 * hc:512 * (hc + 1)],
                        start=(k == 0), stop=(k == 3)), reads=[B_mixed[4 * half + k], B_wo], writes=[bpb])
        for half in range(2):
            for hc in range(2):
                pb = psS[half][:, 512 * hc:512 * (hc + 1)]
                bpb = B_psS[half]
                P.op("dve", lambda e, half=half, hc=hc, pb=pb: e.scalar_tensor_tensor(
                    out=u[0:n, 512 * hc:512 * (hc + 1)], in0=pb[0:n, :], scalar=rs[i][0:n, 2 + half:3 + half],
                    in1=u[0:n, 512 * hc:512 * (hc + 1)], op0=ALU.mult, op1=ALU.add),
                    reads=[bpb, B_rs[i], bu], writes=[bu])
        ln_front(t, n, u, bu)

    pipeline3(NTT, w_front, lambda t: ln_mid(t, TT[t][1], uw[t % 2], B_uw[t % 2]),
              lambda t: ln_back(t, TT[t][1], TT[t][0], "ln1_g", "ln1_b"), pre=w_pre)

    flush_ln()
    if stop_after == 2:
        P.emit(final)
        phW.close(); stack.close()
        return nc
    P.barrier(None)
    phW.close()

    phX = ExitStack()
    seek(TAIL_END[0])
    hr = [sb("hr%d" % i, [128, D], F32, phX) for i in range(2)]
    B_hr = [Buf("hr%d" % i) for i in range(2)]
    wA = sb("wA", [128, 8, D], BF16, phX)
    wB = sb("wB", [128, 8, D], BF16, phX)
    B_wA, B_wB = Buf("wA"), Buf("wB")
    memT = sb("memT", [128, 8, MEM], BF16, phX)
    kxT = sb("kxT", [128, 8, MEM], BF16, phX)
    vx = sb("vx", [128, 2, D], BF16, phX)
    qxT = sb("qxT", [128, 8, NMID], BF16, phX)
    oxT = sb("oxT", [128, 8, NMID], BF16, phX)
    Px = [sb("Px%d" % i, [128, 2, 512], BF16, phX) for i in range(2)]
    rx = [sb("rx%d" % i, [128, 512], F32, phX) for i in range(2)]
    B_memT, B_kxT, B_vx, B_qxT, B_oxT = Buf("memT"), Buf("kxT"), Buf("vx"), Buf("qxT"), Buf("oxT")
    B_Px = [Buf("Px%d" % i) for i in range(2)]
    B_rx = [Buf("rx%d" % i) for i in range(2)]
    load_cast(wA[:], B_wA, w_xk, 0, D)
    load_cast(wB[:], B_wB, w_xv, 0, D)
    for mt in range(2):
        i = mt % 2
        u, bu = uw[i], B_uw[i]
        P.dma(u[:], mem_d[mt * 128:(mt + 1) * 128, :], writes=[bu], sembuf=bu)
        ln_stats(u, 128, mr[i][:, 0:1], mr[i][:, 1:2], [bu], [B_mr[i]])
        P.op("dve", lambda e, u=u, i=i: e.tensor_scalar(out=ub[i][:], in0=u[:], scalar1=mr[i][:, 0:1],
                                                        scalar2=mr[i][:, 1:2], op0=ALU.subtract, op1=ALU.mult),
             reads=[bu, B_mr[i]], writes=[B_ub[i]])
        psb = psS[i][:, 0:512].bitcast(BF16)
        for k in range(8):
            P.op("pe", lambda e, k=k, i=i, psb=psb: e.transpose(psb[:, k * 128:(k + 1) * 128],
                                                                 ub[i][:, k * 128:(k + 1) * 128], ident),
                 reads=[B_ub[i], B_const], writes=[B_psS[i]])
        for k in range(8):
            P.op("act", lambda e, mt=mt, k=k, psb=psb: e.activation(
                out=memT[:, k, mt * 128:(mt + 1) * 128], in_=psb[:, k * 128:(k + 1) * 128], func=AF.Identity,
                scale=pcol("mem_g", k), bias=pcol("mem_b", k)), reads=[B_psS[i], B_const], writes=[B_memT])
    for c in range(8):
        pz, bpz = psB[c % 2], B_psB[c % 2]
        for k in range(8):
            P.op("pe", lambda e, c=c, k=k, pz=pz: e.matmul(pz[:, 0:MEM], lhsT=wA[:, k, c * 128:(c + 1) * 128],
                                                          rhs=memT[:, k, :], start=(k == 0), stop=(k == 7)),
                 reads=[B_wA, B_memT], writes=[bpz])
        P.op("act", lambda e, c=c, pz=pz: e.copy(out=kxT[:, c, :], in_=pz[:, 0:MEM]), reads=[bpz], writes=[B_kxT])
    for mt in range(2):
        for hc in range(2):
            pz, bpz = psB[2 + hc], B_psB[2 + hc]
            for k in range(8):
                P.op("pe", lambda e, mt=mt, hc=hc, k=k, pz=pz: e.matmul(
                    pz[:, :], lhsT=memT[:, k, mt * 128:(mt + 1) * 128], rhs=wB[:, k, 512 * hc:512 * (hc + 1)],
                    start=(k == 0), stop=(k == 7)), reads=[B_wB, B_memT], writes=[bpz])
            P.op("act", lambda e, mt=mt, hc=hc, pz=pz: e.copy(out=vx[:, mt, 512 * hc:512 * (hc + 1)], in_=pz[:, :]),
                 reads=[bpz], writes=[B_vx])
    load_cast(wA[:], B_wA, w_xq, 0, D)
    CB = [(512 * i, 512) for i in range(4)] + [(TOWN, 2)]
    zc = 0
    for c in range(8):
        for (c0, n) in CB:
            pz, bpz = psB[zc % 2], B_psB[zc % 2]
            zc += 1
            for k in range(8):
                P.op("pe", lambda e, c=c, k=k, pz=pz, c0=c0, n=n: e.matmul(
                    pz[:, 0:n], lhsT=wA[:, k, c * 128:(c + 1) * 128], rhs=hxT[:, k, c0:c0 + n],
                    start=(k == 0), stop=(k == 7)), reads=[B_wA] + B_hxT, writes=[bpz])
            P.op("act", lambda e, c=c, pz=pz, c0=c0, n=n: e.copy(out=qxT[:, c, c0:c0 + n], in_=pz[:, 0:n]),
                 reads=[bpz], writes=[B_qxT])
    load_cast(wB[:], B_wB, w_xo, 0, D)
    load_bc(2)
    xa_items = [(hd, c0, n) for hd in range(4) for (c0, n) in CB]

    def xa_front(it):
        hd, c0, n = xa_items[it]
        si = it % 2
        pS, bpS = psS[si], B_psS[si]
        for m in range(2):
            for kk in range(2):
                P.op("pe", lambda e: e.matmul(
                    pS[:, 512 * m:512 * m + n], lhsT=kxT[:, 2 * hd + kk, m * 128:(m + 1) * 128],
                    rhs=qxT[:, 2 * hd + kk, c0:c0 + n], start=(kk == 0), stop=(kk == 1)),
                    reads=[B_kxT, B_qxT], writes=[bpS])
        P.op("act", lambda e: e.activation(
            out=Px[si][:, :, 0:n], in_=pS[:, :].rearrange("p (m q) -> p m q", m=2)[:, :, 0:n], func=AF.Exp,
            scale=1.0 / 16.0), reads=[bpS], writes=[B_Px[si]])

    def xa_back(it):
        hd, c0, n = xa_items[it]
        si = it % 2
        pD, bpD = psB[2], B_psB[2]
        for m in range(2):
            P.op("pe", lambda e: e.matmul(pD[:, 0:n], lhsT=ones_bf[:, :], rhs=Px[si][:, m, 0:n],
                                          start=(m == 0), stop=(m == 1)),
                 reads=[B_Px[si], B_const], writes=[bpD])
        P.op("act", lambda e: e.activation(out=rx[si][:, 0:n], in_=pD[:, 0:n], func=AF.Ln),
             reads=[bpD], writes=[B_rx[si]])
        P.op("act", lambda e: e.activation(out=rx[si][:, 0:n], in_=rx[si][:, 0:n], func=AF.Exp, scale=-1.0),
             reads=[B_rx[si]], writes=[B_rx[si]])
        for kk in range(2):
            pO, bpO = psB[kk], B_psB[kk]
            for m in range(2):
                P.op("pe", lambda e: e.matmul(
                    pO[:, 0:n], lhsT=vx[:, m, hd * 256 + kk * 128:hd * 256 + (kk + 1) * 128], rhs=Px[si][:, m, 0:n],
                    start=(m == 0), stop=(m == 1)), reads=[B_Px[si], B_vx], writes=[bpO])
            P.op("dve", lambda e: e.tensor_tensor(
                out=oxT[:, 2 * hd + kk, c0:c0 + n], in0=pO[:, 0:n], in1=rx[si][:, 0:n], op=ALU.mult),
                reads=[bpO, B_rx[si]], writes=[B_oxT])

    for it in range(len(xa_items) + 1):
        if it < len(xa_items):
            xa_front(it)
        if it >= 1:
            xa_back(it - 1)
    def x_pre(t):
        c0, n = TT[t]
        i = t % 2
        P.dma(hr[i][0:n, :], hres_d[t * 128:t * 128 + n, :], reads=[B_hres[t]], writes=[B_hr[i]], sembuf=B_hr[i])

    def x_front(t):
        c0, n = TT[t]
        i = t % 2
        u, bu = uw[i], B_uw[i]
        P.op("dve", lambda e, t=t, u=u: e.tensor_tensor(out=u[0:n, :], in0=hr[i][0:n, :], in1=bcg[0:n, :], op=ALU.mult),
             reads=[B_hr[i], B_bc], writes=[bu])
        P.op("dve", lambda e, u=u: e.tensor_tensor(out=u[0:n, :], in0=u[0:n, :], in1=bcb[0:n, :], op=ALU.add),
             reads=[bu, B_bc], writes=[bu])
        for hc in range(2):
            pb, bpb = psS[t % 2][:, 512 * hc:512 * (hc + 1)], B_psS[t % 2]
            for k in range(8):
                P.op("pe", lambda e, hc=hc, k=k, pb=pb: e.matmul(
                    pb[0:n, :], lhsT=oxT[:, k, c0:c0 + n], rhs=wB[:, k, 512 * hc:512 * (hc + 1)],
                    start=(k == 0), stop=(k == 7)), reads=[B_oxT, B_wB], writes=[bpb])
        for hc in range(2):
            pb, bpb = psS[t % 2][:, 512 * hc:512 * (hc + 1)], B_psS[t % 2]
            P.op("dve", lambda e, hc=hc, pb=pb, u=u: e.tensor_tensor(
                out=u[0:n, 512 * hc:512 * (hc + 1)], in0=pb[0:n, :], in1=u[0:n, 512 * hc:512 * (hc + 1)], op=ALU.add),
                reads=[bpb, bu], writes=[bu])
        ln_front(t, n, u, bu)

    pipeline3(NTT, x_front, lambda t: ln_mid(t, TT[t][1], uw[t % 2], B_uw[t % 2]),
              lambda t: ln_back(t, TT[t][1], TT[t][0], "ln2_g", "ln2_b"), pre=x_pre)

    flush_ln()
    if stop_after == 3:
        P.emit(final)
        phX.close(); stack.close()
        return nc
    P.barrier(None)
    phX.close()

    phF = ExitStack()
    seek(TAIL_END[0])
    hr = [sb("hr%d" % i, [128, D], F32, phF) for i in range(2)]
    B_hr = [Buf("hrF%d" % i) for i in range(2)]
    HT = 1024
    aT = sb("aT", [128, NFF, HT], BF16, phF)
    B_aT = Buf("aT")
    wg = [sb("wg%d" % i, [128, 8, 128], BF16, phF) for i in range(2)]
    wu = [sb("wu%d" % i, [128, 8, 128], BF16, phF) for i in range(2)]
    B_wg = [Buf("wg%d" % i) for i in range(2)]
    B_wu = [Buf("wu%d" % i) for i in range(2)]
    wdn = sb("wdn", [128, NFF, D], BF16, phF)
    B_wdn = Buf("wdn")
    gp = [sb("gp0", [128, HT + 2], F32, phF)] * 2
    B_gp = [Buf("gp0")] * 2
    gc = [sb("gc0", [128, HT], F32, phF)] * 2
    B_gc = [Buf("gc0")] * 2
    ge = [sb("ge0", [128, HT], F32, phF)] * 2
    B_ge = [Buf("ge0")] * 2
    g3b = sb("g3b", [128, D], F32, phF)
    b3b = sb("b3b", [128, D], F32, phF)
    B_g3 = Buf("g3")
    B_ostg = [Buf("ostg%d" % i) for i in range(2)]
    def f_dma(g):
        j = g % NFF
        i = g % 2
        load_cast(wg[i][:], B_wg[i], w_gate, j * 128, 128)
        load_cast(wu[i][:], B_wu[i], w_up, j * 128, 128)

    f_dma(0)
    load_bc(4)
    def wdn_piece(r0, kc=2):
        load_cast(wdn[:, r0:r0 + kc, :], B_wdn, w_down, 0, D, kchunks=kc, row0=r0 * 128)
    P.dma(g3b[:], rows_d[6:7, :].partition_broadcast(128), writes=[B_g3], sembuf=B_g3)
    P.dma(b3b[:], rows_d[7:8, :].partition_broadcast(128), writes=[B_g3], sembuf=B_g3)

    def mcol(tok):
        if tok < 0:
            return TOWN
        if tok >= TOWN:
            return TOWN + 1
        return tok

    for hf in range(2):
        t0 = hf * HT
        for j in range(NFF):
            i = j % 2
            gidx = hf * NFF + j
            if gidx + 1 < 2 * NFF:
                f_dma(gidx + 1)
            if hf == 0 and 1 <= j <= NFF // 2:
                wdn_piece(2 * (j - 1))
            pieces = [(mcol(t0 - 1), 1, 0), (t0, 512, 1), (t0 + 512, 512, 513), (mcol(t0 + HT), 1, HT + 1)]
            for pi, (cc, n, go) in enumerate(pieces):
                pz, bpz = psB[pi], B_psB[pi]
                for k in range(8):
                    P.op("pe", lambda e, k=k, pz=pz, cc=cc, n=n: e.matmul(
                        pz[:, 0:n], lhsT=wg[i][:, k, :], rhs=hxT[:, k, cc:cc + n], start=(k == 0), stop=(k == 7)),
                        reads=[B_wg[i]] + B_hxT, writes=[bpz])
                if n == 1 and cc >= TOWN:
                    P.op("act", lambda e, pz=pz, go=go, cc=cc: e.activation(
                        out=gp[i][:, go:go + 1], in_=pz[:, 0:1], func=AF.Copy, scale=gval[:, cc - TOWN:cc - TOWN + 1]),
                        reads=[bpz, B_const], writes=[B_gp[i]])
                else:
                    P.op("act", lambda e, pz=pz, go=go, n=n: e.copy(out=gp[i][:, go:go + n], in_=pz[:, 0:n]),
                         reads=[bpz], writes=[B_gp[i]])
            P.op("dve", lambda e, j=j: e.tensor_scalar(out=gc[i][:], in0=gp[i][:, 0:HT], scalar1=pcol("cw0", j),
                                                       scalar2=pcol("cb", j), op0=ALU.mult, op1=ALU.add),
                 reads=[B_gp[i], B_const], writes=[B_gc[i]])
            P.op("dve", lambda e, j=j: e.scalar_tensor_tensor(out=gc[i][:], in0=gp[i][:, 1:HT + 1], scalar=pcol("cw1", j),
                                                              in1=gc[i][:], op0=ALU.mult, op1=ALU.add),
                 reads=[B_gp[i], B_gc[i], B_const], writes=[B_gc[i]])
            P.op("dve", lambda e, j=j: e.scalar_tensor_tensor(out=gc[i][:], in0=gp[i][:, 2:HT + 2], scalar=pcol("cw2", j),
                                                               in1=gc[i][:], op0=ALU.mult, op1=ALU.add),
                 reads=[B_gp[i], B_gc[i], B_const], writes=[B_gc[i]])
            P.op("act", lambda e: e.activation(out=ge[i][:], in_=gc[i][:], func=AF.Gelu), reads=[B_gc[i]],
                 writes=[B_ge[i]])
            for ub_ in range(2):
                pz, bpz = psS[j % 2][:, 512 * ub_:512 * (ub_ + 1)], B_psS[j % 2]
                for k in range(8):
                    P.op("pe", lambda e, k=k, pz=pz, ub_=ub_: e.matmul(
                        pz[:, :], lhsT=wu[i][:, k, :], rhs=hxT[:, k, t0 + 512 * ub_:t0 + 512 * (ub_ + 1)],
                        start=(k == 0), stop=(k == 7)), reads=[B_wu[i]] + B_hxT, writes=[bpz])
            for ub_ in range(2):
                pz, bpz = psS[j % 2][:, 512 * ub_:512 * (ub_ + 1)], B_psS[j % 2]
                P.op("dve", lambda e, j=j, pz=pz, ub_=ub_: e.tensor_tensor(
                    out=aT[:, j, 512 * ub_:512 * (ub_ + 1)], in0=pz[:, :], in1=ge[i][:, 512 * ub_:512 * (ub_ + 1)],
                    op=ALU.mult), reads=[bpz, B_ge[i]], writes=[B_aT])
        def f_pre(tt, hf=hf):
            t = hf * 8 + tt
            i = t % 2
            P.dma(hr[i][:], hres_d[t * 128:(t + 1) * 128, :], reads=[B_hres[t]], writes=[B_hr[i]], sembuf=B_hr[i])

        def f_front(tt, hf=hf):
            t = hf * 8 + tt
            i = t % 2
            u, bu = uw[i], B_uw[i]
            P.op("dve", lambda e, t=t, u=u: e.tensor_tensor(out=u[:], in0=hr[i][:], in1=bcg[:], op=ALU.mult),
                 reads=[B_hr[i], B_bc], writes=[bu])
            P.op("dve", lambda e, u=u: e.tensor_tensor(out=u[:], in0=u[:], in1=bcb[:], op=ALU.add),
                 reads=[bu, B_bc], writes=[bu])
            if tt == 0:
                pass
            for hc in range(2):
                pb, bpb = psS[tt % 2][:, 512 * hc:512 * (hc + 1)], B_psS[tt % 2]
                for j in range(NFF):
                    P.op("pe", lambda e, hc=hc, j=j, pb=pb, tt=tt: e.matmul(
                        pb[:, :], lhsT=aT[:, j, tt * 128:(tt + 1) * 128], rhs=wdn[:, j, 512 * hc:512 * (hc + 1)],
                        start=(j == 0), stop=(j == NFF - 1)), reads=[B_aT, B_wdn], writes=[bpb])
            for hc in range(2):
                pb, bpb = psS[tt % 2][:, 512 * hc:512 * (hc + 1)], B_psS[tt % 2]
                P.op("dve", lambda e, hc=hc, pb=pb, u=u: e.tensor_tensor(
                    out=u[:, 512 * hc:512 * (hc + 1)], in0=pb[:, :], in1=u[:, 512 * hc:512 * (hc + 1)], op=ALU.add),
                    reads=[bpb, bu], writes=[bu])
            ln_stats(u, 128, mr[i][:, 0:1], mr[i][:, 1:2], [bu], [B_mr[i]])

        def f_mid(tt, hf=hf):
            t = hf * 8 + tt
            i = t % 2
            u, bu = uw[i], B_uw[i]
            P.op("dve", lambda e, u=u, i=i: e.tensor_scalar(out=u[:], in0=u[:], scalar1=mr[i][:, 0:1],
                                                            scalar2=mr[i][:, 1:2], op0=ALU.subtract, op1=ALU.mult),
                 reads=[bu, B_mr[i]], writes=[bu])
            P.op("dve", lambda e, u=u: e.tensor_tensor(out=u[:], in0=u[:], in1=g3b[:], op=ALU.mult),
                 reads=[bu, B_g3], writes=[bu])
            P.op("dve", lambda e, u=u, i=i: e.tensor_tensor(out=u[:], in0=u[:], in1=b3b[:], op=ALU.add),
                 reads=[bu, B_g3], writes=[bu])
            final.append(P.dma(out_d[t * 128:(t + 1) * 128, :], u[:], reads=[bu], sembuf=B_ostg[i]))
        pipeline3(8, f_front, f_mid, None, pre=f_pre)
    P.emit(final)
    phF.close()
    stack.close()
    return nc


def host_consts():
    c = np.zeros((128, 1024), np.float32)
    c[:, 0:128] = np.eye(128)
    pm = np.zeros((128, 128), np.float32)
    for m in range(128):
        d = m % 64
        if d < 8:
            pm[m + 8, m] = 1.0
        elif d < 16:
            pm[m - 8, m] = 1.0
    c[:, 128:256] = pm
    r = np.arange(128)[:, None]
    xx = np.arange(384)[None, :]
    c[:, 256:640] = (np.abs(xx - 128 - r) <= 64)
    c[:, 640:1024] = (np.abs(xx - 128 - r) <= 128)
    return c.astype(ml_dtypes.bfloat16)


def host_consts2():
    r = np.arange(128)[:, None]
    cc = np.arange(128)[None, :]
    out = np.zeros((128, 640), np.float32)
    for t_ in range(2):
        out[:, t_ * 128:(t_ + 1) * 128] = (np.abs(64 - 128 * t_ + cc - r) <= 64)
    for t_ in range(3):
        out[:, 256 + t_ * 128:256 + (t_ + 1) * 128] = (np.abs(128 - 128 * t_ + cc - r) <= 128)
    return out.astype(ml_dtypes.bfloat16)


def host_pcols(inp):
    pcv = np.zeros((128, 256), np.float32)
    o = [0]

    def put(arr, n):
        a = np.asarray(arr, np.float32).reshape(n, 128).T
        pcv[:, o[0]:o[0] + n] = a
        o[0] += n

    put(inp["ln_in_g"], 8)
    put(inp["ln_in_b"], 8)
    put(inp["ln1_g"][0], 8)
    put(inp["ln1_b"][0], 8)
    put(inp["ln2_g"][0], 8)
    put(inp["ln2_b"][0], 8)
    put(inp["mem_ln_g"][0], 8)
    put(inp["mem_ln_b"][0], 8)
    put(inp["g_win"][0], 4)
    put(inp["g_dil"][0], 4)
    sk = np.asarray(inp["attn_sink"][0], np.float32)
    pcv[:, o[0]:o[0] + 8] = sk[None, :]
    o[0] += 8
    p = np.arange(128)
    d = p % 64
    invf = (np.float32(500000.0) ** (-(np.arange(0, 16, 2, dtype=np.float32)) / np.float32(16))).astype(np.float32)
    pcv[:, o[0]] = invf[d % 8]
    pcv[:, o[0] + 1] = (d < 16)
    pcv[:, o[0] + 2] = np.where(d < 8, -1.0, np.where(d < 16, 1.0, 0.0))
    pcv[:, o[0] + 3] = EPS
    o[0] += 5
    cw = np.asarray(inp["conv_w"][0], np.float32)
    for j in range(3):
        put(cw[j], NFF)
    put(inp["conv_b"][0], NFF)
    return pcv


def core_inputs(inp, core, shared):
    b = core // 4
    s0 = (core % 4) * TOWN
    e0 = s0 - HALO
    idx = np.arange(EXT) + e0
    ok = (idx >= 0) & (idx < SEQ)
    ci = np.clip(idx, 0, SEQ - 1)
    x_ext = np.where(ok[:, None], inp["x"][b][ci], np.float32(0)).astype(np.float32)
    pos = np.where(ok, inp["positions"][b][ci], 0).astype(np.int32)[None, :]
    valid = ok.astype(np.float32)
    kv = np.zeros((128, NVT), np.float32)
    for g in GROUPS:
        for (r, c, ks, nk), vi in PLANS[g][1].items():
            e = c + r * (ks + np.arange(nk))
            kv[:nk, VT_BASE[g] + vi] = valid[e]
    gv = np.array([[valid[EXTRA_E[0]], valid[EXTRA_E[1]]]], np.float32)
    m = dict(shared)
    m.update(x_ext=x_ext, pos=pos, kvtab=kv, gvalid=gv, mem=np.ascontiguousarray(inp["mem"][b], np.float32))
    return m


def shared_inputs(inp):
    f = lambda a: np.ascontiguousarray(np.asarray(a, np.float32))
    rows = np.stack([inp["ln_in_g"], inp["ln_in_b"], inp["ln1_g"][0], inp["ln1_b"][0], inp["ln2_g"][0],
                     inp["ln2_b"][0], inp["ln3_g"][0], inp["ln3_b"][0]]).astype(np.float32)
    return dict(cbf=host_consts(), pcols=host_pcols(inp), w_in=f(inp["w_in"][0]), w_out=f(inp["w_mix_out"][0]),
                w_xq=f(inp["w_xq"][0]), w_xk=f(inp["w_xk"][0]), w_xv=f(inp["w_xv"][0]), w_xo=f(inp["w_xo"][0]),
                w_gate=f(inp["w_gate"][0]), w_up=f(inp["w_up"][0]), w_down=f(inp["w_down"][0]), rows=rows)


def kernel(**inp):
    inp = {k: np.asarray(v) for k, v in inp.items()}
    nc = build_program()
    shared = shared_inputs(inp)
    in_maps = [core_inputs(inp, c, shared) for c in range(8)]
    res = run_bass_kernel_spmd(nc, in_maps, core_ids=list(range(8)))
    out = np.zeros((2, SEQ, D), np.float32)
    for c in range(8):
        out[c // 4, (c % 4) * TOWN:(c % 4 + 1) * TOWN] = res.results[c]["out"]
    return out
```

```python
import math
from contextlib import ExitStack
import numpy as np
import ml_dtypes
import concourse.bass as bass
import concourse.mybir as mybir
from concourse.bass_utils import run_bass_kernel_spmd

F32 = mybir.dt.float32
BF16 = mybir.dt.bfloat16
I32 = mybir.dt.int32
AF = mybir.ActivationFunctionType
ALU = mybir.AluOpType
AX = mybir.AxisListType

D = 1024
SEQ = 8192
TOWN = 2048
HALO = 1152
EXT = TOWN + 2 * HALO
NXT = EXT // 128
OWN0 = HALO
NMID = TOWN + 2
EXTRA_E = (OWN0 - 1, OWN0 + TOWN)
DFF = 2816
NFF = DFF // 128
MEM = 256
ALPHA = 2.0 ** 0.25
EPS = 1e-5
QA0, KA0, VA0 = 0, 512, 640
QB0, KB0, VB0 = 768, 768 + 1536, 768 + 3072
GROUPS = {"A": (1, 128), "g1": (1, 64), "g2": (4, 64), "g3": (16, 64)}
PCOLS = (("lnin_g", 8), ("lnin_b", 8), ("ln1_g", 8), ("ln1_b", 8), ("ln2_g", 8), ("ln2_b", 8), ("mem_g", 8),
         ("mem_b", 8), ("gwin", 4), ("gdil", 4), ("sink0", 8), ("invf", 1), ("rotm", 1), ("sgn", 1), ("eps", 1), ("zero", 1),
         ("cw0", 22), ("cw1", 22), ("cw2", 22), ("cb", 22))


def mid_col_e(c):
    return OWN0 + c if c < TOWN else EXTRA_E[c - TOWN]


def e_mid_col(e):
    if OWN0 <= e < OWN0 + TOWN:
        return e - OWN0
    return TOWN + EXTRA_E.index(e)


def q_blocks(r):
    out = []
    for c in range(r):
        j0 = (OWN0 - c + r - 1) // r
        j1 = (OWN0 + TOWN - c + r - 1) // r
        for ja in range(j0, j1, 128):
            out.append((c, ja, min(128, j1 - ja)))
    for e in EXTRA_E:
        out.append((e % r, e // r, 1))
    return out


def key_tiles(r, W, blk):
    c, ja, nq = blk
    lo, hi = ja - W, ja + nq + W
    tiles = []
    ks = lo
    while ks < hi:
        nk = min(128, hi - ks)
        tiles.append((r, c, ks, nk, ja - ks))
        ks += 128
    return tiles


def group_plan(gname):
    r, W = GROUPS[gname]
    blks = q_blocks(r)
    vt = {}
    plan = []
    for b in blks:
        kts = key_tiles(r, W, b)
        for (rr, c, ks, nk, dl) in kts:
            key = (rr, c, ks, nk)
            if key not in vt:
                vt[key] = len(vt)
        plan.append((b, kts))
    emin = min(k[1] + k[0] * k[2] for k in vt)
    emax = max(k[1] + k[0] * (k[2] + k[3] - 1) for k in vt)
    return plan, vt, emin, emax + 1


PLANS = {g: group_plan(g) for g in GROUPS}
VT_BASE = {}
_n = 0
for _g in GROUPS:
    VT_BASE[_g] = _n
    _n += len(PLANS[_g][1])
NVT = _n


class Buf:
    __slots__ = ("name", "w", "r", "sem", "cnt", "excl")

    def __init__(self, name, excl=False):
        self.name = name
        self.excl = excl
        self.w = None
        self.r = {}
        self.sem = None
        self.cnt = 0


class _Rec:
    def __init__(self):
        self.calls = []

    def __getattr__(self, name):
        def f(*a, **k):
            self.calls.append((name, a, k))
            return self
        return f


class Op:
    __slots__ = ("eng", "fn", "deps", "signal", "count", "dma")

    def __init__(self, eng, fn):
        self.eng = eng
        if fn is not None:
            rec = _Rec()
            fn(rec)
            calls = rec.calls

            def replay(e, calls=calls):
                ins = None
                for (name, a, k) in calls:
                    ins = getattr(e, name)(*a, **k)
                return ins
            self.fn = replay
        else:
            self.fn = None
        self.deps = []
        self.signal = False
        self.count = 0
        self.dma = None


class Prog:
    ENGS = ("pe", "act", "dve", "pool", "sp")

    def __init__(self, nc, stack):
        self.nc = nc
        self.stack = stack
        self.streams = {e: [] for e in self.ENGS}
        self.sems = {e: stack.enter_context(nc.semaphore("s_" + e)) for e in self.ENGS}
        self.nsem = 0

    def _deps(self, eng, reads, writes):
        deps = []
        for b in reads:
            if b.w is not None:
                deps.append(b.w)
            if b.excl:
                deps.extend(v for k, v in b.r.items() if k != eng)
        for b in writes:
            if b.w is not None:
                deps.append(b.w)
            deps.extend(b.r.values())
        out = []
        for d in deps:
            if d[0] == "op":
                if eng == "pe" and d[1].eng == "pe":
                    continue
                d[1].signal = True
            out.append(d)
        return out

    def op(self, eng, fn, reads=(), writes=()):
        o = Op(eng, fn)
        o.deps = self._deps(eng, reads, writes)
        d = ("op", o)
        for b in reads:
            b.r[eng] = d
        for b in writes:
            b.w = d
            b.r = {}
        self.streams[eng].append(o)
        return o

    def dma(self, out, in_, reads=(), writes=(), sembuf=None, eng="sp", **kw):
        if sembuf.sem is None:
            sembuf.sem = self.stack.enter_context(self.nc.semaphore("d%d" % self.nsem))
            self.nsem += 1
        o = Op(eng, lambda e: e.dma_start(out=out, in_=in_, **kw))
        o.deps = self._deps(eng, reads, writes)
        sembuf.cnt += 16
        o.dma = (sembuf.sem, 16)
        d = ("dma", sembuf.sem, sembuf.cnt)
        for b in reads:
            b.r["dma%d" % id(sembuf)] = d
        for b in writes:
            b.w = d
            b.r = {}
        self.streams[eng].append(o)
        return d

    def barrier(self, bufs):
        lasts = []
        for e in ("pe", "act", "dve", "pool"):
            for o in reversed(self.streams[e]):
                if o.fn is None:
                    break
                if o.dma is None:
                    o.signal = True
                    lasts.append(("op", o))
                    break
        for e in ("pe", "act", "dve", "pool", "sp"):
            o = Op(e, None)
            o.deps = list(lasts)
            self.streams[e].append(o)

    def emit(self, final_deps):
        nc = self.nc
        for e in self.ENGS:
            c = 0
            for o in self.streams[e]:
                if o.signal:
                    c += 1
                o.count = c
        handles = {}

        def run(ename, eng, tail=None):
            seen = {}
            for o in self.streams[ename]:
                for d in o.deps:
                    if d[0] == "op":
                        key, val, sem = d[1].eng, d[1].count, self.sems[d[1].eng]
                    else:
                        key, val, sem = id(d[1]), d[2], d[1]
                    if seen.get(key, 0) < val:
                        eng.wait_ge(sem, val)
                        seen[key] = val
                if o.fn is None:
                    continue
                ins = o.fn(eng)
                if o.dma is not None:
                    ins.then_inc(o.dma[0], o.dma[1])
                elif o.signal:
                    ins.then_inc(self.sems[ename], 1)
            if tail:
                for d in tail:
                    eng.wait_ge(d[1], d[2])

        with nc.Block() as block:
            @block.tensor
            def _(e):
                run("pe", e)

            @block.scalar
            def _(e):
                run("act", e)

            @block.vector
            def _(e):
                run("dve", e)

            @block.gpsimd
            def _(e):
                run("pool", e)

            @block.sync
            def _(e):
                run("sp", e, tail=final_deps)


def build_program(dbg=None, stop_after=None):
    dbg = dbg or {}
    nc = bass.Bass("TRN2", target_bir_lowering=False)
    stack = ExitStack()
    P = Prog(nc, stack)

    def dram(name, shape, dt, kind="ExternalInput"):
        return nc.dram_tensor(name, list(shape), dt, kind=kind).ap()

    def sbp(name, shape, dt):
        return stack.enter_context(nc.sbuf_tensor("sb_" + name, list(shape), dt))

    arena_box = {}
    cur = [0]

    def seek(off):
        cur[0] = off

    def sb(name, shape, dt, st=None):
        esz = 2 if dt == BF16 else 4
        n = 1
        for d_ in shape[1:]:
            n *= d_
        nbytes = (n * esz + 31) // 32 * 32
        off = cur[0]
        assert off + nbytes <= arena_box["size"], (name, off, nbytes, arena_box["size"])
        cur[0] = off + nbytes
        ap = arena_box["t"][0:shape[0], off // 2:off // 2 + n * esz // 2]
        if esz == 4:
            ap = ap.bitcast(dt)
        if len(shape) == 3:
            ap = ap.rearrange("p (a b) -> p a b", a=shape[1])
        elif len(shape) == 4:
            ap = ap.rearrange("p (a b c) -> p a b c", a=shape[1], b=shape[2])
        return ap

    x_ext = dram("x_ext", [EXT, D], F32)
    pos_d = dram("pos", [1, EXT], I32)
    kvtab_d = dram("kvtab", [128, NVT], F32)
    gvalid_d = dram("gvalid", [1, 2], F32)
    cbf_d = dram("cbf", [128, 1024], BF16)
    pc_d = dram("pcols", [128, 256], F32)
    mem_d = dram("mem", [MEM, D], F32)
    w_in = dram("w_in", [D, 5376], F32)
    w_out = dram("w_out", [D, D], F32)
    w_xq = dram("w_xq", [D, D], F32)
    w_xk = dram("w_xk", [D, D], F32)
    w_xv = dram("w_xv", [D, D], F32)
    w_xo = dram("w_xo", [D, D], F32)
    w_gate = dram("w_gate", [D, DFF], F32)
    w_up = dram("w_up", [D, DFF], F32)
    w_down = dram("w_down", [DFF, D], F32)
    rows_d = dram("rows", [8, D], F32)
    out_d = dram("out", [TOWN, D], F32, kind="ExternalOutput")
    dbg_out = {}
    for k, (shape, dt) in dbg.items():
        dbg_out[k] = dram("dbg_" + k, shape, dt, kind="ExternalOutput")

    PC = {}
    _o = 0
    for nm, n in PCOLS:
        PC[nm] = _o
        _o += n
    assert _o <= 256

    cbf = sbp("cbf", [128, 1024], BF16)
    pc = sbp("pc", [128, 256], F32)
    kvt = sbp("kvt", [128, NVT], F32)
    stats = sbp("stats", [128, NXT, 2], F32)
    ones_bf = sbp("ones_bf", [128, 128], BF16)
    esk = sbp("esk", [128, 8], F32)
    gval = sbp("gval", [128, 2], F32)
    B_const = Buf("const")
    ident = cbf[:, 0:128]
    perm = cbf[:, 128:256]
    bnd = {64: cbf[:, 256:640], 128: cbf[:, 640:1024]}
    psS = [stack.enter_context(nc.psum_tensor("psS%d" % i, [128, 1024], F32)) for i in range(2)]
    psB0 = stack.enter_context(nc.psum_tensor("psB0", [128, 512], F32))
    psZZ = stack.enter_context(nc.psum_tensor("psZZ", [128, 1024], F32))
    psB3 = stack.enter_context(nc.psum_tensor("psB3", [128, 512], F32))
    psB = [psB0, psZZ[:, 0:512], psZZ[:, 512:1024], psB3]
    B_psS = [Buf("psS%d" % i, excl=True) for i in range(2)]
    B_psB = [Buf("psB%d" % i, excl=True) for i in range(4)]

    P.dma(cbf[:], cbf_d, writes=[B_const], sembuf=B_const)
    mpk = {64: bnd[64][:, 64:320].rearrange("p (t q) -> p t q", t=2), 128: bnd[128][:, 0:384].rearrange("p (t q) -> p t q", t=3)}
    P.dma(pc[:], pc_d, writes=[B_const], sembuf=B_const)
    P.dma(kvt[:], kvtab_d, writes=[B_const], sembuf=B_const)
    P.dma(gval[:], gvalid_d.partition_broadcast(128), writes=[B_const], sembuf=B_const)
    P.op("pool", lambda e: e.memset(ones_bf[:], 1.0), writes=[B_const])
    P.op("act", lambda e: e.activation(out=esk[:], in_=pc[:, PC["sink0"]:PC["sink0"] + 8], func=AF.Exp),
         reads=[B_const], writes=[B_const])

    def pcol(nm, k=0, p0=0, p1=128):
        return pc[p0:p1, PC[nm] + k: PC[nm] + k + 1]

    final = []
    B_dbg = Buf("dbg")

    def tap(name, ap, reads):
        if name in dbg_out:
            final.append(P.dma(dbg_out[name], ap, reads=reads, sembuf=B_dbg))

    lnw = [sbp("lnw%d" % i, [128, 16], F32) for i in range(2)]
    B_lnw = [Buf("lnw%d" % i) for i in range(2)]
    ln_ctr = [0]

    def ln_stats(x_ap, n, mean_out, rstd_out, rd, wr):
        i = ln_ctr[0] % 2
        ln_ctr[0] += 1
        w = lnw[i]
        bw = [B_lnw[i]]
        P.op("dve", lambda e: e.bn_stats(w[0:n, 0:6], x_ap[0:n, 0:512]), reads=rd, writes=bw)
        P.op("dve", lambda e: e.bn_stats(w[0:n, 6:12], x_ap[0:n, 512:1024]), reads=rd, writes=bw)
        P.op("dve", lambda e: e.bn_aggr(w[0:n, 12:14], w[0:n, 0:12]), reads=bw, writes=bw)
        P.op("dve", lambda e: e.tensor_copy(out=mean_out, in_=w[0:n, 12:13]), reads=bw, writes=wr)
        P.op("act", lambda e: e.activation(out=w[0:n, 14:15], in_=w[0:n, 13:14], func=AF.Ln, bias=pcol("eps", 0, 0, n)),
             reads=bw + [B_const], writes=bw)
        P.op("act", lambda e: e.activation(out=rstd_out, in_=w[0:n, 14:15], func=AF.Exp, scale=-0.5),
             reads=bw, writes=wr)

    mr = [sbp("mr%d" % i, [128, 4], F32) for i in range(2)]
    B_mr = [Buf("mr%d" % i) for i in range(2)]
    arena_box["size"] = (nc.sbuf_bytes_remaining - 256) // 64 * 64
    arena_box["t"] = stack.enter_context(nc.sbuf_tensor("arena", [128, arena_box["size"] // 2], BF16))
    stM = ExitStack()
    seek(0)
    hT = sb("hT", [128, 8, EXT], BF16, stM)
    B_hT = [Buf("hT%d" % t) for t in range(NXT)]
    ctab = sb("ctab", [128, EXT], F32, stM)
    stab = sb("stab", [128, EXT], F32, stM)
    B_tab = Buf("tab")
    B_stats = [Buf("stats%d" % t) for t in range(NXT)]
    mixedT = sb("mixedT", [128, 8, NMID], BF16)
    B_mixed = [Buf("mixed%d" % k) for k in range(8)]
    MIX_END = cur[0]

    ph0 = ExitStack()
    xs = [sb("xs%d" % i, [128, D], F32, ph0) for i in range(3)]
    B_xs = [Buf("xs%d" % i) for i in range(3)]
    xh = [sb("xh%d" % i, [128, D], BF16, ph0) for i in range(2)]
    B_xh = [Buf("xh%d" % i) for i in range(2)]
    posi = sb("posi", [128, EXT], I32, ph0)
    ang = sb("ang", [128, EXT], F32, ph0)
    ktf = sb("ktf", [128, EXT], F32, ph0)
    kti = posi
    B_pos = Buf("pos")
    B_ang = Buf("ang")
    B_kt = Buf("kt")

    deferred = []

    def dop(*a_, **k_):
        deferred.append((a_, k_))

    dop("dve", lambda e: e.tensor_copy(out=ang[:], in_=posi[:]), reads=[B_pos, B_const], writes=[B_ang])
    dop("dve", lambda e: e.tensor_scalar(out=ang[:], in0=ang[:], scalar1=pcol("invf"), scalar2=None,
                                          op0=ALU.mult), reads=[B_ang, B_const], writes=[B_ang])
    TWO_PI = 2.0 * math.pi
    C1 = 6.28125
    C2 = TWO_PI - C1

    def range_reduce(dst, bdst, shift):
        dop("dve", lambda e: e.tensor_scalar(out=ktf[:], in0=ang[:], scalar1=shift, scalar2=1.0 / TWO_PI,
                                              op0=ALU.add, op1=ALU.mult), reads=[B_ang], writes=[B_kt])
        dop("dve", lambda e: e.tensor_copy(out=kti[:], in_=ktf[:]), reads=[B_kt], writes=[B_kt, B_pos])
        dop("dve", lambda e: e.tensor_copy(out=ktf[:], in_=kti[:]), reads=[B_kt], writes=[B_kt])
        dop("dve", lambda e: e.tensor_scalar(out=dst[:], in0=ang[:], scalar1=shift, scalar2=None, op0=ALU.add),
             reads=[B_ang], writes=[bdst])
        for cc in (C1, C2):
            dop("dve", lambda e, cc=cc: e.scalar_tensor_tensor(out=dst[:], in0=ktf[:], scalar=-cc, in1=dst[:],
                                                                op0=ALU.mult, op1=ALU.add),
                 reads=[B_kt, bdst], writes=[bdst])
        dop("dve", lambda e: e.tensor_scalar(out=ktf[:], in0=dst[:], scalar1=math.pi, scalar2=-TWO_PI,
                                              op0=ALU.is_gt, op1=ALU.mult), reads=[bdst], writes=[B_kt])
        dop("dve", lambda e: e.tensor_tensor(out=dst[:], in0=dst[:], in1=ktf[:], op=ALU.add),
             reads=[B_kt, bdst], writes=[bdst])
        dop("dve", lambda e: e.tensor_scalar(out=dst[:], in0=dst[:], scalar1=3.1415925, scalar2=-3.1415925,
                                              op0=ALU.min, op1=ALU.max), reads=[bdst], writes=[bdst])

    range_reduce(stab, B_tab, 0.0)
    range_reduce(ctab, B_tab, 0.5 * math.pi)
    dop("act", lambda e: e.activation(out=stab[:], in_=stab[:], func=AF.Sin), reads=[B_tab], writes=[B_tab])
    dop("act", lambda e: e.activation(out=ctab[:], in_=ctab[:], func=AF.Sin), reads=[B_tab], writes=[B_tab])
    dop("dve", lambda e: e.tensor_scalar(out=stab[:], in0=stab[:], scalar1=pcol("sgn"), scalar2=None,
                                          op0=ALU.mult), reads=[B_tab, B_const], writes=[B_tab])
    dop("dve", lambda e: e.tensor_scalar(out=ctab[:], in0=ctab[:], scalar1=-1.0, scalar2=pcol("rotm"),
                                          op0=ALU.add, op1=ALU.mult), reads=[B_tab, B_const], writes=[B_tab])
    dop("dve", lambda e: e.tensor_scalar(out=ctab[:], in0=ctab[:], scalar1=1.0, scalar2=None,
                                          op0=ALU.add), reads=[B_tab], writes=[B_tab])

    def bc8(nm):
        return pc[:, PC[nm]:PC[nm] + 8].unsqueeze(2).to_broadcast([128, 8, 128])

    def p0_pre(t):
        j = t % 3
        P.dma(xs[j][:], x_ext[t * 128:(t + 1) * 128, :], writes=[B_xs[j]], sembuf=B_xs[j])

    def p0_front(t):
        j = t % 3
        ln_stats(xs[j], 128, stats[:, t, 0:1], stats[:, t, 1:2], [B_xs[j]], [B_stats[t]])

    def p0_back(t):
        i = t % 2
        j = t % 3
        P.op("dve", lambda e: e.tensor_scalar(out=xh[i][:], in0=xs[j][:], scalar1=stats[:, t, 0:1],
                                              scalar2=stats[:, t, 1:2], op0=ALU.subtract, op1=ALU.mult),
             reads=[B_xs[j], B_stats[t]], writes=[B_xh[i]])
        pb = t % 2
        psb = psS[pb][:, 0:512].bitcast(BF16)
        for k in range(8):
            P.op("pe", lambda e, k=k: e.transpose(psb[:, k * 128:(k + 1) * 128], xh[i][:, k * 128:(k + 1) * 128], ident),
                 reads=[B_xh[i], B_const], writes=[B_psS[pb]])
        for k in range(8):
            P.op("act", lambda e, k=k: e.activation(
                out=hT[:, k, t * 128:(t + 1) * 128], in_=psb[:, k * 128:(k + 1) * 128], func=AF.Identity,
                scale=pcol("lnin_g", k), bias=pcol("lnin_b", k)), reads=[B_psS[pb], B_const], writes=[B_hT[t]])

    p0_pre(0)
    p0_pre(1)
    P.dma(posi[:], pos_d.partition_broadcast(128), writes=[B_pos], sembuf=B_pos)
    for t in range(NXT + 1):
        if t < NXT:
            p0_front(t)
        if t >= 1:
            p0_back(t - 1)
        if t + 2 < NXT:
            p0_pre(t + 2)
        if deferred:
            a_, k_ = deferred.pop(0)
            P.op(*a_, **k_)
    while deferred:
        a_, k_ = deferred.pop(0)
        P.op(*a_, **k_)

    tap("hT", hT[:], B_hT)
    tap("ctab", ctab[:], [B_tab])
    tap("stab", stab[:], [B_tab])
    if stop_after == 0:
        P.emit(final)
        ph0.close(); stM.close(); stack.close()
        return nc
    P.barrier(None)
    ph0.close()

    phM = ExitStack()
    seek(MIX_END)
    stw = [sb("stw%d" % i, [128, 2, 128], F32, phM) for i in range(2)]
    stw.append(stw[0])
    wbf = [sb("wbf%d" % i, [128, 8, 128], BF16, phM) for i in range(3)]
    B_stw = [Buf("stw%d" % i) for i in range(2)]
    B_stw.append(B_stw[0])
    wtmp = [stw[0][:, 0:2, :].bitcast(BF16)[:, :, :].rearrange("p a b -> p (a b)")[:, 0:512].rearrange("p (k c) -> p k c", k=8),
            stw[1][:, 0:2, :].bitcast(BF16)[:, :, :].rearrange("p a b -> p (a b)")[:, 0:512].rearrange("p (k c) -> p k c", k=8)]
    B_wtmp = [Buf("wtmp0"), Buf("wtmp1")]
    B_wbf = [Buf("wbf%d" % i) for i in range(3)]
    QT = sb("QT", [128, NMID], BF16, phM)
    KT = sb("KT", [128, EXT], BF16, phM)
    NVMAX = max(len(PLANS[g][1]) for g in GROUPS)
    VT = sb("VT", [128, NVMAX, 192], BF16, phM)
    acc = sb("acc", [128, 2, NMID], F32, phM)
    Pt = [sb("Pt%d" % i, [128, 2, 3, 128], BF16, phM) for i in range(3)]
    zb = [sb("zb%d" % i, [128, 512], BF16, phM) for i in range(2)]
    t1 = [sb("t1%d" % i, [128, 512], F32, phM) for i in range(2)]
    t2 = [sb("t2_0", [128, 512], F32, phM)]
    t2.append(t2[0])
    rtmp = sb("rtmp", [128, 2, 256], F32, phM)
    B_QT, B_KT, B_acc, B_rtmp = Buf("QT"), Buf("KT"), Buf("acc"), Buf("rtmp")
    B_rtmp2 = [Buf("rtmpA"), Buf("rtmpB")]
    B_VT = [Buf("VT%d" % i) for i in range(NVMAX)]
    B_Pt = [Buf("Pt%d" % i) for i in range(3)]
    SSETS = [(psS[0], [B_psS[0]]), (psS[1], [B_psS[1]]), (psZZ, [B_psB[1], B_psB[2]])]
    B_zb = [Buf("zb%d" % i) for i in range(2)]
    B_t1 = [Buf("t1%d" % i) for i in range(2)]
    B_t2 = [Buf("t2_0")]
    B_t2.append(B_t2[0])
    psUD, psSW = psB[0], psB[3]
    B_psUD, B_psSW = B_psB[0], B_psB[3]
    psZ = [psB[1], psB[2]]
    B_psZ = [B_psB[1], B_psB[2]]
    ctr = {"z": 0, "r": 0, "s": 0}

    def w_dma(role, col0, ncols, dup):
        src = w_in[:, col0:col0 + ncols].rearrange("(k p) c -> p k c", p=128)
        if dup:
            P.dma(wtmp[role - 1][:], src, writes=[B_wtmp[role - 1]], sembuf=B_wtmp[role - 1], eng="pool")
            for hh in range(2):
                P.op("act", lambda e, hh=hh: e.copy(out=wbf[role][:, :, 64 * hh:64 * hh + 64], in_=wtmp[role - 1][:]),
                     reads=[B_wtmp[role - 1]], writes=[B_wbf[role]])
        else:
            P.dma(wbf[role][:], src, writes=[B_wbf[role]], sembuf=B_wbf[role], eng="pool")

    def w_cast(role):
        pass

    def ht_bufs(e0, n):
        return B_hT[e0 // 128:(e0 + n - 1) // 128 + 1]

    def rope_block(role, esl, n, dst_ap, dst_buf, hbufs):
        zi = ctr["z"] % 2
        ctr["z"] += 1
        pz, bpz = psZ[zi], B_psZ[zi]
        for k in range(8):
            P.op("pe", lambda e, k=k: e.matmul(pz[:, 0:n], lhsT=wbf[role][:, k, :], rhs=hT[:, k, esl],
                                               start=(k == 0), stop=(k == 7)),
                 reads=[B_wbf[role]] + hbufs, writes=[bpz])
        ri = ctr["r"] % 2
        ctr["r"] += 1
        P.op("act", lambda e: e.copy(out=zb[ri][:, 0:n], in_=pz[:, 0:n]), reads=[bpz], writes=[B_zb[ri]])

        def post():
            P.op("pe", lambda e: e.matmul(psSW[:, 0:n], lhsT=perm, rhs=zb[ri][:, 0:n], start=True, stop=True),
                 reads=[B_zb[ri], B_const], writes=[B_psSW])
            P.op("pool", lambda e: e.tensor_tensor(out=t1[ri][:, 0:n], in0=zb[ri][:, 0:n], in1=ctab[:, esl], op=ALU.mult),
                 reads=[B_zb[ri], B_tab], writes=[B_t1[ri]])
            P.op("dve", lambda e: e.tensor_tensor(out=t2[ri][:, 0:n], in0=psSW[:, 0:n], in1=stab[:, esl], op=ALU.mult),
                 reads=[B_psSW, B_tab], writes=[B_t2[ri]])
            P.op("dve", lambda e: e.tensor_tensor(out=dst_ap, in0=t1[ri][:, 0:n], in1=t2[ri][:, 0:n], op=ALU.add),
                 reads=[B_t1[ri], B_t2[ri]], writes=[dst_buf])
        prev = pend[0]
        pend[0] = post
        if prev is not None:
            prev()

    pend = [None]

    def flush_rope():
        if pend[0] is not None:
            pend[0]()
            pend[0] = None

    def do_unit(gname, qcol, kcol, vcol, dup, out_chunk, esk_cols, mode, hook_v=None, hook_att=None,
                prev_norm=None, reuse_kv=False):
        r, W = GROUPS[gname]
        plan, vts, emin, emax = PLANS[gname]
        stage = dbg.get("_stage", "all")
        for b4 in range(1 if stage == "q1" else 4):
            e0 = OWN0 + 512 * b4
            rope_block(0, slice(e0, e0 + 512), 512, QT[:, 512 * b4:512 * (b4 + 1)], B_QT, ht_bufs(e0, 512))
        if stage in ("q1", "q4"):
            flush_rope()
            return
        rope_block(0, slice(EXTRA_E[0], EXTRA_E[1] + 1, EXTRA_E[1] - EXTRA_E[0]), 2, QT[:, TOWN:TOWN + 2], B_QT,
                   [B_hT[EXTRA_E[0] // 128], B_hT[EXTRA_E[1] // 128]])
        if stage == "q":
            flush_rope()
            return
        if not reuse_kv:
            e0 = emin
            while e0 < emax:
                n = min(512, emax - e0)
                rope_block(1, slice(e0, e0 + n), n, KT[:, e0:e0 + n], B_KT, ht_bufs(e0, n))
                e0 += n
        flush_rope()
        pparts = list(prev_norm) if prev_norm else []
        if pparts:
            pparts.pop(0)()
        if hook_v is not None:
            hook_v()
        if stage == "k":
            return
        if not reuse_kv:
            nvt = len(vts)
            vb0 = VT_BASE[gname]
            P.op("dve", lambda e: e.tensor_copy(out=VT[:, 0:nvt, 64:128],
                                                in_=kvt[:, vb0:vb0 + nvt].unsqueeze(2).to_broadcast([128, nvt, 64])),
                 reads=[B_const], writes=B_VT[0:nvt])
            for (rr, c, ks, nk), vi in vts.items():
                zi = ctr["z"] % 2
                ctr["z"] += 1
                pz, bpz = psZ[zi], B_psZ[zi]
                tsl = slice(c + rr * ks, c + rr * (ks + nk - 1) + 1, rr)
                for k in range(8):
                    P.op("pe", lambda e, k=k, tsl=tsl, nk=nk, pz=pz: e.matmul(
                        pz[0:nk, 0:128], lhsT=hT[:, k, tsl], rhs=wbf[2][:, k, :], start=(k == 0), stop=(k == 7)),
                        reads=[B_wbf[2]] + B_hT, writes=[bpz])
                kc = VT_BASE[gname] + vi
                P.op("act", lambda e, nk=nk, vi=vi, pz=pz, kc=kc: e.activation(
                    out=VT[0:nk, vi, :].rearrange("p (a b) -> p a b", a=3)[:, 0:3:2, :],
                    in_=pz[0:nk, 0:128].rearrange("p (a b) -> p a b", a=2), func=AF.Copy, scale=kvt[0:nk, kc:kc + 1]),
                    reads=[bpz, B_const], writes=[B_VT[vi]])
                if pparts:
                    pparts.pop(0)()
        while pparts:
            pparts.pop(0)()
        if stage == "v":
            return
        def front(bi):
            blk, kts = plan[bi]
            c, ja, nq = blk
            qc0 = e_mid_col(c + r * ja)
            qsl = slice(qc0, qc0 + r * (nq - 1) + 1, r)
            si = bi % 3
            pS, bpS_l = SSETS[si]
            pt, bpt = Pt[si], B_Pt[si]
            T = len(kts)
            for ti, (rr, cc, ks, nk, dl) in enumerate(kts):
                ksl = slice(cc + rr * ks, cc + rr * (ks + nk - 1) + 1, rr)
                for h in range(2):
                    o0 = h * 512 + (T - 1 - ti) * 128
                    P.op("pe", lambda e: e.matmul(
                        pS[0:nk, o0:o0 + nq], lhsT=KT[64 * h:64 * h + 64, ksl], rhs=QT[64 * h:64 * h + 64, qsl],
                        start=True, stop=True), reads=[B_KT, B_QT], writes=bpS_l)
            pv4 = pS[:, :].rearrange("p (h t q) -> p h t q", h=2, t=4)
            if all(k[3] == 128 for k in kts):
                P.op("act", lambda e: e.activation(out=pt[:, :, 0:T, 0:nq], in_=pv4[:, :, 0:T, 0:nq],
                                                   func=AF.Exp, scale=0.125), reads=bpS_l, writes=[bpt])
            else:
                for ti, (rr, cc, ks, nk, dl) in enumerate(kts):
                    P.op("act", lambda e: e.activation(out=pt[0:nk, :, T - 1 - ti, 0:nq], in_=pv4[0:nk, :, T - 1 - ti, 0:nq],
                                                       func=AF.Exp, scale=0.125), reads=bpS_l, writes=[bpt])
            if nq == 128 and all(k[3] == 128 for k in kts):
                mp = mpk[W]
                P.op("dve", lambda e: e.tensor_tensor(out=pt[:, :, 0:T, :], in0=pt[:, :, 0:T, :],
                                                      in1=mp.unsqueeze(1).to_broadcast([128, 2, T, 128]), op=ALU.mult),
                     reads=[bpt, B_const], writes=[bpt])
            else:
                for ti, (rr, cc, ks, nk, dl) in enumerate(kts):
                    if dl - (nk - 1) >= -W and dl + nq - 1 <= W:
                        continue
                    msl = bnd[W][0:nk, dl + 128:dl + 128 + nq].unsqueeze(1).to_broadcast([nk, 2, nq])
                    P.op("dve", lambda e: e.tensor_tensor(out=pt[0:nk, :, T - 1 - ti, 0:nq], in0=pt[0:nk, :, T - 1 - ti, 0:nq],
                                                          in1=msl, op=ALU.mult), reads=[bpt, B_const], writes=[bpt])

        def back(bi):
            blk, kts = plan[bi]
            c, ja, nq = blk
            qc0 = e_mid_col(c + r * ja)
            qsl = slice(qc0, qc0 + r * (nq - 1) + 1, r)
            si = bi % 3
            pt, bpt = Pt[si], B_Pt[si]
            T = len(kts)
            for h in range(2):
                for ti, (rr, cc, ks, nk, dl) in enumerate(kts):
                    vi = vts[(rr, cc, ks, nk)]
                    P.op("pe", lambda e: e.matmul(
                        psUD[:, h * 128:h * 128 + nq], lhsT=VT[0:nk, vi, 64 * h:64 * h + 128],
                        rhs=pt[0:nk, h, T - 1 - ti, 0:nq], start=(ti == 0), stop=(ti == T - 1)),
                        reads=[bpt, B_VT[vi]], writes=[B_psUD])
            udv = psUD[:, 0:256].rearrange("p (h q) -> p h q", h=2)[:, :, 0:nq]
            if mode in ("A", "first"):
                P.op("dve", lambda e: e.tensor_copy(out=acc[:, :, qsl], in_=udv), reads=[B_psUD], writes=[B_acc])
            else:
                P.op("dve", lambda e: e.tensor_tensor(out=acc[:, :, qsl], in0=udv, in1=acc[:, :, qsl], op=ALU.add),
                     reads=[B_psUD, B_acc], writes=[B_acc])

        if hook_att is not None:
            hook_att()
        nb = len(plan)
        for bi in range(nb + 2):
            if bi < nb:
                front(bi)
            if bi >= 2:
                back(bi - 2)
        def norm_parts():
            parts = []
            if mode not in ("last", "A"):
                return parts
            if mode == "A":
                b0 = esk[64:128, esk_cols[0]:esk_cols[0] + 1]
                b1 = esk[0:64, esk_cols[1]:esk_cols[1] + 1]
            else:
                b0 = pcol("zero", 0, 64, 128)
                b1 = pcol("zero", 0, 0, 64)

            def lnpart():
                P.op("act", lambda e: e.activation(out=acc[64:128, 0, :], in_=acc[64:128, 0, :], func=AF.Ln, bias=b0),
                     reads=[B_acc, B_const], writes=[B_acc])
                P.op("act", lambda e: e.activation(out=acc[0:64, 1, :], in_=acc[0:64, 1, :], func=AF.Ln, bias=b1),
                     reads=[B_acc, B_const], writes=[B_acc])
            parts.append(lnpart)
            c0 = 0
            bix = 0
            while c0 < NMID:
                n = min(128, NMID - c0)

                def blk(c0=c0, n=n, o=128 * (bix % 2), brt=B_rtmp2[bix % 2]):
                    csl = slice(c0, c0 + n)
                    P.op("act", lambda e: e.activation(out=rtmp[0:64, 0, o:o + n], in_=acc[64:128, 0, csl], func=AF.Exp, scale=-1.0),
                         reads=[B_acc], writes=[brt])
                    P.op("act", lambda e: e.activation(out=rtmp[64:128, 1, o:o + n], in_=acc[0:64, 1, csl], func=AF.Exp, scale=-1.0),
                         reads=[B_acc], writes=[brt])
                    P.op("dve", lambda e: e.tensor_tensor(out=mixedT[0:64, out_chunk, csl], in0=acc[0:64, 0, csl],
                                                          in1=rtmp[0:64, 0, o:o + n], op=ALU.mult),
                         reads=[B_acc, brt], writes=[B_mixed[out_chunk]])
                    P.op("dve", lambda e: e.tensor_tensor(out=mixedT[64:128, out_chunk, csl], in0=acc[64:128, 1, csl],
                                                          in1=rtmp[64:128, 1, o:o + n], op=ALU.mult),
                         reads=[B_acc, brt], writes=[B_mixed[out_chunk]])
                parts.append(blk)
                c0 += n
                bix += 1
            return parts

        return norm_parts()

    ulist = []
    for ch in range(4):
        g = ch // 2
        ulist.append(("A", QA0 + 128 * ch, KA0 + 64 * g, VA0 + 64 * g, True, ch, (2 * ch, 2 * ch + 1), "A"))
    for ch in range(4):
        for gi, gname in enumerate(("g1", "g2", "g3")):
            ulist.append((gname, QB0 + 512 * gi + 128 * ch, KB0 + 512 * gi + 128 * ch, VB0 + 512 * gi + 128 * ch, False,
                          4 + ch, None, ("first", "mid", "last")[gi]))
    NU = len(ulist)

    def reuse(ui):
        return ulist[ui][0] == "A" and ui % 2 == 1

    def dmaQ(ui):
        if ui < NU:
            w_dma(0, ulist[ui][1], 128, False)

    def dmaK(ui):
        if ui < NU and not reuse(ui):
            w_dma(1, ulist[ui][2], 64 if ulist[ui][4] else 128, ulist[ui][4])

    def dmaV(ui):
        if ui < NU and not reuse(ui):
            w_dma(2, ulist[ui][3], 64 if ulist[ui][4] else 128, ulist[ui][4])

    def castQ(ui):
        if ui < NU:
            w_cast(0)

    def castK(ui):
        if ui < NU and not reuse(ui):
            w_cast(1)

    def castV(ui):
        if ui < NU and not reuse(ui):
            w_cast(2)

    dmaQ(0); dmaK(0); dmaV(0)
    pnorm = None
    for ui, uu in enumerate(ulist):
        def hook_att(ui=ui):
            dmaQ(ui + 1)
            dmaK(ui + 1)
            dmaV(ui + 1)
        pnorm = do_unit(*uu, hook_v=None, hook_att=hook_att, prev_norm=pnorm, reuse_kv=reuse(ui))
    for pp in (pnorm or []):
        pp()
    tap("mixedT", mixedT[:], B_mixed)
    tap("QT", QT[:], [B_QT])
    tap("KT", KT[:], [B_KT])
    tap("wbf", wbf[1][:], [B_wbf[1]])
    if stop_after == 1:
        P.emit(final)
        phM.close(); stM.close(); stack.close()
        return nc
    P.barrier(None)
    phM.close()
    stM.close()

    TT = [(128 * i, 128) for i in range(16)] + [(TOWN, 2)]
    NTT = len(TT)
    seek(0)
    wst = [sb("wst%d" % i, [128, 8, 256], F32) for i in range(2)]
    B_wst = [Buf("wst%d" % i) for i in range(2)]
    wctr = [0]

    def load_big(dst, bdst, wd, col0, ncols, kchunks=8, rscale=None, row0=0):
        c = 0
        while c < ncols:
            n = min(256, ncols - c)
            i = wctr[0] % 2
            wctr[0] += 1
            src = wd[row0:row0 + 128 * kchunks, col0 + c:col0 + c + n].rearrange("(k p) c -> p k c", p=128)
            P.dma(wst[i][:, 0:kchunks, 0:n], src, writes=[B_wst[i]], sembuf=B_wst[i])
            if rscale is None:
                P.op("act", lambda e, i=i, n=n, c=c: e.copy(out=dst[:, 0:kchunks, c:c + n],
                                                            in_=wst[i][:, 0:kchunks, 0:n]),
                     reads=[B_wst[i]], writes=[bdst])
            else:
                for k in range(kchunks):
                    P.op("act", lambda e, i=i, n=n, c=c, k=k: e.activation(
                        out=dst[:, k, c:c + n], in_=wst[i][:, k, 0:n], func=AF.Copy,
                        scale=pcol(rscale[k][0], rscale[k][1])), reads=[B_wst[i], B_const], writes=[bdst])
            c += n

    def load_cast(dst, bdst, wd, col0, ncols, kchunks=8, row0=0):
        src = wd[row0:row0 + 128 * kchunks, col0:col0 + ncols].rearrange("(k p) c -> p k c", p=128)
        P.dma(dst, src, writes=[bdst], sembuf=bdst, eng="pool")

    bcg = sb("bcg", [128, D], F32)
    bcb = sb("bcb", [128, D], F32)
    B_bc = Buf("bc")

    def load_bc(gi):
        P.dma(bcg[:], rows_d[gi:gi + 1, :].partition_broadcast(128), writes=[B_bc], sembuf=B_bc)
        P.dma(bcb[:], rows_d[gi + 1:gi + 2, :].partition_broadcast(128), writes=[B_bc], sembuf=B_bc)
        P.op("dve", lambda e: e.tensor_scalar(out=bcg[:], in0=bcg[:], scalar1=ALPHA, scalar2=None, op0=ALU.mult),
             reads=[B_bc], writes=[B_bc])
        P.op("dve", lambda e: e.tensor_scalar(out=bcb[:], in0=bcb[:], scalar1=ALPHA, scalar2=None, op0=ALU.mult),
             reads=[B_bc], writes=[B_bc])

    hres_d = dram("hres_scratch", [NTT * 128, D], F32, kind="Internal")
    B_hres = [Buf("hres%d" % t) for t in range(NTT)]
    hn = [sb("hn%d" % i, [128, D], F32) for i in range(2)]
    B_hn = [Buf("hn%d" % i) for i in range(2)]
    hxT = sb("hxT", [128, 8, NMID], BF16)
    B_hxT = [Buf("hxT%d" % t) for t in range(NTT)]
    uw = [sb("uw%d" % i, [128, D], F32) for i in range(2)]
    B_uw = [Buf("uw%d" % i) for i in range(2)]
    ub = [sb("ub%d" % i, [128, D], BF16) for i in range(2)]
    B_ub = [Buf("ub%d" % i) for i in range(2)]
    TAIL_END = [0]

    def ln_front(t, n, u, bu):
        i = t % 2
        ln_stats(u, n, mr[i][0:n, 0:1], mr[i][0:n, 1:2], [bu], [B_mr[i]])

    def ln_mid(t, n, u, bu):
        i = t % 2
        P.op("dve", lambda e: e.tensor_scalar(out=hn[i][0:n, :], in0=u[0:n, :], scalar1=mr[i][0:n, 0:1],
                                              scalar2=mr[i][0:n, 1:2], op0=ALU.subtract, op1=ALU.mult),
             reads=[bu, B_mr[i]], writes=[B_hn[i]])
        P.dma(hres_d[t * 128:t * 128 + n, :], hn[i][0:n, :], reads=[B_hn[i]], writes=[B_hres[t]], sembuf=B_hn[i],
              eng="pool")
        P.op("act", lambda e: e.copy(out=ub[i][0:n, :], in_=hn[i][0:n, :]), reads=[B_hn[i]],
             writes=[B_ub[i]])

    def ln_back(t, n, c0, gname, bname):
        i = t % 2
        pb = t % 2
        psb = psB[pb][:, 0:512].bitcast(BF16)
        for k in range(8):
            P.op("pe", lambda e, k=k: e.transpose(psb[:, k * 128:k * 128 + n], ub[i][0:n, k * 128:(k + 1) * 128],
                                                  ident[0:n, 0:n]),
                 reads=[B_ub[i], B_const], writes=[B_psB[pb]])
        for k in range(8):
            P.op("act", lambda e, k=k: e.activation(out=hxT[:, k, c0:c0 + n], in_=psb[:, k * 128:k * 128 + n],
                                                    func=AF.Identity, scale=pcol(gname, k), bias=pcol(bname, k)),
                 reads=[B_psB[pb], B_const], writes=[B_hxT[t]])

    def pipeline3(nt, front, mid, back, pre=None):
        if pre is not None:
            pre(0)
        for t in range(nt + 2):
            if pre is not None and t + 1 < nt:
                pre(t + 1)
            if t < nt:
                front(t)
            if 1 <= t <= nt:
                mid(t - 1)
            if t >= 2 and back is not None:
                back(t - 2)

    def flush_ln():
        pass

    phW = ExitStack()
    TAIL_END[0] = cur[0]
    woT = sb("woT", [128, 8, D], BF16, phW)
    B_wo = Buf("wo")
    xr = [sb("xr%d" % i, [128, D], F32, phW) for i in range(2)]
    B_xr = [Buf("xr%d" % i) for i in range(2)]
    rs = [sb("rs%d" % i, [128, 8], F32, phW) for i in range(2)]
    B_rs = [Buf("rs%d" % i) for i in range(2)]
    assert cur[0] <= MIX_END - 8 * NMID * 2, cur[0]
    seek(MIX_END)
    sq = sb("sq", [128, 8, NMID], BF16, phW)
    B_sq = Buf("sq")
    load_cast(woT[:], B_wo, w_out, 0, D)
    for k in range(8):
        nm = ("gwin", k) if k < 4 else ("gdil", k - 4)
        P.op("act", lambda e, k=k, nm=nm: e.activation(out=woT[:, k, :], in_=woT[:, k, :], func=AF.Copy,
                                                       scale=pcol(nm[0], nm[1])), reads=[B_wo, B_const], writes=[B_wo])
    load_bc(0)
    for k in range(8):
        P.op("act", lambda e, k=k: e.activation(out=sq[:, k, :], in_=mixedT[:, k, :], func=AF.Square),
             reads=[B_mixed[k]], writes=[B_sq])
    psSS = psB[3]
    B_psSS = B_psB[3]
    def w_pre(t):
        c0, n = TT[t]
        i = t % 2
        if n == 128:
            e0 = OWN0 + c0
            P.dma(xr[i][:], x_ext[e0:e0 + 128, :], writes=[B_xr[i]], sembuf=B_xr[i])
        else:
            P.dma(xr[i][0:1, :], x_ext[EXTRA_E[0]:EXTRA_E[0] + 1, :], writes=[B_xr[i]], sembuf=B_xr[i])
            P.dma(xr[i][1:2, :], x_ext[EXTRA_E[1]:EXTRA_E[1] + 1, :], writes=[], sembuf=B_xr[i])
            B_xr[i].w = ("dma", B_xr[i].sem, B_xr[i].cnt)

    def w_front(t):
        c0, n = TT[t]
        i = t % 2
        for half in range(2):
            for k in range(4):
                P.op("pe", lambda e, half=half, k=k: e.matmul(
                    psSS[0:n, 2 * i + half:2 * i + half + 1], lhsT=sq[:, 4 * half + k, c0:c0 + n], rhs=ones_bf[:, 0:1],
                    start=(k == 0), stop=(k == 3)), reads=[B_sq, B_const], writes=[B_psSS])
        P.op("act", lambda e: e.activation(out=rs[i][0:n, 0:2], in_=psSS[0:n, 2 * i:2 * i + 2], func=AF.Ln,
                                           bias=pcol("eps", 0, 0, n), scale=1.0 / 512.0),
             reads=[B_psSS, B_const], writes=[B_rs[i]])
        P.op("act", lambda e: e.activation(out=rs[i][0:n, 2:4], in_=rs[i][0:n, 0:2], func=AF.Exp, scale=-0.5),
             reads=[B_rs[i]], writes=[B_rs[i]])
        if n == 128:
            e0 = OWN0 + c0
            st_m, st_r = stats[:, e0 // 128, 0:1], stats[:, e0 // 128, 1:2]
            xin = xr[i]
        else:
            ln_stats(xr[i], n, mr[i][0:n, 2:3], mr[i][0:n, 3:4], [B_xr[i]], [B_mr[i]])
            st_m, st_r = mr[i][0:n, 2:3], mr[i][0:n, 3:4]
            xin = xr[i]
        u, bu = uw[i], B_uw[i]
        P.op("dve", lambda e, xin=xin, st_m=st_m: e.scalar_tensor_tensor(
            out=u[0:n, :], in0=xin[0:n, :], scalar=st_m[0:n, :], in1=bcg[0:n, :], op0=ALU.subtract, op1=ALU.mult),
            reads=[B_xr[i], B_bc, B_mr[i]] + B_stats, writes=[bu])
        P.op("dve", lambda e, st_r=st_r: e.scalar_tensor_tensor(
            out=u[0:n, :], in0=u[0:n, :], scalar=st_r[0:n, :], in1=bcb[0:n, :], op0=ALU.mult, op1=ALU.add),
            reads=[bu, B_bc, B_mr[i]] + B_stats, writes=[bu])
        for half in range(2):
            for hc in range(2):
                pb = psS[half][:, 512 * hc:512 * (hc + 1)]
                bpb = B_psS[half]
                for k in range(4):
                    P.op("pe", lambda e, half=half, hc=hc, k=k, pb=pb: e.matmul(
                        pb[0:n, :], lhsT=mixedT[:, 4 * half + k, c0:c0 + n], rhs=woT[:, 4 * half + k, 512 * hc:512 * (hc + 1)],
                        start=(k == 0), stop=(k == 3)), reads=[B_mixed[4 * half + k], B_wo], writes=[bpb])
        for half in range(2):
            for hc in range(2):
                pb = psS[half][:, 512 * hc:512 * (hc + 1)]
                bpb = B_psS[half]
                P.op("dve", lambda e, half=half, hc=hc, pb=pb: e.scalar_tensor_tensor(
                    out=u[0:n, 512 * hc:512 * (hc + 1)], in0=pb[0:n, :], scalar=rs[i][0:n, 2 + half:3 + half],
                    in1=u[0:n, 512 * hc:512 * (hc + 1)], op0=ALU.mult, op1=ALU.add),
                    reads=[bpb, B_rs[i], bu], writes=[bu])
        ln_front(t, n, u, bu)

    pipeline3(NTT, w_front, lambda t: ln_mid(t, TT[t][1], uw[t % 2], B_uw[t % 2]),
              lambda t: ln_back(t, TT[t][1], TT[t][0], "ln1_g", "ln1_b"), pre=w_pre)

    flush_ln()
    if stop_after == 2:
        P.emit(final)
        phW.close(); stack.close()
        return nc
    P.barrier(None)
    phW.close()

    phX = ExitStack()
    seek(TAIL_END[0])
    hr = [sb("hr%d" % i, [128, D], F32, phX) for i in range(2)]
    B_hr = [Buf("hr%d" % i) for i in range(2)]
    wA = sb("wA", [128, 8, D], BF16, phX)
    wB = sb("wB", [128, 8, D], BF16, phX)
    B_wA, B_wB = Buf("wA"), Buf("wB")
    memT = sb("memT", [128, 8, MEM], BF16, phX)
    kxT = sb("kxT", [128, 8, MEM], BF16, phX)
    vx = sb("vx", [128, 2, D], BF16, phX)
    qxT = sb("qxT", [128, 8, NMID], BF16, phX)
    oxT = sb("oxT", [128, 8, NMID], BF16, phX)
    Px = [sb("Px%d" % i, [128, 2, 512], BF16, phX) for i in range(2)]
    rx = [sb("rx%d" % i, [128, 512], F32, phX) for i in range(2)]
    B_memT, B_kxT, B_vx, B_qxT, B_oxT = Buf("memT"), Buf("kxT"), Buf("vx"), Buf("qxT"), Buf("oxT")
    B_Px = [Buf("Px%d" % i) for i in range(2)]
    B_rx = [Buf("rx%d" % i) for i in range(2)]
    load_cast(wA[:], B_wA, w_xk, 0, D)
    load_cast(wB[:], B_wB, w_xv, 0, D)
    for mt in range(2):
        i = mt % 2
        u, bu = uw[i], B_uw[i]
        P.dma(u[:], mem_d[mt * 128:(mt + 1) * 128, :], writes=[bu], sembuf=bu)
        ln_stats(u, 128, mr[i][:, 0:1], mr[i][:, 1:2], [bu], [B_mr[i]])
        P.op("dve", lambda e, u=u, i=i: e.tensor_scalar(out=ub[i][:], in0=u[:], scalar1=mr[i][:, 0:1],
                                                        scalar2=mr[i][:, 1:2], op0=ALU.subtract, op1=ALU.mult),
             reads=[bu, B_mr[i]], writes=[B_ub[i]])
        psb = psS[i][:, 0:512].bitcast(BF16)
        for k in range(8):
            P.op("pe", lambda e, k=k, i=i, psb=psb: e.transpose(psb[:, k * 128:(k + 1) * 128],
                                                                 ub[i][:, k * 128:(k + 1) * 128], ident),
                 reads=[B_ub[i], B_const], writes=[B_psS[i]])
        for k in range(8):
            P.op("act", lambda e, mt=mt, k=k, psb=psb: e.activation(
                out=memT[:, k, mt * 128:(mt + 1) * 128], in_=psb[:, k * 128:(k + 1) * 128], func=AF.Identity,
                scale=pcol("mem_g", k), bias=pcol("mem_b", k)), reads=[B_psS[i], B_const], writes=[B_memT])
    for c in range(8):
        pz, bpz = psB[c % 2], B_psB[c % 2]
        for k in range(8):
            P.op("pe", lambda e, c=c, k=k, pz=pz: e.matmul(pz[:, 0:MEM], lhsT=wA[:, k, c * 128:(c + 1) * 128],
                                                          rhs=memT[:, k, :], start=(k == 0), stop=(k == 7)),
                 reads=[B_wA, B_memT], writes=[bpz])
        P.op("act", lambda e, c=c, pz=pz: e.copy(out=kxT[:, c, :], in_=pz[:, 0:MEM]), reads=[bpz], writes=[B_kxT])
    for mt in range(2):
        for hc in range(2):
            pz, bpz = psB[2 + hc], B_psB[2 + hc]
            for k in range(8):
                P.op("pe", lambda e, mt=mt, hc=hc, k=k, pz=pz: e.matmul(
                    pz[:, :], lhsT=memT[:, k, mt * 128:(mt + 1) * 128], rhs=wB[:, k, 512 * hc:512 * (hc + 1)],
                    start=(k == 0), stop=(k == 7)), reads=[B_wB, B_memT], writes=[bpz])
            P.op("act", lambda e, mt=mt, hc=hc, pz=pz: e.copy(out=vx[:, mt, 512 * hc:512 * (hc + 1)], in_=pz[:, :]),
                 reads=[bpz], writes=[B_vx])
    load_cast(wA[:], B_wA, w_xq, 0, D)
    CB = [(512 * i, 512) for i in range(4)] + [(TOWN, 2)]
    zc = 0
    for c in range(8):
        for (c0, n) in CB:
            pz, bpz = psB[zc % 2], B_psB[zc % 2]
            zc += 1
            for k in range(8):
                P.op("pe", lambda e, c=c, k=k, pz=pz, c0=c0, n=n: e.matmul(
                    pz[:, 0:n], lhsT=wA[:, k, c * 128:(c + 1) * 128], rhs=hxT[:, k, c0:c0 + n],
                    start=(k == 0), stop=(k == 7)), reads=[B_wA] + B_hxT, writes=[bpz])
            P.op("act", lambda e, c=c, pz=pz, c0=c0, n=n: e.copy(out=qxT[:, c, c0:c0 + n], in_=pz[:, 0:n]),
                 reads=[bpz], writes=[B_qxT])
    load_cast(wB[:], B_wB, w_xo, 0, D)
    load_bc(2)
    xa_items = [(hd, c0, n) for hd in range(4) for (c0, n) in CB]

    def xa_front(it):
        hd, c0, n = xa_items[it]
        si = it % 2
        pS, bpS = psS[si], B_psS[si]
        for m in range(2):
            for kk in range(2):
                P.op("pe", lambda e: e.matmul(
                    pS[:, 512 * m:512 * m + n], lhsT=kxT[:, 2 * hd + kk, m * 128:(m + 1) * 128],
                    rhs=qxT[:, 2 * hd + kk, c0:c0 + n], start=(kk == 0), stop=(kk == 1)),
                    reads=[B_kxT, B_qxT], writes=[bpS])
        P.op("act", lambda e: e.activation(
            out=Px[si][:, :, 0:n], in_=pS[:, :].rearrange("p (m q) -> p m q", m=2)[:, :, 0:n], func=AF.Exp,
            scale=1.0 / 16.0), reads=[bpS], writes=[B_Px[si]])

    def xa_back(it):
        hd, c0, n = xa_items[it]
        si = it % 2
        pD, bpD = psB[2], B_psB[2]
        for m in range(2):
            P.op("pe", lambda e: e.matmul(pD[:, 0:n], lhsT=ones_bf[:, :], rhs=Px[si][:, m, 0:n],
                                          start=(m == 0), stop=(m == 1)),
                 reads=[B_Px[si], B_const], writes=[bpD])
        P.op("act", lambda e: e.activation(out=rx[si][:, 0:n], in_=pD[:, 0:n], func=AF.Ln),
             reads=[bpD], writes=[B_rx[si]])
        P.op("act", lambda e: e.activation(out=rx[si][:, 0:n], in_=rx[si][:, 0:n], func=AF.Exp, scale=-1.0),
             reads=[B_rx[si]], writes=[B_rx[si]])
        for kk in range(2):
            pO, bpO = psB[kk], B_psB[kk]
            for m in range(2):
                P.op("pe", lambda e: e.matmul(
                    pO[:, 0:n], lhsT=vx[:, m, hd * 256 + kk * 128:hd * 256 + (kk + 1) * 128], rhs=Px[si][:, m, 0:n],
                    start=(m == 0), stop=(m == 1)), reads=[B_Px[si], B_vx], writes=[bpO])
            P.op("dve", lambda e: e.tensor_tensor(
                out=oxT[:, 2 * hd + kk, c0:c0 + n], in0=pO[:, 0:n], in1=rx[si][:, 0:n], op=ALU.mult),
                reads=[bpO, B_rx[si]], writes=[B_oxT])

    for it in range(len(xa_items) + 1):
        if it < len(xa_items):
            xa_front(it)
        if it >= 1:
            xa_back(it - 1)
    def x_pre(t):
        c0, n = TT[t]
        i = t % 2
        P.dma(hr[i][0:n, :], hres_d[t * 128:t * 128 + n, :], reads=[B_hres[t]], writes=[B_hr[i]], sembuf=B_hr[i])

    def x_front(t):
        c0, n = TT[t]
        i = t % 2
        u, bu = uw[i], B_uw[i]
        P.op("dve", lambda e, t=t, u=u: e.tensor_tensor(out=u[0:n, :], in0=hr[i][0:n, :], in1=bcg[0:n, :], op=ALU.mult),
             reads=[B_hr[i], B_bc], writes=[bu])
        P.op("dve", lambda e, u=u: e.tensor_tensor(out=u[0:n, :], in0=u[0:n, :], in1=bcb[0:n, :], op=ALU.add),
             reads=[bu, B_bc], writes=[bu])
        for hc in range(2):
            pb, bpb = psS[t % 2][:, 512 * hc:512 * (hc + 1)], B_psS[t % 2]
            for k in range(8):
                P.op("pe", lambda e, hc=hc, k=k, pb=pb: e.matmul(
                    pb[0:n, :], lhsT=oxT[:, k, c0:c0 + n], rhs=wB[:, k, 512 * hc:512 * (hc + 1)],
                    start=(k == 0), stop=(k == 7)), reads=[B_oxT, B_wB], writes=[bpb])
        for hc in range(2):
            pb, bpb = psS[t % 2][:, 512 * hc:512 * (hc + 1)], B_psS[t % 2]
            P.op("dve", lambda e, hc=hc, pb=pb, u=u: e.tensor_tensor(
                out=u[0:n, 512 * hc:512 * (hc + 1)], in0=pb[0:n, :], in1=u[0:n, 512 * hc:512 * (hc + 1)], op=ALU.add),
                reads=[bpb, bu], writes=[bu])
        ln_front(t, n, u, bu)

    pipeline3(NTT, x_front, lambda t: ln_mid(t, TT[t][1], uw[t % 2], B_uw[t % 2]),
              lambda t: ln_back(t, TT[t][1], TT[t][0], "ln2_g", "ln2_b"), pre=x_pre)

    flush_ln()
    if stop_after == 3:
        P.emit(final)
        phX.close(); stack.close()
        return nc
    P.barrier(None)
    phX.close()

    phF = ExitStack()
    seek(TAIL_END[0])
    hr = [sb("hr%d" % i, [128, D], F32, phF) for i in range(2)]
    B_hr = [Buf("hrF%d" % i) for i in range(2)]
    HT = 1024
    aT = sb("aT", [128, NFF, HT], BF16, phF)
    B_aT = Buf("aT")
    wg = [sb("wg%d" % i, [128, 8, 128], BF16, phF) for i in range(2)]
    wu = [sb("wu%d" % i, [128, 8, 128], BF16, phF) for i in range(2)]
    B_wg = [Buf("wg%d" % i) for i in range(2)]
    B_wu = [Buf("wu%d" % i) for i in range(2)]
    wdn = sb("wdn", [128, NFF, D], BF16, phF)
    B_wdn = Buf("wdn")
    gp = [sb("gp0", [128, HT + 2], F32, phF)] * 2
    B_gp = [Buf("gp0")] * 2
    gc = [sb("gc0", [128, HT], F32, phF)] * 2
    B_gc = [Buf("gc0")] * 2
    ge = [sb("ge0", [128, HT], F32, phF)] * 2
    B_ge = [Buf("ge0")] * 2
    g3b = sb("g3b", [128, D], F32, phF)
    b3b = sb("b3b", [128, D], F32, phF)
    B_g3 = Buf("g3")
    B_ostg = [Buf("ostg%d" % i) for i in range(2)]
    def f_dma(g):
        j = g % NFF
        i = g % 2
        load_cast(wg[i][:], B_wg[i], w_gate, j * 128, 128)
        load_cast(wu[i][:], B_wu[i], w_up, j * 128, 128)

    f_dma(0)
    load_bc(4)
    def wdn_piece(r0, kc=2):
        load_cast(wdn[:, r0:r0 + kc, :], B_wdn, w_down, 0, D, kchunks=kc, row0=r0 * 128)

    def mcol(tok):
        if tok < 0:
            return TOWN
        if tok >= TOWN:
            return TOWN + 1
        return tok

    for hf in range(2):
        t0 = hf * HT
        for j in range(NFF):
            i = j % 2
            gidx = hf * NFF + j
            if gidx + 1 < 2 * NFF:
                f_dma(gidx + 1)
            if hf == 0 and j == 12:
                P.dma(g3b[:], rows_d[6:7, :].partition_broadcast(128), writes=[B_g3], sembuf=B_g3)
                P.dma(b3b[:], rows_d[7:8, :].partition_broadcast(128), writes=[B_g3], sembuf=B_g3)
            if hf == 0 and 1 <= j <= NFF // 2:
                wdn_piece(2 * (j - 1))
            pieces = [(mcol(t0 - 1), 1, 0), (t0, 512, 1), (t0 + 512, 512, 513), (mcol(t0 + HT), 1, HT + 1)]
            for pi, (cc, n, go) in enumerate(pieces):
                pz, bpz = psB[pi], B_psB[pi]
                for k in range(8):
                    P.op("pe", lambda e, k=k, pz=pz, cc=cc, n=n: e.matmul(
                        pz[:, 0:n], lhsT=wg[i][:, k, :], rhs=hxT[:, k, cc:cc + n], start=(k == 0), stop=(k == 7)),
                        reads=[B_wg[i]] + B_hxT, writes=[bpz])
                if n == 1 and cc >= TOWN:
                    P.op("act", lambda e, pz=pz, go=go, cc=cc: e.activation(
                        out=gp[i][:, go:go + 1], in_=pz[:, 0:1], func=AF.Copy, scale=gval[:, cc - TOWN:cc - TOWN + 1]),
                        reads=[bpz, B_const], writes=[B_gp[i]])
                else:
                    P.op("act", lambda e, pz=pz, go=go, n=n: e.copy(out=gp[i][:, go:go + n], in_=pz[:, 0:n]),
                         reads=[bpz], writes=[B_gp[i]])
            P.op("dve", lambda e, j=j: e.tensor_scalar(out=gc[i][:], in0=gp[i][:, 0:HT], scalar1=pcol("cw0", j),
                                                       scalar2=pcol("cb", j), op0=ALU.mult, op1=ALU.add),
                 reads=[B_gp[i], B_const], writes=[B_gc[i]])
            P.op("dve", lambda e, j=j: e.scalar_tensor_tensor(out=gc[i][:], in0=gp[i][:, 1:HT + 1], scalar=pcol("cw1", j),
                                                              in1=gc[i][:], op0=ALU.mult, op1=ALU.add),
                 reads=[B_gp[i], B_gc[i], B_const], writes=[B_gc[i]])
            P.op("dve", lambda e, j=j: e.scalar_tensor_tensor(out=gc[i][:], in0=gp[i][:, 2:HT + 2], scalar=pcol("cw2", j),
                                                               in1=gc[i][:], op0=ALU.mult, op1=ALU.add),
                 reads=[B_gp[i], B_gc[i], B_const], writes=[B_gc[i]])
            P.op("act", lambda e: e.activation(out=ge[i][:], in_=gc[i][:], func=AF.Gelu), reads=[B_gc[i]],
                 writes=[B_ge[i]])
            for ub_ in range(2):
                pz, bpz = psS[j % 2][:, 512 * ub_:512 * (ub_ + 1)], B_psS[j % 2]
                for k in range(8):
                    P.op("pe", lambda e, k=k, pz=pz, ub_=ub_: e.matmul(
                        pz[:, :], lhsT=wu[i][:, k, :], rhs=hxT[:, k, t0 + 512 * ub_:t0 + 512 * (ub_ + 1)],
                        start=(k == 0), stop=(k == 7)), reads=[B_wu[i]] + B_hxT, writes=[bpz])
            for ub_ in range(2):
                pz, bpz = psS[j % 2][:, 512 * ub_:512 * (ub_ + 1)], B_psS[j % 2]
                P.op("dve", lambda e, j=j, pz=pz, ub_=ub_: e.tensor_tensor(
                    out=aT[:, j, 512 * ub_:512 * (ub_ + 1)], in0=pz[:, :], in1=ge[i][:, 512 * ub_:512 * (ub_ + 1)],
                    op=ALU.mult), reads=[bpz, B_ge[i]], writes=[B_aT])
        def f_pre(tt, hf=hf):
            t = hf * 8 + tt
            i = t % 2
            P.dma(hr[i][:], hres_d[t * 128:(t + 1) * 128, :], reads=[B_hres[t]], writes=[B_hr[i]], sembuf=B_hr[i])

        def f_front(tt, hf=hf):
            t = hf * 8 + tt
            i = t % 2
            u, bu = uw[i], B_uw[i]
            P.op("dve", lambda e, t=t, u=u: e.tensor_tensor(out=u[:], in0=hr[i][:], in1=bcg[:], op=ALU.mult),
                 reads=[B_hr[i], B_bc], writes=[bu])
            P.op("dve", lambda e, u=u: e.tensor_tensor(out=u[:], in0=u[:], in1=bcb[:], op=ALU.add),
                 reads=[bu, B_bc], writes=[bu])
            if tt == 0:
                pass
            for hc in range(2):
                pb, bpb = psS[tt % 2][:, 512 * hc:512 * (hc + 1)], B_psS[tt % 2]
                for j in range(NFF):
                    P.op("pe", lambda e, hc=hc, j=j, pb=pb, tt=tt: e.matmul(
                        pb[:, :], lhsT=aT[:, j, tt * 128:(tt + 1) * 128], rhs=wdn[:, j, 512 * hc:512 * (hc + 1)],
                        start=(j == 0), stop=(j == NFF - 1)), reads=[B_aT, B_wdn], writes=[bpb])
            for hc in range(2):
                pb, bpb = psS[tt % 2][:, 512 * hc:512 * (hc + 1)], B_psS[tt % 2]
                P.op("dve", lambda e, hc=hc, pb=pb, u=u: e.tensor_tensor(
                    out=u[:, 512 * hc:512 * (hc + 1)], in0=pb[:, :], in1=u[:, 512 * hc:512 * (hc + 1)], op=ALU.add),
                    reads=[bpb, bu], writes=[bu])
            ln_stats(u, 128, mr[i][:, 0:1], mr[i][:, 1:2], [bu], [B_mr[i]])

        def f_mid(tt, hf=hf):
            t = hf * 8 + tt
            i = t % 2
            u, bu = uw[i], B_uw[i]
            P.op("dve", lambda e, u=u, i=i: e.tensor_scalar(out=u[:], in0=u[:], scalar1=mr[i][:, 0:1],
                                                            scalar2=mr[i][:, 1:2], op0=ALU.subtract, op1=ALU.mult),
                 reads=[bu, B_mr[i]], writes=[bu])
            P.op("dve", lambda e, u=u: e.tensor_tensor(out=u[:], in0=u[:], in1=g3b[:], op=ALU.mult),
                 reads=[bu, B_g3], writes=[bu])
            P.op("dve", lambda e, u=u, i=i: e.tensor_tensor(out=u[:], in0=u[:], in1=b3b[:], op=ALU.add),
                 reads=[bu, B_g3], writes=[bu])
            final.append(P.dma(out_d[t * 128:(t + 1) * 128, :], u[:], reads=[bu], sembuf=B_ostg[i]))
        pipeline3(8, f_front, f_mid, None, pre=f_pre)
    P.emit(final)
    phF.close()
    stack.close()
    return nc


def host_consts():
    c = np.zeros((128, 1024), np.float32)
    c[:, 0:128] = np.eye(128)
    pm = np.zeros((128, 128), np.float32)
    for m in range(128):
        d = m % 64
        if d < 8:
            pm[m + 8, m] = 1.0
        elif d < 16:
            pm[m - 8, m] = 1.0
    c[:, 128:256] = pm
    r = np.arange(128)[:, None]
    xx = np.arange(384)[None, :]
    c[:, 256:640] = (np.abs(xx - 128 - r) <= 64)
    c[:, 640:1024] = (np.abs(xx - 128 - r) <= 128)
    return c.astype(ml_dtypes.bfloat16)


def host_consts2():
    r = np.arange(128)[:, None]
    cc = np.arange(128)[None, :]
    out = np.zeros((128, 640), np.float32)
    for t_ in range(2):
        out[:, t_ * 128:(t_ + 1) * 128] = (np.abs(64 - 128 * t_ + cc - r) <= 64)
    for t_ in range(3):
        out[:, 256 + t_ * 128:256 + (t_ + 1) * 128] = (np.abs(128 - 128 * t_ + cc - r) <= 128)
    return out.astype(ml_dtypes.bfloat16)


def host_pcols(inp):
    pcv = np.zeros((128, 256), np.float32)
    o = [0]

    def put(arr, n):
        a = np.asarray(arr, np.float32).reshape(n, 128).T
        pcv[:, o[0]:o[0] + n] = a
        o[0] += n

    put(inp["ln_in_g"], 8)
    put(inp["ln_in_b"], 8)
    put(inp["ln1_g"][0], 8)
    put(inp["ln1_b"][0], 8)
    put(inp["ln2_g"][0], 8)
    put(inp["ln2_b"][0], 8)
    put(inp["mem_ln_g"][0], 8)
    put(inp["mem_ln_b"][0], 8)
    put(inp["g_win"][0], 4)
    put(inp["g_dil"][0], 4)
    sk = np.asarray(inp["attn_sink"][0], np.float32)
    pcv[:, o[0]:o[0] + 8] = sk[None, :]
    o[0] += 8
    p = np.arange(128)
    d = p % 64
    invf = (np.float32(500000.0) ** (-(np.arange(0, 16, 2, dtype=np.float32)) / np.float32(16))).astype(np.float32)
    pcv[:, o[0]] = invf[d % 8]
    pcv[:, o[0] + 1] = (d < 16)
    pcv[:, o[0] + 2] = np.where(d < 8, -1.0, np.where(d < 16, 1.0, 0.0))
    pcv[:, o[0] + 3] = EPS
    o[0] += 5
    cw = np.asarray(inp["conv_w"][0], np.float32)
    for j in range(3):
        put(cw[j], NFF)
    put(inp["conv_b"][0], NFF)
    return pcv


def core_inputs(inp, core, shared):
    b = core // 4
    s0 = (core % 4) * TOWN
    e0 = s0 - HALO
    idx = np.arange(EXT) + e0
    ok = (idx >= 0) & (idx < SEQ)
    ci = np.clip(idx, 0, SEQ - 1)
    x_ext = np.where(ok[:, None], inp["x"][b][ci], np.float32(0)).astype(np.float32)
    pos = np.where(ok, inp["positions"][b][ci], 0).astype(np.int32)[None, :]
    valid = ok.astype(np.float32)
    kv = np.zeros((128, NVT), np.float32)
    for g in GROUPS:
        for (r, c, ks, nk), vi in PLANS[g][1].items():
            e = c + r * (ks + np.arange(nk))
            kv[:nk, VT_BASE[g] + vi] = valid[e]
    gv = np.array([[valid[EXTRA_E[0]], valid[EXTRA_E[1]]]], np.float32)
    m = dict(shared)
    m.update(x_ext=x_ext, pos=pos, kvtab=kv, gvalid=gv, mem=np.ascontiguousarray(inp["mem"][b], np.float32))
    return m


def shared_inputs(inp):
    f = lambda a: np.ascontiguousarray(np.asarray(a, np.float32))
    rows = np.stack([inp["ln_in_g"], inp["ln_in_b"], inp["ln1_g"][0], inp["ln1_b"][0], inp["ln2_g"][0],
                     inp["ln2_b"][0], inp["ln3_g"][0], inp["ln3_b"][0]]).astype(np.float32)
    return dict(cbf=host_consts(), pcols=host_pcols(inp), w_in=f(inp["w_in"][0]), w_out=f(inp["w_mix_out"][0]),
                w_xq=f(inp["w_xq"][0]), w_xk=f(inp["w_xk"][0]), w_xv=f(inp["w_xv"][0]), w_xo=f(inp["w_xo"][0]),
                w_gate=f(inp["w_gate"][0]), w_up=f(inp["w_up"][0]), w_down=f(inp["w_down"][0]), rows=rows)


def kernel(**inp):
    inp = {k: np.asarray(v) for k, v in inp.items()}
    nc = build_program()
    shared = shared_inputs(inp)
    in_maps = [core_inputs(inp, c, shared) for c in range(8)]
    res = run_bass_kernel_spmd(nc, in_maps, core_ids=list(range(8)))
    out = np.zeros((2, SEQ, D), np.float32)
    for c in range(8):
        out[c // 4, (c % 4) * TOWN:(c % 4 + 1) * TOWN] = res.results[c]["out"]
    return out
```

```python
import math
from contextlib import ExitStack
import numpy as np
import ml_dtypes
import concourse.bass as bass
import concourse.mybir as mybir
from concourse.bass_utils import run_bass_kernel_spmd

F32 = mybir.dt.float32
BF16 = mybir.dt.bfloat16
I32 = mybir.dt.int32
AF = mybir.ActivationFunctionType
ALU = mybir.AluOpType
AX = mybir.AxisListType

D = 1024
SEQ = 8192
TOWN = 2048
HALO = 1152
EXT = TOWN + 2 * HALO
NXT = EXT // 128
OWN0 = HALO
NMID = TOWN + 2
EXTRA_E = (OWN0 - 1, OWN0 + TOWN)
DFF = 2816
NFF = DFF // 128
MEM = 256
ALPHA = 2.0 ** 0.25
EPS = 1e-5
QA0, KA0, VA0 = 0, 512, 640
QB0, KB0, VB0 = 768, 768 + 1536, 768 + 3072
GROUPS = {"A": (1, 128), "g1": (1, 64), "g2": (4, 64), "g3": (16, 64)}
PCOLS = (("lnin_g", 8), ("lnin_b", 8), ("ln1_g", 8), ("ln1_b", 8), ("ln2_g", 8), ("ln2_b", 8), ("mem_g", 8),
         ("mem_b", 8), ("gwin", 4), ("gdil", 4), ("sink0", 8), ("invf", 1), ("rotm", 1), ("sgn", 1), ("eps", 1), ("zero", 1),
         ("cw0", 22), ("cw1", 22), ("cw2", 22), ("cb", 22))


def mid_col_e(c):
    return OWN0 + c if c < TOWN else EXTRA_E[c - TOWN]


def e_mid_col(e):
    if OWN0 <= e < OWN0 + TOWN:
        return e - OWN0
    return TOWN + EXTRA_E.index(e)


def q_blocks(r):
    out = []
    for c in range(r):
        j0 = (OWN0 - c + r - 1) // r
        j1 = (OWN0 + TOWN - c + r - 1) // r
        for ja in range(j0, j1, 128):
            out.append((c, ja, min(128, j1 - ja)))
    for e in EXTRA_E:
        out.append((e % r, e // r, 1))
    return out


def key_tiles(r, W, blk):
    c, ja, nq = blk
    lo, hi = ja - W, ja + nq + W
    tiles = []
    ks = lo
    while ks < hi:
        nk = min(128, hi - ks)
        tiles.append((r, c, ks, nk, ja - ks))
        ks += 128
    return tiles


def group_plan(gname):
    r, W = GROUPS[gname]
    blks = q_blocks(r)
    vt = {}
    plan = []
    for b in blks:
        kts = key_tiles(r, W, b)
        for (rr, c, ks, nk, dl) in kts:
            key = (rr, c, ks, nk)
            if key not in vt:
                vt[key] = len(vt)
        plan.append((b, kts))
    emin = min(k[1] + k[0] * k[2] for k in vt)
    emax = max(k[1] + k[0] * (k[2] + k[3] - 1) for k in vt)
    return plan, vt, emin, emax + 1


PLANS = {g: group_plan(g) for g in GROUPS}
VT_BASE = {}
_n = 0
for _g in GROUPS:
    VT_BASE[_g] = _n
    _n += len(PLANS[_g][1])
NVT = _n


class Buf:
    __slots__ = ("name", "w", "r", "sem", "cnt", "excl")

    def __init__(self, name, excl=False):
        self.name = name
        self.excl = excl
        self.w = None
        self.r = {}
        self.sem = None
        self.cnt = 0


class _Rec:
    def __init__(self):
        self.calls = []

    def __getattr__(self, name):
        def f(*a, **k):
            self.calls.append((name, a, k))
            return self
        return f


class Op:
    __slots__ = ("eng", "fn", "deps", "signal", "count", "dma")

    def __init__(self, eng, fn):
        self.eng = eng
        if fn is not None:
            rec = _Rec()
            fn(rec)
            calls = rec.calls

            def replay(e, calls=calls):
                ins = None
                for (name, a, k) in calls:
                    ins = getattr(e, name)(*a, **k)
                return ins
            self.fn = replay
        else:
            self.fn = None
        self.deps = []
        self.signal = False
        self.count = 0
        self.dma = None


class Prog:
    ENGS = ("pe", "act", "dve", "pool", "sp")

    def __init__(self, nc, stack):
        self.nc = nc
        self.stack = stack
        self.streams = {e: [] for e in self.ENGS}
        self.sems = {e: stack.enter_context(nc.semaphore("s_" + e)) for e in self.ENGS}
        self.nsem = 0

    def _deps(self, eng, reads, writes):
        deps = []
        for b in reads:
            if b.w is not None:
                deps.append(b.w)
            if b.excl:
                deps.extend(v for k, v in b.r.items() if k != eng)
        for b in writes:
            if b.w is not None:
                deps.append(b.w)
            deps.extend(b.r.values())
        out = []
        for d in deps:
            if d[0] == "op":
                if eng == "pe" and d[1].eng == "pe":
                    continue
                d[1].signal = True
            out.append(d)
        return out

    def op(self, eng, fn, reads=(), writes=()):
        o = Op(eng, fn)
        o.deps = self._deps(eng, reads, writes)
        d = ("op", o)
        for b in reads:
            b.r[eng] = d
        for b in writes:
            b.w = d
            b.r = {}
        self.streams[eng].append(o)
        return o

    def dma(self, out, in_, reads=(), writes=(), sembuf=None, eng="sp", **kw):
        if sembuf.sem is None:
            sembuf.sem = self.stack.enter_context(self.nc.semaphore("d%d" % self.nsem))
            self.nsem += 1
        o = Op(eng, lambda e: e.dma_start(out=out, in_=in_, **kw))
        o.deps = self._deps(eng, reads, writes)
        sembuf.cnt += 16
        o.dma = (sembuf.sem, 16)
        d = ("dma", sembuf.sem, sembuf.cnt)
        for b in reads:
            b.r["dma%d" % id(sembuf)] = d
        for b in writes:
            b.w = d
            b.r = {}
        self.streams[eng].append(o)
        return d

    def barrier(self, bufs):
        lasts = []
        for e in ("pe", "act", "dve", "pool"):
            for o in reversed(self.streams[e]):
                if o.fn is None:
                    break
                if o.dma is None:
                    o.signal = True
                    lasts.append(("op", o))
                    break
        for e in ("pe", "act", "dve", "pool", "sp"):
            o = Op(e, None)
            o.deps = list(lasts)
            self.streams[e].append(o)

    def emit(self, final_deps):
        nc = self.nc
        for e in self.ENGS:
            c = 0
            for o in self.streams[e]:
                if o.signal:
                    c += 1
                o.count = c
        handles = {}

        def run(ename, eng, tail=None):
            seen = {}
            for o in self.streams[ename]:
                for d in o.deps:
                    if d[0] == "op":
                        key, val, sem = d[1].eng, d[1].count, self.sems[d[1].eng]
                    else:
                        key, val, sem = id(d[1]), d[2], d[1]
                    if seen.get(key, 0) < val:
                        eng.wait_ge(sem, val)
                        seen[key] = val
                if o.fn is None:
                    continue
                ins = o.fn(eng)
                if o.dma is not None:
                    ins.then_inc(o.dma[0], o.dma[1])
                elif o.signal:
                    ins.then_inc(self.sems[ename], 1)
            if tail:
                for d in tail:
                    eng.wait_ge(d[1], d[2])

        with nc.Block() as block:
            @block.tensor
            def _(e):
                run("pe", e)

            @block.scalar
            def _(e):
                run("act", e)

            @block.vector
            def _(e):
                run("dve", e)

            @block.gpsimd
            def _(e):
                run("pool", e)

            @block.sync
            def _(e):
                run("sp", e, tail=final_deps)


def build_program(dbg=None, stop_after=None):
    dbg = dbg or {}
    nc = bass.Bass("TRN2", target_bir_lowering=False)
    stack = ExitStack()
    P = Prog(nc, stack)

    def dram(name, shape, dt, kind="ExternalInput"):
        return nc.dram_tensor(name, list(shape), dt, kind=kind).ap()

    def sbp(name, shape, dt):
        return stack.enter_context(nc.sbuf_tensor("sb_" + name, list(shape), dt))

    arena_box = {}
    cur = [0]

    def seek(off):
        cur[0] = off

    def sb(name, shape, dt, st=None):
        esz = 2 if dt == BF16 else 4
        n = 1
        for d_ in shape[1:]:
            n *= d_
        nbytes = (n * esz + 31) // 32 * 32
        off = cur[0]
        assert off + nbytes <= arena_box["size"], (name, off, nbytes, arena_box["size"])
        cur[0] = off + nbytes
        ap = arena_box["t"][0:shape[0], off // 2:off // 2 + n * esz // 2]
        if esz == 4:
            ap = ap.bitcast(dt)
        if len(shape) == 3:
            ap = ap.rearrange("p (a b) -> p a b", a=shape[1])
        elif len(shape) == 4:
            ap = ap.rearrange("p (a b c) -> p a b c", a=shape[1], b=shape[2])
        return ap

    x_ext = dram("x_ext", [EXT, D], F32)
    pos_d = dram("pos", [1, EXT], I32)
    kvtab_d = dram("kvtab", [128, NVT], F32)
    gvalid_d = dram("gvalid", [1, 2], F32)
    cbf_d = dram("cbf", [128, 1024], BF16)
    pc_d = dram("pcols", [128, 256], F32)
    mem_d = dram("mem", [MEM, D], F32)
    w_in = dram("w_in", [D, 5376], F32)
    w_out = dram("w_out", [D, D], F32)
    w_xq = dram("w_xq", [D, D], F32)
    w_xk = dram("w_xk", [D, D], F32)
    w_xv = dram("w_xv", [D, D], F32)
    w_xo = dram("w_xo", [D, D], F32)
    w_gate = dram("w_gate", [D, DFF], F32)
    w_up = dram("w_up", [D, DFF], F32)
    w_down = dram("w_down", [DFF, D], F32)
    rows_d = dram("rows", [8, D], F32)
    out_d = dram("out", [TOWN, D], F32, kind="ExternalOutput")
    dbg_out = {}
    for k, (shape, dt) in dbg.items():
        dbg_out[k] = dram("dbg_" + k, shape, dt, kind="ExternalOutput")

    PC = {}
    _o = 0
    for nm, n in PCOLS:
        PC[nm] = _o
        _o += n
    assert _o <= 256

    cbf = sbp("cbf", [128, 1024], BF16)
    pc = sbp("pc", [128, 256], F32)
    kvt = sbp("kvt", [128, NVT], F32)
    stats = sbp("stats", [128, NXT, 2], F32)
    ones_bf = sbp("ones_bf", [128, 128], BF16)
    esk = sbp("esk", [128, 8], F32)
    gval = sbp("gval", [128, 2], F32)
    B_const = Buf("const")
    ident = cbf[:, 0:128]
    perm = cbf[:, 128:256]
    bnd = {64: cbf[:, 256:640], 128: cbf[:, 640:1024]}
    psS = [stack.enter_context(nc.psum_tensor("psS%d" % i, [128, 1024], F32)) for i in range(2)]
    psB0 = stack.enter_context(nc.psum_tensor("psB0", [128, 512], F32))
    psZZ = stack.enter_context(nc.psum_tensor("psZZ", [128, 1024], F32))
    psB3 = stack.enter_context(nc.psum_tensor("psB3", [128, 512], F32))
    psB = [psB0, psZZ[:, 0:512], psZZ[:, 512:1024], psB3]
    B_psS = [Buf("psS%d" % i, excl=True) for i in range(2)]
    B_psB = [Buf("psB%d" % i, excl=True) for i in range(4)]

    P.dma(cbf[:], cbf_d, writes=[B_const], sembuf=B_const)
    mpk = {64: bnd[64][:, 64:320].rearrange("p (t q) -> p t q", t=2), 128: bnd[128][:, 0:384].rearrange("p (t q) -> p t q", t=3)}
    P.dma(pc[:], pc_d, writes=[B_const], sembuf=B_const)
    P.dma(kvt[:], kvtab_d, writes=[B_const], sembuf=B_const)
    P.dma(gval[:], gvalid_d.partition_broadcast(128), writes=[B_const], sembuf=B_const)
    P.op("pool", lambda e: e.memset(ones_bf[:], 1.0), writes=[B_const])
    P.op("act", lambda e: e.activation(out=esk[:], in_=pc[:, PC["sink0"]:PC["sink0"] + 8], func=AF.Exp),
         reads=[B_const], writes=[B_const])

    def pcol(nm, k=0, p0=0, p1=128):
        return pc[p0:p1, PC[nm] + k: PC[nm] + k + 1]

    final = []
    B_dbg = Buf("dbg")

    def tap(name, ap, reads):
        if name in dbg_out:
            final.append(P.dma(dbg_out[name], ap, reads=reads, sembuf=B_dbg))

    lnw = [sbp("lnw%d" % i, [128, 16], F32) for i in range(2)]
    B_lnw = [Buf("lnw%d" % i) for i in range(2)]
    ln_ctr = [0]

    def ln_stats(x_ap, n, mean_out, rstd_out, rd, wr):
        i = ln_ctr[0] % 2
        ln_ctr[0] += 1
        w = lnw[i]
        bw = [B_lnw[i]]
        P.op("dve", lambda e: e.bn_stats(w[0:n, 0:6], x_ap[0:n, 0:512]), reads=rd, writes=bw)
        P.op("dve", lambda e: e.bn_stats(w[0:n, 6:12], x_ap[0:n, 512:1024]), reads=rd, writes=bw)
        P.op("dve", lambda e: e.bn_aggr(w[0:n, 12:14], w[0:n, 0:12]), reads=bw, writes=bw)
        P.op("dve", lambda e: e.tensor_copy(out=mean_out, in_=w[0:n, 12:13]), reads=bw, writes=wr)
        P.op("act", lambda e: e.activation(out=w[0:n, 14:15], in_=w[0:n, 13:14], func=AF.Ln, bias=pcol("eps", 0, 0, n)),
             reads=bw + [B_const], writes=bw)
        P.op("act", lambda e: e.activation(out=rstd_out, in_=w[0:n, 14:15], func=AF.Exp, scale=-0.5),
             reads=bw, writes=wr)

    mr = [sbp("mr%d" % i, [128, 4], F32) for i in range(2)]
    B_mr = [Buf("mr%d" % i) for i in range(2)]
    arena_box["size"] = (nc.sbuf_bytes_remaining - 256) // 64 * 64
    arena_box["t"] = stack.enter_context(nc.sbuf_tensor("arena", [128, arena_box["size"] // 2], BF16))
    stM = ExitStack()
    seek(0)
    hT = sb("hT", [128, 8, EXT], BF16, stM)
    B_hT = [Buf("hT%d" % t) for t in range(NXT)]
    ctab = sb("ctab", [128, EXT], F32, stM)
    stab = sb("stab", [128, EXT], F32, stM)
    B_tab = Buf("tab")
    B_stats = [Buf("stats%d" % t) for t in range(NXT)]
    mixedT = sb("mixedT", [128, 8, NMID], BF16)
    B_mixed = [Buf("mixed%d" % k) for k in range(8)]
    MIX_END = cur[0]

    ph0 = ExitStack()
    xs = [sb("xs%d" % i, [128, D], F32, ph0) for i in range(3)]
    B_xs = [Buf("xs%d" % i) for i in range(3)]
    xh = [sb("xh%d" % i, [128, D], BF16, ph0) for i in range(2)]
    B_xh = [Buf("xh%d" % i) for i in range(2)]
    posi = sb("posi", [128, EXT], I32, ph0)
    ang = sb("ang", [128, EXT], F32, ph0)
    ktf = sb("ktf", [128, EXT], F32, ph0)
    kti = posi
    B_pos = Buf("pos")
    B_ang = Buf("ang")
    B_kt = Buf("kt")

    deferred = []

    def dop(*a_, **k_):
        deferred.append((a_, k_))

    dop("dve", lambda e: e.tensor_copy(out=ang[:], in_=posi[:]), reads=[B_pos, B_const], writes=[B_ang])
    dop("dve", lambda e: e.tensor_scalar(out=ang[:], in0=ang[:], scalar1=pcol("invf"), scalar2=None,
                                          op0=ALU.mult), reads=[B_ang, B_const], writes=[B_ang])
    TWO_PI = 2.0 * math.pi
    C1 = 6.28125
    C2 = TWO_PI - C1

    def range_reduce(dst, bdst, shift):
        dop("dve", lambda e: e.tensor_scalar(out=ktf[:], in0=ang[:], scalar1=shift, scalar2=1.0 / TWO_PI,
                                              op0=ALU.add, op1=ALU.mult), reads=[B_ang], writes=[B_kt])
        dop("dve", lambda e: e.tensor_copy(out=kti[:], in_=ktf[:]), reads=[B_kt], writes=[B_kt, B_pos])
        dop("dve", lambda e: e.tensor_copy(out=ktf[:], in_=kti[:]), reads=[B_kt], writes=[B_kt])
        dop("dve", lambda e: e.tensor_scalar(out=dst[:], in0=ang[:], scalar1=shift, scalar2=None, op0=ALU.add),
             reads=[B_ang], writes=[bdst])
        for cc in (C1, C2):
            dop("dve", lambda e, cc=cc: e.scalar_tensor_tensor(out=dst[:], in0=ktf[:], scalar=-cc, in1=dst[:],
                                                                op0=ALU.mult, op1=ALU.add),
                 reads=[B_kt, bdst], writes=[bdst])
        dop("dve", lambda e: e.tensor_scalar(out=ktf[:], in0=dst[:], scalar1=math.pi, scalar2=-TWO_PI,
                                              op0=ALU.is_gt, op1=ALU.mult), reads=[bdst], writes=[B_kt])
        dop("dve", lambda e: e.tensor_tensor(out=dst[:], in0=dst[:], in1=ktf[:], op=ALU.add),
             reads=[B_kt, bdst], writes=[bdst])
        dop("dve", lambda e: e.tensor_scalar(out=dst[:], in0=dst[:], scalar1=3.1415925, scalar2=-3.1415925,
                                              op0=ALU.min, op1=ALU.max), reads=[bdst], writes=[bdst])

    range_reduce(stab, B_tab, 0.0)
    range_reduce(ctab, B_tab, 0.5 * math.pi)
    dop("act", lambda e: e.activation(out=stab[:], in_=stab[:], func=AF.Sin), reads=[B_tab], writes=[B_tab])
    dop("act", lambda e: e.activation(out=ctab[:], in_=ctab[:], func=AF.Sin), reads=[B_tab], writes=[B_tab])
    dop("dve", lambda e: e.tensor_scalar(out=stab[:], in0=stab[:], scalar1=pcol("sgn"), scalar2=None,
                                          op0=ALU.mult), reads=[B_tab, B_const], writes=[B_tab])
    dop("dve", lambda e: e.tensor_scalar(out=ctab[:], in0=ctab[:], scalar1=-1.0, scalar2=pcol("rotm"),
                                          op0=ALU.add, op1=ALU.mult), reads=[B_tab, B_const], writes=[B_tab])
    dop("dve", lambda e: e.tensor_scalar(out=ctab[:], in0=ctab[:], scalar1=1.0, scalar2=None,
                                          op0=ALU.add), reads=[B_tab], writes=[B_tab])

    def bc8(nm):
        return pc[:, PC[nm]:PC[nm] + 8].unsqueeze(2).to_broadcast([128, 8, 128])

    def p0_pre(t):
        j = t % 3
        P.dma(xs[j][:], x_ext[t * 128:(t + 1) * 128, :], writes=[B_xs[j]], sembuf=B_xs[j])

    def p0_front(t):
        j = t % 3
        ln_stats(xs[j], 128, stats[:, t, 0:1], stats[:, t, 1:2], [B_xs[j]], [B_stats[t]])

    def p0_back(t):
        i = t % 2
        j = t % 3
        P.op("dve", lambda e: e.tensor_scalar(out=xh[i][:], in0=xs[j][:], scalar1=stats[:, t, 0:1],
                                              scalar2=stats[:, t, 1:2], op0=ALU.subtract, op1=ALU.mult),
             reads=[B_xs[j], B_stats[t]], writes=[B_xh[i]])
        pb = t % 2
        psb = psS[pb][:, 0:512].bitcast(BF16)
        for k in range(8):
            P.op("pe", lambda e, k=k: e.transpose(psb[:, k * 128:(k + 1) * 128], xh[i][:, k * 128:(k + 1) * 128], ident),
                 reads=[B_xh[i], B_const], writes=[B_psS[pb]])
        for k in range(8):
            P.op("act", lambda e, k=k: e.activation(
                out=hT[:, k, t * 128:(t + 1) * 128], in_=psb[:, k * 128:(k + 1) * 128], func=AF.Identity,
                scale=pcol("lnin_g", k), bias=pcol("lnin_b", k)), reads=[B_psS[pb], B_const], writes=[B_hT[t]])

    p0_pre(0)
    p0_pre(1)
    P.dma(posi[:], pos_d.partition_broadcast(128), writes=[B_pos], sembuf=B_pos)
    for t in range(NXT + 1):
        if t < NXT:
            p0_front(t)
        if t >= 1:
            p0_back(t - 1)
        if t + 2 < NXT:
            p0_pre(t + 2)
        if deferred:
            a_, k_ = deferred.pop(0)
            P.op(*a_, **k_)
    while deferred:
        a_, k_ = deferred.pop(0)
        P.op(*a_, **k_)

    tap("hT", hT[:], B_hT)
    tap("ctab", ctab[:], [B_tab])
    tap("stab", stab[:], [B_tab])
    if stop_after == 0:
        P.emit(final)
        ph0.close(); stM.close(); stack.close()
        return nc
    P.barrier(None)
    ph0.close()

    phM = ExitStack()
    seek(MIX_END)
    stw = [sb("stw%d" % i, [128, 2, 128], F32, phM) for i in range(2)]
    stw.append(stw[0])
    wbf = [sb("wbf%d" % i, [128, 8, 128], BF16, phM) for i in range(3)]
    B_stw = [Buf("stw%d" % i) for i in range(2)]
    B_stw.append(B_stw[0])
    wtmp = [stw[0][:, 0:2, :].bitcast(BF16)[:, :, :].rearrange("p a b -> p (a b)")[:, 0:512].rearrange("p (k c) -> p k c", k=8),
            stw[1][:, 0:2, :].bitcast(BF16)[:, :, :].rearrange("p a b -> p (a b)")[:, 0:512].rearrange("p (k c) -> p k c", k=8)]
    B_wtmp = [Buf("wtmp0"), Buf("wtmp1")]
    B_wbf = [Buf("wbf%d" % i) for i in range(3)]
    QT = sb("QT", [128, NMID], BF16, phM)
    KT = sb("KT", [128, EXT], BF16, phM)
    NVMAX = max(len(PLANS[g][1]) for g in GROUPS)
    VT = sb("VT", [128, NVMAX, 192], BF16, phM)
    acc = sb("acc", [128, 2, NMID], F32, phM)
    Pt = [sb("Pt%d" % i, [128, 2, 3, 128], BF16, phM) for i in range(3)]
    zb = [sb("zb%d" % i, [128, 512], BF16, phM) for i in range(2)]
    t1 = [sb("t1%d" % i, [128, 512], F32, phM) for i in range(2)]
    t2 = [sb("t2_0", [128, 512], F32, phM)]
    t2.append(t2[0])
    rtmp = sb("rtmp", [128, 2, 256], F32, phM)
    B_QT, B_KT, B_acc, B_rtmp = Buf("QT"), Buf("KT"), Buf("acc"), Buf("rtmp")
    B_rtmp2 = [Buf("rtmpA"), Buf("rtmpB")]
    B_VT = [Buf("VT%d" % i) for i in range(NVMAX)]
    B_Pt = [Buf("Pt%d" % i) for i in range(3)]
    SSETS = [(psS[0], [B_psS[0]]), (psS[1], [B_psS[1]]), (psZZ, [B_psB[1], B_psB[2]])]
    B_zb = [Buf("zb%d" % i) for i in range(2)]
    B_t1 = [Buf("t1%d" % i) for i in range(2)]
    B_t2 = [Buf("t2_0")]
    B_t2.append(B_t2[0])
    psUD, psSW = psB[0], psB[3]
    B_psUD, B_psSW = B_psB[0], B_psB[3]
    psZ = [psB[1], psB[2]]
    B_psZ = [B_psB[1], B_psB[2]]
    ctr = {"z": 0, "r": 0, "s": 0}

    def w_dma(role, col0, ncols, dup):
        src = w_in[:, col0:col0 + ncols].rearrange("(k p) c -> p k c", p=128)
        if dup:
            P.dma(wtmp[role - 1][:], src, writes=[B_wtmp[role - 1]], sembuf=B_wtmp[role - 1], eng="pool")
            for hh in range(2):
                P.op("act", lambda e, hh=hh: e.copy(out=wbf[role][:, :, 64 * hh:64 * hh + 64], in_=wtmp[role - 1][:]),
                     reads=[B_wtmp[role - 1]], writes=[B_wbf[role]])
        else:
            P.dma(wbf[role][:], src, writes=[B_wbf[role]], sembuf=B_wbf[role], eng="pool")

    def w_cast(role):
        pass

    def ht_bufs(e0, n):
        return B_hT[e0 // 128:(e0 + n - 1) // 128 + 1]

    def rope_block(role, esl, n, dst_ap, dst_buf, hbufs):
        zi = ctr["z"] % 2
        ctr["z"] += 1
        pz, bpz = psZ[zi], B_psZ[zi]
        for k in range(8):
            P.op("pe", lambda e, k=k: e.matmul(pz[:, 0:n], lhsT=wbf[role][:, k, :], rhs=hT[:, k, esl],
                                               start=(k == 0), stop=(k == 7)),
                 reads=[B_wbf[role]] + hbufs, writes=[bpz])
        ri = ctr["r"] % 2
        ctr["r"] += 1
        P.op("act", lambda e: e.copy(out=zb[ri][:, 0:n], in_=pz[:, 0:n]), reads=[bpz], writes=[B_zb[ri]])

        def post():
            P.op("pe", lambda e: e.matmul(psSW[:, 0:n], lhsT=perm, rhs=zb[ri][:, 0:n], start=True, stop=True),
                 reads=[B_zb[ri], B_const], writes=[B_psSW])
            P.op("pool", lambda e: e.tensor_tensor(out=t1[ri][:, 0:n], in0=zb[ri][:, 0:n], in1=ctab[:, esl], op=ALU.mult),
                 reads=[B_zb[ri], B_tab], writes=[B_t1[ri]])
            P.op("dve", lambda e: e.tensor_tensor(out=t2[ri][:, 0:n], in0=psSW[:, 0:n], in1=stab[:, esl], op=ALU.mult),
                 reads=[B_psSW, B_tab], writes=[B_t2[ri]])
            P.op("dve", lambda e: e.tensor_tensor(out=dst_ap, in0=t1[ri][:, 0:n], in1=t2[ri][:, 0:n], op=ALU.add),
                 reads=[B_t1[ri], B_t2[ri]], writes=[dst_buf])
        prev = pend[0]
        pend[0] = post
        if prev is not None:
            prev()

    pend = [None]

    def flush_rope():
        if pend[0] is not None:
            pend[0]()
            pend[0] = None

    def do_unit(gname, qcol, kcol, vcol, dup, out_chunk, esk_cols, mode, hook_v=None, hook_att=None,
                prev_norm=None, reuse_kv=False):
        r, W = GROUPS[gname]
        plan, vts, emin, emax = PLANS[gname]
        stage = dbg.get("_stage", "all")
        for b4 in range(1 if stage == "q1" else 4):
            e0 = OWN0 + 512 * b4
            rope_block(0, slice(e0, e0 + 512), 512, QT[:, 512 * b4:512 * (b4 + 1)], B_QT, ht_bufs(e0, 512))
        if stage in ("q1", "q4"):
            flush_rope()
            return
        rope_block(0, slice(EXTRA_E[0], EXTRA_E[1] + 1, EXTRA_E[1] - EXTRA_E[0]), 2, QT[:, TOWN:TOWN + 2], B_QT,
                   [B_hT[EXTRA_E[0] // 128], B_hT[EXTRA_E[1] // 128]])
        if stage == "q":
            flush_rope()
            return
        if not reuse_kv:
            e0 = emin
            while e0 < emax:
                n = min(512, emax - e0)
                rope_block(1, slice(e0, e0 + n), n, KT[:, e0:e0 + n], B_KT, ht_bufs(e0, n))
                e0 += n
        flush_rope()
        pparts = list(prev_norm) if prev_norm else []
        if pparts:
            pparts.pop(0)()
        if hook_v is not None:
            hook_v()
        if stage == "k":
            return
        if not reuse_kv:
            nvt = len(vts)
            vb0 = VT_BASE[gname]
            P.op("dve", lambda e: e.tensor_copy(out=VT[:, 0:nvt, 64:128],
                                                in_=kvt[:, vb0:vb0 + nvt].unsqueeze(2).to_broadcast([128, nvt, 64])),
                 reads=[B_const], writes=B_VT[0:nvt])
            for (rr, c, ks, nk), vi in vts.items():
                zi = ctr["z"] % 2
                ctr["z"] += 1
                pz, bpz = psZ[zi], B_psZ[zi]
                tsl = slice(c + rr * ks, c + rr * (ks + nk - 1) + 1, rr)
                for k in range(8):
                    P.op("pe", lambda e, k=k, tsl=tsl, nk=nk, pz=pz: e.matmul(
                        pz[0:nk, 0:128], lhsT=hT[:, k, tsl], rhs=wbf[2][:, k, :], start=(k == 0), stop=(k == 7)),
                        reads=[B_wbf[2]] + B_hT, writes=[bpz])
                kc = VT_BASE[gname] + vi
                P.op("act", lambda e, nk=nk, vi=vi, pz=pz, kc=kc: e.activation(
                    out=VT[0:nk, vi, :].rearrange("p (a b) -> p a b", a=3)[:, 0:3:2, :],
                    in_=pz[0:nk, 0:128].rearrange("p (a b) -> p a b", a=2), func=AF.Copy, scale=kvt[0:nk, kc:kc + 1]),
                    reads=[bpz, B_const], writes=[B_VT[vi]])
                if pparts:
                    pparts.pop(0)()
        while pparts:
            pparts.pop(0)()
        if stage == "v":
            return
        def front(bi):
            blk, kts = plan[bi]
            c, ja, nq = blk
            qc0 = e_mid_col(c + r * ja)
            qsl = slice(qc0, qc0 + r * (nq - 1) + 1, r)
            si = bi % 3
            pS, bpS_l = SSETS[si]
            pt, bpt = Pt[si], B_Pt[si]
            T = len(kts)
            for ti, (rr, cc, ks, nk, dl) in enumerate(kts):
                ksl = slice(cc + rr * ks, cc + rr * (ks + nk - 1) + 1, rr)
                for h in range(2):
                    o0 = h * 512 + (T - 1 - ti) * 128
                    P.op("pe", lambda e: e.matmul(
                        pS[0:nk, o0:o0 + nq], lhsT=KT[64 * h:64 * h + 64, ksl], rhs=QT[64 * h:64 * h + 64, qsl],
                        start=True, stop=True), reads=[B_KT, B_QT], writes=bpS_l)
            pv4 = pS[:, :].rearrange("p (h t q) -> p h t q", h=2, t=4)
            if all(k[3] == 128 for k in kts):
                P.op("act", lambda e: e.activation(out=pt[:, :, 0:T, 0:nq], in_=pv4[:, :, 0:T, 0:nq],
                                                   func=AF.Exp, scale=0.125), reads=bpS_l, writes=[bpt])
            else:
                for ti, (rr, cc, ks, nk, dl) in enumerate(kts):
                    P.op("act", lambda e: e.activation(out=pt[0:nk, :, T - 1 - ti, 0:nq], in_=pv4[0:nk, :, T - 1 - ti, 0:nq],
                                                       func=AF.Exp, scale=0.125), reads=bpS_l, writes=[bpt])
            if nq == 128 and all(k[3] == 128 for k in kts):
                mp = mpk[W]
                P.op("dve", lambda e: e.tensor_tensor(out=pt[:, :, 0:T, :], in0=pt[:, :, 0:T, :],
                                                      in1=mp.unsqueeze(1).to_broadcast([128, 2, T, 128]), op=ALU.mult),
                     reads=[bpt, B_const], writes=[bpt])
            else:
                for ti, (rr, cc, ks, nk, dl) in enumerate(kts):
                    if dl - (nk - 1) >= -W and dl + nq - 1 <= W:
                        continue
                    msl = bnd[W][0:nk, dl + 128:dl + 128 + nq].unsqueeze(1).to_broadcast([nk, 2, nq])
                    P.op("dve", lambda e: e.tensor_tensor(out=pt[0:nk, :, T - 1 - ti, 0:nq], in0=pt[0:nk, :, T - 1 - ti, 0:nq],
                                                          in1=msl, op=ALU.mult), reads=[bpt, B_const], writes=[bpt])

        def back(bi):
            blk, kts = plan[bi]
            c, ja, nq = blk
            qc0 = e_mid_col(c + r * ja)
            qsl = slice(qc0, qc0 + r * (nq - 1) + 1, r)
            si = bi % 3
            pt, bpt = Pt[si], B_Pt[si]
            T = len(kts)
            for h in range(2):
                for ti, (rr, cc, ks, nk, dl) in enumerate(kts):
                    vi = vts[(rr, cc, ks, nk)]
                    P.op("pe", lambda e: e.matmul(
                        psUD[:, h * 128:h * 128 + nq], lhsT=VT[0:nk, vi, 64 * h:64 * h + 128],
                        rhs=pt[0:nk, h, T - 1 - ti, 0:nq], start=(ti == 0), stop=(ti == T - 1)),
                        reads=[bpt, B_VT[vi]], writes=[B_psUD])
            udv = psUD[:, 0:256].rearrange("p (h q) -> p h q", h=2)[:, :, 0:nq]
            if mode in ("A", "first"):
                P.op("dve", lambda e: e.tensor_copy(out=acc[:, :, qsl], in_=udv), reads=[B_psUD], writes=[B_acc])
            else:
                P.op("dve", lambda e: e.tensor_tensor(out=acc[:, :, qsl], in0=udv, in1=acc[:, :, qsl], op=ALU.add),
                     reads=[B_psUD, B_acc], writes=[B_acc])

        if hook_att is not None:
            hook_att()
        nb = len(plan)
        for bi in range(nb + 2):
            if bi < nb:
                front(bi)
            if bi >= 2:
                back(bi - 2)
        def norm_parts():
            parts = []
            if mode not in ("last", "A"):
                return parts
            if mode == "A":
                b0 = esk[64:128, esk_cols[0]:esk_cols[0] + 1]
                b1 = esk[0:64, esk_cols[1]:esk_cols[1] + 1]
            else:
                b0 = pcol("zero", 0, 64, 128)
                b1 = pcol("zero", 0, 0, 64)

            def lnpart():
                P.op("act", lambda e: e.activation(out=acc[64:128, 0, :], in_=acc[64:128, 0, :], func=AF.Ln, bias=b0),
                     reads=[B_acc, B_const], writes=[B_acc])
                P.op("act", lambda e: e.activation(out=acc[0:64, 1, :], in_=acc[0:64, 1, :], func=AF.Ln, bias=b1),
                     reads=[B_acc, B_const], writes=[B_acc])
            parts.append(lnpart)
            c0 = 0
            bix = 0
            while c0 < NMID:
                n = min(128, NMID - c0)

                def blk(c0=c0, n=n, o=128 * (bix % 2), brt=B_rtmp2[bix % 2]):
                    csl = slice(c0, c0 + n)
                    P.op("act", lambda e: e.activation(out=rtmp[0:64, 0, o:o + n], in_=acc[64:128, 0, csl], func=AF.Exp, scale=-1.0),
                         reads=[B_acc], writes=[brt])
                    P.op("act", lambda e: e.activation(out=rtmp[64:128, 1, o:o + n], in_=acc[0:64, 1, csl], func=AF.Exp, scale=-1.0),
                         reads=[B_acc], writes=[brt])
                    P.op("dve", lambda e: e.tensor_tensor(out=mixedT[0:64, out_chunk, csl], in0=acc[0:64, 0, csl],
                                                          in1=rtmp[0:64, 0, o:o + n], op=ALU.mult),
                         reads=[B_acc, brt], writes=[B_mixed[out_chunk]])
                    P.op("dve", lambda e: e.tensor_tensor(out=mixedT[64:128, out_chunk, csl], in0=acc[64:128, 1, csl],
                                                          in1=rtmp[64:128, 1, o:o + n], op=ALU.mult),
                         reads=[B_acc, brt], writes=[B_mixed[out_chunk]])
                parts.append(blk)
                c0 += n
                bix += 1
            return parts

        return norm_parts()

    ulist = []
    for ch in range(4):
        g = ch // 2
        ulist.append(("A", QA0 + 128 * ch, KA0 + 64 * g, VA0 + 64 * g, True, ch, (2 * ch, 2 * ch + 1), "A"))
    for ch in range(4):
        for gi, gname in enumerate(("g1", "g2", "g3")):
            ulist.append((gname, QB0 + 512 * gi + 128 * ch, KB0 + 512 * gi + 128 * ch, VB0 + 512 * gi + 128 * ch, False,
                          4 + ch, None, ("first", "mid", "last")[gi]))
    NU = len(ulist)

    def reuse(ui):
        return ulist[ui][0] == "A" and ui % 2 == 1

    def dmaQ(ui):
        if ui < NU:
            w_dma(0, ulist[ui][1], 128, False)

    def dmaK(ui):
        if ui < NU and not reuse(ui):
            w_dma(1, ulist[ui][2], 64 if ulist[ui][4] else 128, ulist[ui][4])

    def dmaV(ui):
        if ui < NU and not reuse(ui):
            w_dma(2, ulist[ui][3], 64 if ulist[ui][4] else 128, ulist[ui][4])

    def castQ(ui):
        if ui < NU:
            w_cast(0)

    def castK(ui):
        if ui < NU and not reuse(ui):
            w_cast(1)

    def castV(ui):
        if ui < NU and not reuse(ui):
            w_cast(2)

    dmaQ(0); dmaK(0); dmaV(0)
    pnorm = None
    for ui, uu in enumerate(ulist):
        def hook_att(ui=ui):
            dmaQ(ui + 1)
            dmaK(ui + 1)
            dmaV(ui + 1)
        pnorm = do_unit(*uu, hook_v=None, hook_att=hook_att, prev_norm=pnorm, reuse_kv=reuse(ui))
    for pp in (pnorm or []):
        pp()
    tap("mixedT", mixedT[:], B_mixed)
    tap("QT", QT[:], [B_QT])
    tap("KT", KT[:], [B_KT])
    tap("wbf", wbf[1][:], [B_wbf[1]])
    if stop_after == 1:
        P.emit(final)
        phM.close(); stM.close(); stack.close()
        return nc
    P.barrier(None)
    phM.close()
    stM.close()

    TT = [(128 * i, 128) for i in range(16)] + [(TOWN, 2)]
    NTT = len(TT)
    seek(0)
    wst = [sb("wst%d" % i, [128, 8, 256], F32) for i in range(2)]
    B_wst = [Buf("wst%d" % i) for i in range(2)]
    wctr = [0]

    def load_big(dst, bdst, wd, col0, ncols, kchunks=8, rscale=None, row0=0):
        c = 0
        while c < ncols:
            n = min(256, ncols - c)
            i = wctr[0] % 2
            wctr[0] += 1
            src = wd[row0:row0 + 128 * kchunks, col0 + c:col0 + c + n].rearrange("(k p) c -> p k c", p=128)
            P.dma(wst[i][:, 0:kchunks, 0:n], src, writes=[B_wst[i]], sembuf=B_wst[i])
            if rscale is None:
                P.op("act", lambda e, i=i, n=n, c=c: e.copy(out=dst[:, 0:kchunks, c:c + n],
                                                            in_=wst[i][:, 0:kchunks, 0:n]),
                     reads=[B_wst[i]], writes=[bdst])
            else:
                for k in range(kchunks):
                    P.op("act", lambda e, i=i, n=n, c=c, k=k: e.activation(
                        out=dst[:, k, c:c + n], in_=wst[i][:, k, 0:n], func=AF.Copy,
                        scale=pcol(rscale[k][0], rscale[k][1])), reads=[B_wst[i], B_const], writes=[bdst])
            c += n

    def load_cast(dst, bdst, wd, col0, ncols, kchunks=8, row0=0):
        src = wd[row0:row0 + 128 * kchunks, col0:col0 + ncols].rearrange("(k p) c -> p k c", p=128)
        P.dma(dst, src, writes=[bdst], sembuf=bdst, eng="pool")

    bcg = sb("bcg", [128, D], F32)
    bcb = sb("bcb", [128, D], F32)
    B_bc = Buf("bc")

    def load_bc(gi):
        P.dma(bcg[:], rows_d[gi:gi + 1, :].partition_broadcast(128), writes=[B_bc], sembuf=B_bc)
        P.dma(bcb[:], rows_d[gi + 1:gi + 2, :].partition_broadcast(128), writes=[B_bc], sembuf=B_bc)
        P.op("dve", lambda e: e.tensor_scalar(out=bcg[:], in0=bcg[:], scalar1=ALPHA, scalar2=None, op0=ALU.mult),
             reads=[B_bc], writes=[B_bc])
        P.op("dve", lambda e: e.tensor_scalar(out=bcb[:], in0=bcb[:], scalar1=ALPHA, scalar2=None, op0=ALU.mult),
             reads=[B_bc], writes=[B_bc])

    hres_d = dram("hres_scratch", [NTT * 128, D], F32, kind="Internal")
    B_hres = [Buf("hres%d" % t) for t in range(NTT)]
    hn = [sb("hn%d" % i, [128, D], F32) for i in range(2)]
    B_hn = [Buf("hn%d" % i) for i in range(2)]
    hxT = sb("hxT", [128, 8, NMID], BF16)
    B_hxT = [Buf("hxT%d" % t) for t in range(NTT)]
    uw = [sb("uw%d" % i, [128, D], F32) for i in range(2)]
    B_uw = [Buf("uw%d" % i) for i in range(2)]
    ub = [sb("ub%d" % i, [128, D], BF16) for i in range(2)]
    B_ub = [Buf("ub%d" % i) for i in range(2)]
    TAIL_END = [0]

    def ln_front(t, n, u, bu):
        i = t % 2
        ln_stats(u, n, mr[i][0:n, 0:1], mr[i][0:n, 1:2], [bu], [B_mr[i]])

    def ln_mid(t, n, u, bu):
        i = t % 2
        P.op("dve", lambda e: e.tensor_scalar(out=hn[i][0:n, :], in0=u[0:n, :], scalar1=mr[i][0:n, 0:1],
                                              scalar2=mr[i][0:n, 1:2], op0=ALU.subtract, op1=ALU.mult),
             reads=[bu, B_mr[i]], writes=[B_hn[i]])
        P.dma(hres_d[t * 128:t * 128 + n, :], hn[i][0:n, :], reads=[B_hn[i]], writes=[B_hres[t]], sembuf=B_hn[i],
              eng="pool")
        P.op("act", lambda e: e.copy(out=ub[i][0:n, :], in_=hn[i][0:n, :]), reads=[B_hn[i]],
             writes=[B_ub[i]])

    def ln_back(t, n, c0, gname, bname):
        i = t % 2
        pb = t % 2
        psb = psB[pb][:, 0:512].bitcast(BF16)
        for k in range(8):
            P.op("pe", lambda e, k=k: e.transpose(psb[:, k * 128:k * 128 + n], ub[i][0:n, k * 128:(k + 1) * 128],
                                                  ident[0:n, 0:n]),
                 reads=[B_ub[i], B_const], writes=[B_psB[pb]])
        for k in range(8):
            P.op("act", lambda e, k=k: e.activation(out=hxT[:, k, c0:c0 + n], in_=psb[:, k * 128:k * 128 + n],
                                                    func=AF.Identity, scale=pcol(gname, k), bias=pcol(bname, k)),
                 reads=[B_psB[pb], B_const], writes=[B_hxT[t]])

    def pipeline3(nt, front, mid, back, pre=None):
        if pre is not None:
            pre(0)
        for t in range(nt + 2):
            if pre is not None and t + 1 < nt:
                pre(t + 1)
            if t < nt:
                front(t)
            if 1 <= t <= nt:
                mid(t - 1)
            if t >= 2 and back is not None:
                back(t - 2)

    def flush_ln():
        pass

    phW = ExitStack()
    TAIL_END[0] = cur[0]
    woT = sb("woT", [128, 8, D], BF16, phW)
    B_wo = Buf("wo")
    xr = [sb("xr%d" % i, [128, D], F32, phW) for i in range(2)]
    B_xr = [Buf("xr%d" % i) for i in range(2)]
    rs = [sb("rs%d" % i, [128, 8], F32, phW) for i in range(2)]
    B_rs = [Buf("rs%d" % i) for i in range(2)]
    assert cur[0] <= MIX_END - 8 * NMID * 2, cur[0]
    seek(MIX_END)
    sq = sb("sq", [128, 8, NMID], BF16, phW)
    B_sq = Buf("sq")
    load_cast(woT[:], B_wo, w_out, 0, D)
    for k in range(8):
        nm = ("gwin", k) if k < 4 else ("gdil", k - 4)
        P.op("act", lambda e, k=k, nm=nm: e.activation(out=woT[:, k, :], in_=woT[:, k, :], func=AF.Copy,
                                                       scale=pcol(nm[0], nm[1])), reads=[B_wo, B_const], writes=[B_wo])
    load_bc(0)
    for k in range(8):
        P.op("act", lambda e, k=k: e.activation(out=sq[:, k, :], in_=mixedT[:, k, :], func=AF.Square),
             reads=[B_mixed[k]], writes=[B_sq])
    psSS = psB[3]
    B_psSS = B_psB[3]
    def w_pre(t):
        c0, n = TT[t]
        i = t % 2
        if n == 128:
            e0 = OWN0 + c0
            P.dma(xr[i][:], x_ext[e0:e0 + 128, :], writes=[B_xr[i]], sembuf=B_xr[i])
        else:
            P.dma(xr[i][0:1, :], x_ext[EXTRA_E[0]:EXTRA_E[0] + 1, :], writes=[B_xr[i]], sembuf=B_xr[i])
            P.dma(xr[i][1:2, :], x_ext[EXTRA_E[1]:EXTRA_E[1] + 1, :], writes=[], sembuf=B_xr[i])
            B_xr[i].w = ("dma", B_xr[i].sem, B_xr[i].cnt)

    def w_front(t):
        c0, n = TT[t]
        i = t % 2
        for half in range(2):
            for k in range(4):
                P.op("pe", lambda e, half=half, k=k: e.matmul(
                    psSS[0:n, 2 * i + half:2 * i + half + 1], lhsT=sq[:, 4 * half + k, c0:c0 + n], rhs=ones_bf[:, 0:1],
                    start=(k == 0), stop=(k == 3)), reads=[B_sq, B_const], writes=[B_psSS])
        P.op("act", lambda e: e.activation(out=rs[i][0:n, 0:2], in_=psSS[0:n, 2 * i:2 * i + 2], func=AF.Ln,
                                           bias=pcol("eps", 0, 0, n), scale=1.0 / 512.0),
             reads=[B_psSS, B_const], writes=[B_rs[i]])
        P.op("act", lambda e: e.activation(out=rs[i][0:n, 2:4], in_=rs[i][0:n, 0:2], func=AF.Exp, scale=-0.5),
             reads=[B_rs[i]], writes=[B_rs[i]])
        if n == 128:
            e0 = OWN0 + c0
            st_m, st_r = stats[:, e0 // 128, 0:1], stats[:, e0 // 128, 1:2]
            xin = xr[i]
        else:
            ln_stats(xr[i], n, mr[i][0:n, 2:3], mr[i][0:n, 3:4], [B_xr[i]], [B_mr[i]])
            st_m, st_r = mr[i][0:n, 2:3], mr[i][0:n, 3:4]
            xin = xr[i]
        u, bu = uw[i], B_uw[i]
        P.op("dve", lambda e, xin=xin, st_m=st_m: e.scalar_tensor_tensor(
            out=u[0:n, :], in0=xin[0:n, :], scalar=st_m[0:n, :], in1=bcg[0:n, :], op0=ALU.subtract, op1=ALU.mult),
            reads=[B_xr[i], B_bc, B_mr[i]] + B_stats, writes=[bu])
        P.op("dve", lambda e, st_r=st_r: e.scalar_tensor_tensor(
            out=u[0:n, :], in0=u[0:n, :], scalar=st_r[0:n, :], in1=bcb[0:n, :], op0=ALU.mult, op1=ALU.add),
            reads=[bu, B_bc, B_mr[i]] + B_stats, writes=[bu])
        for half in range(2):
            for hc in range(2):
                pb = psS[half][:, 512 * hc:512 * (hc + 1)]
                bpb = B_psS[half]
                for k in range(4):
                    P.op("pe", lambda e, half=half, hc=hc, k=k, pb=pb: e.matmul(
                        pb[0:n, :], lhsT=mixedT[:, 4 * half + k, c0:c0 + n], rhs=woT[:, 4 * half + k, 512 * hc:512 * (hc + 1)],
                        start=(k == 0), stop=(k == 3)), reads=[B_mixed[4 * half + k], B_wo], writes=[bpb])
        for half in range(2):
            for hc in range(2):
                pb = psS[half][:, 512 * hc:512 * (hc + 1)]
                bpb = B_psS[half]
                P.op("dve", lambda e, half=half, hc=hc, pb=pb: e.scalar_tensor_tensor(
                    out=u[0:n, 512 * hc:512 * (hc + 1)], in0=pb[0:n, :], scalar=rs[i][0:n, 2 + half:3 + half],
                    in1=u[0:n, 512 * hc:512 * (hc + 1)], op0=ALU.mult, op1=ALU.add),
                    reads=[bpb, B_rs[i], bu], writes=[bu])
        ln_front(t, n, u, bu)

    pipeline3(NTT, w_front, lambda t: ln_mid(t, TT[t][1], uw[t % 2], B_uw[t % 2]),
              lambda t: ln_back(t, TT[t][1], TT[t][0], "ln1_g", "ln1_b"), pre=w_pre)

    flush_ln()
    if stop_after == 2:
        P.emit(final)
        phW.close(); stack.close()
        return nc
    P.barrier(None)
    phW.close()

    phX = ExitStack()
    seek(TAIL_END[0])
    hr = [sb("hr%d" % i, [128, D], F32, phX) for i in range(2)]
    B_hr = [Buf("hr%d" % i) for i in range(2)]
    wA = sb("wA", [128, 8, D], BF16, phX)
    wB = sb("wB", [128, 8, D], BF16, phX)
    B_wA, B_wB = Buf("wA"), Buf("wB")
    memT = sb("memT", [128, 8, MEM], BF16, phX)
    kxT = sb("kxT", [128, 8, MEM], BF16, phX)
    vx = sb("vx", [128, 2, D], BF16, phX)
    qxT = sb("qxT", [128, 8, NMID], BF16, phX)
    oxT = sb("oxT", [128, 8, NMID], BF16, phX)
    Px = [sb("Px%d" % i, [128, 2, 512], BF16, phX) for i in range(2)]
    rx = [sb("rx%d" % i, [128, 512], F32, phX) for i in range(2)]
    B_memT, B_kxT, B_vx, B_qxT, B_oxT = Buf("memT"), Buf("kxT"), Buf("vx"), Buf("qxT"), Buf("oxT")
    B_Px = [Buf("Px%d" % i) for i in range(2)]
    B_rx = [Buf("rx%d" % i) for i in range(2)]
    load_cast(wA[:], B_wA, w_xk, 0, D)
    load_cast(wB[:], B_wB, w_xv, 0, D)
    for mt in range(2):
        i = mt % 2
        u, bu = uw[i], B_uw[i]
        P.dma(u[:], mem_d[mt * 128:(mt + 1) * 128, :], writes=[bu], sembuf=bu)
        ln_stats(u, 128, mr[i][:, 0:1], mr[i][:, 1:2], [bu], [B_mr[i]])
        P.op("dve", lambda e, u=u, i=i: e.tensor_scalar(out=ub[i][:], in0=u[:], scalar1=mr[i][:, 0:1],
                                                        scalar2=mr[i][:, 1:2], op0=ALU.subtract, op1=ALU.mult),
             reads=[bu, B_mr[i]], writes=[B_ub[i]])
        psb = psS[i][:, 0:512].bitcast(BF16)
        for k in range(8):
            P.op("pe", lambda e, k=k, i=i, psb=psb: e.transpose(psb[:, k * 128:(k + 1) * 128],
                                                                 ub[i][:, k * 128:(k + 1) * 128], ident),
                 reads=[B_ub[i], B_const], writes=[B_psS[i]])
        for k in range(8):
            P.op("act", lambda e, mt=mt, k=k, psb=psb: e.activation(
                out=memT[:, k, mt * 128:(mt + 1) * 128], in_=psb[:, k * 128:(k + 1) * 128], func=AF.Identity,
                scale=pcol("mem_g", k), bias=pcol("mem_b", k)), reads=[B_psS[i], B_const], writes=[B_memT])
    load_bc(2)
    for c in range(8):
        pz, bpz = psB[c % 2], B_psB[c % 2]
        for k in range(8):
            P.op("pe", lambda e, c=c, k=k, pz=pz: e.matmul(pz[:, 0:MEM], lhsT=wA[:, k, c * 128:(c + 1) * 128],
                                                          rhs=memT[:, k, :], start=(k == 0), stop=(k == 7)),
                 reads=[B_wA, B_memT], writes=[bpz])
        P.op("act", lambda e, c=c, pz=pz: e.copy(out=kxT[:, c, :], in_=pz[:, 0:MEM]), reads=[bpz], writes=[B_kxT])
    for mt in range(2):
        for hc in range(2):
            pz, bpz = psB[2 + hc], B_psB[2 + hc]
            for k in range(8):
                P.op("pe", lambda e, mt=mt, hc=hc, k=k, pz=pz: e.matmul(
                    pz[:, :], lhsT=memT[:, k, mt * 128:(mt + 1) * 128], rhs=wB[:, k, 512 * hc:512 * (hc + 1)],
                    start=(k == 0), stop=(k == 7)), reads=[B_wB, B_memT], writes=[bpz])
            P.op("act", lambda e, mt=mt, hc=hc, pz=pz: e.copy(out=vx[:, mt, 512 * hc:512 * (hc + 1)], in_=pz[:, :]),
                 reads=[bpz], writes=[B_vx])
    load_cast(wA[:], B_wA, w_xq, 0, D)
    CB = [(512 * i, 512) for i in range(4)] + [(TOWN, 2)]
    zc = 0
    for c in range(8):
        for (c0, n) in CB:
            pz, bpz = psB[zc % 2], B_psB[zc % 2]
            zc += 1
            for k in range(8):
                P.op("pe", lambda e, c=c, k=k, pz=pz, c0=c0, n=n: e.matmul(
                    pz[:, 0:n], lhsT=wA[:, k, c * 128:(c + 1) * 128], rhs=hxT[:, k, c0:c0 + n],
                    start=(k == 0), stop=(k == 7)), reads=[B_wA] + B_hxT, writes=[bpz])
            P.op("act", lambda e, c=c, pz=pz, c0=c0, n=n: e.copy(out=qxT[:, c, c0:c0 + n], in_=pz[:, 0:n]),
                 reads=[bpz], writes=[B_qxT])
    load_cast(wB[:], B_wB, w_xo, 0, D)
    xa_items = [(hd, c0, n) for hd in range(4) for (c0, n) in CB]

    def xa_front(it):
        hd, c0, n = xa_items[it]
        si = it % 2
        pS, bpS = psS[si], B_psS[si]
        for m in range(2):
            for kk in range(2):
                P.op("pe", lambda e: e.matmul(
                    pS[:, 512 * m:512 * m + n], lhsT=kxT[:, 2 * hd + kk, m * 128:(m + 1) * 128],
                    rhs=qxT[:, 2 * hd + kk, c0:c0 + n], start=(kk == 0), stop=(kk == 1)),
                    reads=[B_kxT, B_qxT], writes=[bpS])
        P.op("act", lambda e: e.activation(
            out=Px[si][:, :, 0:n], in_=pS[:, :].rearrange("p (m q) -> p m q", m=2)[:, :, 0:n], func=AF.Exp,
            scale=1.0 / 16.0), reads=[bpS], writes=[B_Px[si]])

    def xa_back(it):
        hd, c0, n = xa_items[it]
        si = it % 2
        pD, bpD = psB[2], B_psB[2]
        for m in range(2):
            P.op("pe", lambda e: e.matmul(pD[:, 0:n], lhsT=ones_bf[:, :], rhs=Px[si][:, m, 0:n],
                                          start=(m == 0), stop=(m == 1)),
                 reads=[B_Px[si], B_const], writes=[bpD])
        P.op("act", lambda e: e.activation(out=rx[si][:, 0:n], in_=pD[:, 0:n], func=AF.Ln),
             reads=[bpD], writes=[B_rx[si]])
        P.op("act", lambda e: e.activation(out=rx[si][:, 0:n], in_=rx[si][:, 0:n], func=AF.Exp, scale=-1.0),
             reads=[B_rx[si]], writes=[B_rx[si]])
        for kk in range(2):
            pO, bpO = psB[kk], B_psB[kk]
            for m in range(2):
                P.op("pe", lambda e: e.matmul(
                    pO[:, 0:n], lhsT=vx[:, m, hd * 256 + kk * 128:hd * 256 + (kk + 1) * 128], rhs=Px[si][:, m, 0:n],
                    start=(m == 0), stop=(m == 1)), reads=[B_Px[si], B_vx], writes=[bpO])
            P.op("dve", lambda e: e.tensor_tensor(
                out=oxT[:, 2 * hd + kk, c0:c0 + n], in0=pO[:, 0:n], in1=rx[si][:, 0:n], op=ALU.mult),
                reads=[bpO, B_rx[si]], writes=[B_oxT])

    for it in range(len(xa_items) + 1):
        if it < len(xa_items):
            xa_front(it)
        if it >= 1:
            xa_back(it - 1)
    def x_pre(t):
        c0, n = TT[t]
        i = t % 2
        P.dma(hr[i][0:n, :], hres_d[t * 128:t * 128 + n, :], reads=[B_hres[t]], writes=[B_hr[i]], sembuf=B_hr[i])

    def x_front(t):
        c0, n = TT[t]
        i = t % 2
        u, bu = uw[i], B_uw[i]
        P.op("dve", lambda e, t=t, u=u: e.tensor_tensor(out=u[0:n, :], in0=hr[i][0:n, :], in1=bcg[0:n, :], op=ALU.mult),
             reads=[B_hr[i], B_bc], writes=[bu])
        P.op("dve", lambda e, u=u: e.tensor_tensor(out=u[0:n, :], in0=u[0:n, :], in1=bcb[0:n, :], op=ALU.add),
             reads=[bu, B_bc], writes=[bu])
        for hc in range(2):
            pb, bpb = psS[t % 2][:, 512 * hc:512 * (hc + 1)], B_psS[t % 2]
            for k in range(8):
                P.op("pe", lambda e, hc=hc, k=k, pb=pb: e.matmul(
                    pb[0:n, :], lhsT=oxT[:, k, c0:c0 + n], rhs=wB[:, k, 512 * hc:512 * (hc + 1)],
                    start=(k == 0), stop=(k == 7)), reads=[B_oxT, B_wB], writes=[bpb])
        for hc in range(2):
            pb, bpb = psS[t % 2][:, 512 * hc:512 * (hc + 1)], B_psS[t % 2]
            P.op("dve", lambda e, hc=hc, pb=pb, u=u: e.tensor_tensor(
                out=u[0:n, 512 * hc:512 * (hc + 1)], in0=pb[0:n, :], in1=u[0:n, 512 * hc:512 * (hc + 1)], op=ALU.add),
                reads=[bpb, bu], writes=[bu])
        ln_front(t, n, u, bu)

    pipeline3(NTT, x_front, lambda t: ln_mid(t, TT[t][1], uw[t % 2], B_uw[t % 2]),
              lambda t: ln_back(t, TT[t][1], TT[t][0], "ln2_g", "ln2_b"), pre=x_pre)

    flush_ln()
    if stop_after == 3:
        P.emit(final)
        phX.close(); stack.close()
        return nc
    P.barrier(None)
    phX.close()

    phF = ExitStack()
    seek(TAIL_END[0])
    hr = [sb("hr%d" % i, [128, D], F32, phF) for i in range(2)]
    B_hr = [Buf("hrF%d" % i) for i in range(2)]
    HT = 1024
    aT = sb("aT", [128, NFF, HT], BF16, phF)
    B_aT = Buf("aT")
    wg = [sb("wg%d" % i, [128, 8, 128], BF16, phF) for i in range(2)]
    wu = [sb("wu%d" % i, [128, 8, 128], BF16, phF) for i in range(2)]
    B_wg = [Buf("wg%d" % i) for i in range(2)]
    B_wu = [Buf("wu%d" % i) for i in range(2)]
    wdn = sb("wdn", [128, NFF, D], BF16, phF)
    B_wdn = Buf("wdn")
    gp = [sb("gp0", [128, HT + 2], F32, phF)] * 2
    B_gp = [Buf("gp0")] * 2
    gc = [sb("gc0", [128, HT], F32, phF)] * 2
    B_gc = [Buf("gc0")] * 2
    ge = [sb("ge0", [128, HT], F32, phF)] * 2
    B_ge = [Buf("ge0")] * 2
    g3b = sb("g3b", [128, D], F32, phF)
    b3b = sb("b3b", [128, D], F32, phF)
    B_g3 = Buf("g3")
    B_ostg = [Buf("ostg%d" % i) for i in range(2)]
    def f_dma(g):
        j = g % NFF
        i = g % 2
        load_cast(wg[i][:], B_wg[i], w_gate, j * 128, 128)
        load_cast(wu[i][:], B_wu[i], w_up, j * 128, 128)

    f_dma(0)
    load_bc(4)
    def wdn_piece(r0, kc=2):
        load_cast(wdn[:, r0:r0 + kc, :], B_wdn, w_down, 0, D, kchunks=kc, row0=r0 * 128)
    P.dma(g3b[:], rows_d[6:7, :].partition_broadcast(128), writes=[B_g3], sembuf=B_g3)
    P.dma(b3b[:], rows_d[7:8, :].partition_broadcast(128), writes=[B_g3], sembuf=B_g3)

    def mcol(tok):
        if tok < 0:
            return TOWN
        if tok >= TOWN:
            return TOWN + 1
        return tok

    for hf in range(2):
        t0 = hf * HT
        for j in range(NFF):
            i = j % 2
            gidx = hf * NFF + j
            if gidx + 1 < 2 * NFF:
                f_dma(gidx + 1)
            if hf == 0 and 1 <= j <= NFF // 2:
                wdn_piece(2 * (j - 1))
            pieces = [(mcol(t0 - 1), 1, 0), (t0, 512, 1), (t0 + 512, 512, 513), (mcol(t0 + HT), 1, HT + 1)]
            for pi, (cc, n, go) in enumerate(pieces):
                pz, bpz = psB[pi], B_psB[pi]
                for k in range(8):
                    P.op("pe", lambda e, k=k, pz=pz, cc=cc, n=n: e.matmul(
                        pz[:, 0:n], lhsT=wg[i][:, k, :], rhs=hxT[:, k, cc:cc + n], start=(k == 0), stop=(k == 7)),
                        reads=[B_wg[i]] + B_hxT, writes=[bpz])
                if n == 1 and cc >= TOWN:
                    P.op("act", lambda e, pz=pz, go=go, cc=cc: e.activation(
                        out=gp[i][:, go:go + 1], in_=pz[:, 0:1], func=AF.Copy, scale=gval[:, cc - TOWN:cc - TOWN + 1]),
                        reads=[bpz, B_const], writes=[B_gp[i]])
                else:
                    P.op("act", lambda e, pz=pz, go=go, n=n: e.copy(out=gp[i][:, go:go + n], in_=pz[:, 0:n]),
                         reads=[bpz], writes=[B_gp[i]])
            P.op("dve", lambda e, j=j: e.tensor_scalar(out=gc[i][:], in0=gp[i][:, 0:HT], scalar1=pcol("cw0", j),
                                                       scalar2=pcol("cb", j), op0=ALU.mult, op1=ALU.add),
                 reads=[B_gp[i], B_const], writes=[B_gc[i]])
            P.op("dve", lambda e, j=j: e.scalar_tensor_tensor(out=gc[i][:], in0=gp[i][:, 1:HT + 1], scalar=pcol("cw1", j),
                                                              in1=gc[i][:], op0=ALU.mult, op1=ALU.add),
                 reads=[B_gp[i], B_gc[i], B_const], writes=[B_gc[i]])
            P.op("dve", lambda e, j=j: e.scalar_tensor_tensor(out=gc[i][:], in0=gp[i][:, 2:HT + 2], scalar=pcol("cw2", j),
                                                               in1=gc[i][:], op0=ALU.mult, op1=ALU.add),
                 reads=[B_gp[i], B_gc[i], B_const], writes=[B_gc[i]])
            P.op("act", lambda e: e.activation(out=ge[i][:], in_=gc[i][:], func=AF.Gelu), reads=[B_gc[i]],
                 writes=[B_ge[i]])
            for ub_ in range(2):
                pz, bpz = psS[j % 2][:, 512 * ub_:512 * (ub_ + 1)], B_psS[j % 2]
                for k in range(8):
                    P.op("pe", lambda e, k=k, pz=pz, ub_=ub_: e.matmul(
                        pz[:, :], lhsT=wu[i][:, k, :], rhs=hxT[:, k, t0 + 512 * ub_:t0 + 512 * (ub_ + 1)],
                        start=(k == 0), stop=(k == 7)), reads=[B_wu[i]] + B_hxT, writes=[bpz])
            for ub_ in range(2):
                pz, bpz = psS[j % 2][:, 512 * ub_:512 * (ub_ + 1)], B_psS[j % 2]
                P.op("dve", lambda e, j=j, pz=pz, ub_=ub_: e.tensor_tensor(
                    out=aT[:, j, 512 * ub_:512 * (ub_ + 1)], in0=pz[:, :], in1=ge[i][:, 512 * ub_:512 * (ub_ + 1)],
                    op=ALU.mult), reads=[bpz, B_ge[i]], writes=[B_aT])
        def f_pre(tt, hf=hf):
            t = hf * 8 + tt
            i = t % 2
            P.dma(hr[i][:], hres_d[t * 128:(t + 1) * 128, :], reads=[B_hres[t]], writes=[B_hr[i]], sembuf=B_hr[i])

        def f_front(tt, hf=hf):
            t = hf * 8 + tt
            i = t % 2
            u, bu = uw[i], B_uw[i]
            P.op("dve", lambda e, t=t, u=u: e.tensor_tensor(out=u[:], in0=hr[i][:], in1=bcg[:], op=ALU.mult),
                 reads=[B_hr[i], B_bc], writes=[bu])
            P.op("dve", lambda e, u=u: e.tensor_tensor(out=u[:], in0=u[:], in1=bcb[:], op=ALU.add),
                 reads=[bu, B_bc], writes=[bu])
            if tt == 0:
                pass
            for hc in range(2):
                pb, bpb = psS[tt % 2][:, 512 * hc:512 * (hc + 1)], B_psS[tt % 2]
                for j in range(NFF):
                    P.op("pe", lambda e, hc=hc, j=j, pb=pb, tt=tt: e.matmul(
                        pb[:, :], lhsT=aT[:, j, tt * 128:(tt + 1) * 128], rhs=wdn[:, j, 512 * hc:512 * (hc + 1)],
                        start=(j == 0), stop=(j == NFF - 1)), reads=[B_aT, B_wdn], writes=[bpb])
            for hc in range(2):
                pb, bpb = psS[tt % 2][:, 512 * hc:512 * (hc + 1)], B_psS[tt % 2]
                P.op("dve", lambda e, hc=hc, pb=pb, u=u: e.tensor_tensor(
                    out=u[:, 512 * hc:512 * (hc + 1)], in0=pb[:, :], in1=u[:, 512 * hc:512 * (hc + 1)], op=ALU.add),
                    reads=[bpb, bu], writes=[bu])
            ln_stats(u, 128, mr[i][:, 0:1], mr[i][:, 1:2], [bu], [B_mr[i]])

        def f_mid(tt, hf=hf):
            t = hf * 8 + tt
            i = t % 2
            u, bu = uw[i], B_uw[i]
            P.op("dve", lambda e, u=u, i=i: e.tensor_scalar(out=u[:], in0=u[:], scalar1=mr[i][:, 0:1],
                                                            scalar2=mr[i][:, 1:2], op0=ALU.subtract, op1=ALU.mult),
                 reads=[bu, B_mr[i]], writes=[bu])
            P.op("dve", lambda e, u=u: e.tensor_tensor(out=u[:], in0=u[:], in1=g3b[:], op=ALU.mult),
                 reads=[bu, B_g3], writes=[bu])
            P.op("dve", lambda e, u=u, i=i: e.tensor_tensor(out=u[:], in0=u[:], in1=b3b[:], op=ALU.add),
                 reads=[bu, B_g3], writes=[bu])
            final.append(P.dma(out_d[t * 128:(t + 1) * 128, :], u[:], reads=[bu], sembuf=B_ostg[i]))
        pipeline3(8, f_front, f_mid, None, pre=f_pre)
    P.emit(final)
    phF.close()
    stack.close()
    return nc


def host_consts():
    c = np.zeros((128, 1024), np.float32)
    c[:, 0:128] = np.eye(128)
    pm = np.zeros((128, 128), np.float32)
    for m in range(128):
        d = m % 64
        if d < 8:
            pm[m + 8, m] = 1.0
        elif d < 16:
            pm[m - 8, m] = 1.0
    c[:, 128:256] = pm
    r = np.arange(128)[:, None]
    xx = np.arange(384)[None, :]
    c[:, 256:640] = (np.abs(xx - 128 - r) <= 64)
    c[:, 640:1024] = (np.abs(xx - 128 - r) <= 128)
    return c.astype(ml_dtypes.bfloat16)


def host_consts2():
    r = np.arange(128)[:, None]
    cc = np.arange(128)[None, :]
    out = np.zeros((128, 640), np.float32)
    for t_ in range(2):
        out[:, t_ * 128:(t_ + 1) * 128] = (np.abs(64 - 128 * t_ + cc - r) <= 64)
    for t_ in range(3):
        out[:, 256 + t_ * 128:256 + (t_ + 1) * 128] = (np.abs(128 - 128 * t_ + cc - r) <= 128)
    return out.astype(ml_dtypes.bfloat16)


def host_pcols(inp):
    pcv = np.zeros((128, 256), np.float32)
    o = [0]

    def put(arr, n):
        a = np.asarray(arr, np.float32).reshape(n, 128).T
        pcv[:, o[0]:o[0] + n] = a
        o[0] += n

    put(inp["ln_in_g"], 8)
    put(inp["ln_in_b"], 8)
    put(inp["ln1_g"][0], 8)
    put(inp["ln1_b"][0], 8)
    put(inp["ln2_g"][0], 8)
    put(inp["ln2_b"][0], 8)
    put(inp["mem_ln_g"][0], 8)
    put(inp["mem_ln_b"][0], 8)
    put(inp["g_win"][0], 4)
    put(inp["g_dil"][0], 4)
    sk = np.asarray(inp["attn_sink"][0], np.float32)
    pcv[:, o[0]:o[0] + 8] = sk[None, :]
    o[0] += 8
    p = np.arange(128)
    d = p % 64
    invf = (np.float32(500000.0) ** (-(np.arange(0, 16, 2, dtype=np.float32)) / np.float32(16))).astype(np.float32)
    pcv[:, o[0]] = invf[d % 8]
    pcv[:, o[0] + 1] = (d < 16)
    pcv[:, o[0] + 2] = np.where(d < 8, -1.0, np.where(d < 16, 1.0, 0.0))
    pcv[:, o[0] + 3] = EPS
    o[0] += 5
    cw = np.asarray(inp["conv_w"][0], np.float32)
    for j in range(3):
        put(cw[j], NFF)
    put(inp["conv_b"][0], NFF)
    return pcv


def core_inputs(inp, core, shared):
    b = core // 4
    s0 = (core % 4) * TOWN
    e0 = s0 - HALO
    idx = np.arange(EXT) + e0
    ok = (idx >= 0) & (idx < SEQ)
    ci = np.clip(idx, 0, SEQ - 1)
    x_ext = np.where(ok[:, None], inp["x"][b][ci], np.float32(0)).astype(np.float32)
    pos = np.where(ok, inp["positions"][b][ci], 0).astype(np.int32)[None, :]
    valid = ok.astype(np.float32)
    kv = np.zeros((128, NVT), np.float32)
    for g in GROUPS:
        for (r, c, ks, nk), vi in PLANS[g][1].items():
            e = c + r * (ks + np.arange(nk))
            kv[:nk, VT_BASE[g] + vi] = valid[e]
    gv = np.array([[valid[EXTRA_E[0]], valid[EXTRA_E[1]]]], np.float32)
    m = dict(shared)
    m.update(x_ext=x_ext, pos=pos, kvtab=kv, gvalid=gv, mem=np.ascontiguousarray(inp["mem"][b], np.float32))
    return m


def shared_inputs(inp):
    f = lambda a: np.ascontiguousarray(np.asarray(a, np.float32))
    rows = np.stack([inp["ln_in_g"], inp["ln_in_b"], inp["ln1_g"][0], inp["ln1_b"][0], inp["ln2_g"][0],
                     inp["ln2_b"][0], inp["ln3_g"][0], inp["ln3_b"][0]]).astype(np.float32)
    return dict(cbf=host_consts(), pcols=host_pcols(inp), w_in=f(inp["w_in"][0]), w_out=f(inp["w_mix_out"][0]),
                w_xq=f(inp["w_xq"][0]), w_xk=f(inp["w_xk"][0]), w_xv=f(inp["w_xv"][0]), w_xo=f(inp["w_xo"][0]),
                w_gate=f(inp["w_gate"][0]), w_up=f(inp["w_up"][0]), w_down=f(inp["w_down"][0]), rows=rows)


def kernel(**inp):
    inp = {k: np.asarray(v) for k, v in inp.items()}
    nc = build_program()
    shared = shared_inputs(inp)
    in_maps = [core_inputs(inp, c, shared) for c in range(8)]
    res = run_bass_kernel_spmd(nc, in_maps, core_ids=list(range(8)))
    out = np.zeros((2, SEQ, D), np.float32)
    for c in range(8):
        out[c // 4, (c % 4) * TOWN:(c % 4 + 1) * TOWN] = res.results[c]["out"]
    return out
```
